# Optimizing a Trainium2 kernel written in Bass

```python
import math
import jax, jax.numpy as jnp
from jax import lax
import numpy as np

D_MODEL = 1024
BATCH = 32
SEQ = 256
DEPTH = 4
DEC_BATCH = 2
DEC_SEQ = 1024
PAST_LEN = 512

GRID_W = 64
N_MIXERS = 2
N_NA_LAYERS = (DEPTH + 1) // 2
N_SSM_LAYERS = DEPTH // 2
NA_HEADS = 16
HEAD_DIM = D_MODEL // NA_HEADS
ATTN_SCALE = HEAD_DIM ** -0.5
WIN_R = 8
WIN_C = 16
Q_BLOCK_C = 16
KEY_BLOCK_C = 2 * WIN_C
ATTN_Q_BLOCK = 128
SSM_GROUP = 16
SSM_GROUPS = D_MODEL // SSM_GROUP
SSM_STATE = 64
N_DIR = 2
D_FF = -(-8 * D_MODEL // (3 * 256)) * 256
N_MOD = 6
EPS = 1e-6

kernel_name = "hybrid_natten_s5_diffusion_step"


def rms_norm(x, g):
    xf = x.astype(jnp.float32)
    y = xf * lax.rsqrt(jnp.mean(xf * xf, axis=-1, keepdims=True) + EPS)
    return (y * g.astype(jnp.float32)).astype(x.dtype)


def ada_modulation(cond, w, b):
    m = jax.nn.silu(cond) @ w + b
    return jnp.split(m[..., None, :], N_MOD, axis=-1)


def modulate(h, shift, scale):
    return h * (1.0 + scale) + shift


def swiglu(h, w1, w3, w2):
    return (jax.nn.silu(h @ w1) * (h @ w3)) @ w2


def na_qkv(h, w_qkv, q_gain, k_gain):
    b, n, _ = h.shape
    qkv = (h @ w_qkv).reshape(b, n, 3, NA_HEADS, HEAD_DIM)
    q = rms_norm(qkv[:, :, 0], q_gain)
    k = rms_norm(qkv[:, :, 1], k_gain)
    return q, k, qkv[:, :, 2]


def context_attention(q, k, v):
    b, s, h, d = q.shape
    nblk = s // ATTN_Q_BLOCK
    qb = jnp.moveaxis(q.reshape(b, nblk, ATTN_Q_BLOCK, h, d), 1, 0)

    def one_block(qi):
        sc = jnp.einsum('bqhd,bshd->bhqs', qi, k).astype(jnp.float32) * ATTN_SCALE
        p = jax.nn.softmax(sc, axis=-1).astype(v.dtype)
        return jnp.einsum('bhqs,bshd->bqhd', p, v)

    o = lax.map(one_block, qb)
    return jnp.moveaxis(o, 0, 1).reshape(b, s, h * d)


def neighbourhood_attention(q, k, v, k_ctx, v_ctx, rpb):
    b, n, h, d = q.shape
    rows = n // GRID_W
    kr = min(WIN_R, rows)
    ncb = GRID_W // Q_BLOCK_C
    r = jnp.arange(rows)
    row_start = jnp.clip(r - kr // 2, 0, rows - kr)
    key_rows = row_start[:, None] + jnp.arange(kr)
    cb = jnp.arange(ncb) * Q_BLOCK_C
    key_col0 = jnp.clip(cb - WIN_C // 2, 0, GRID_W - KEY_BLOCK_C)
    key_cols = key_col0[:, None] + jnp.arange(KEY_BLOCK_C)
    q_cols = cb[:, None] + jnp.arange(Q_BLOCK_C)
    col_start = jnp.clip(q_cols - WIN_C // 2, 0, GRID_W - WIN_C)
    rel = key_cols[:, None, :] - col_start[:, :, None]
    in_win = (rel >= 0) & (rel < WIN_C)
    dc = jnp.clip(key_cols[:, None, :] - q_cols[:, :, None], -(WIN_C - 1), WIN_C - 1)
    dr = key_rows - r[:, None]
    bias = rpb[:, dr[:, None, None, :, None] + (WIN_R - 1), dc[None, :, :, None, :] + (WIN_C - 1)]
    bias = jnp.where(in_win[None, None, :, :, None, :], bias.astype(jnp.float32), -jnp.inf)
    k_grid = k.reshape(b, rows, GRID_W, h, d)
    v_grid = v.reshape(b, rows, GRID_W, h, d)
    ridx = key_rows[:, :, None, None]
    cidx = key_cols[None, None, :, :]
    kg = k_grid[:, ridx, cidx]
    vg = v_grid[:, ridx, cidx]
    qg = q.reshape(b, rows, ncb, Q_BLOCK_C, h, d)
    s_loc = jnp.einsum('brnqhd,brknchd->bhrnqkc', qg, kg).astype(jnp.float32) * ATTN_SCALE + bias[None]
    s_ctx = jnp.einsum('brnqhd,bshd->bhrnqs', qg, k_ctx).astype(jnp.float32) * ATTN_SCALE
    n_loc = kr * KEY_BLOCK_C
    scores = jnp.concatenate([s_loc.reshape(s_loc.shape[:5] + (n_loc,)), s_ctx], axis=-1)
    p = jax.nn.softmax(scores, axis=-1).astype(v.dtype)
    p_loc = p[..., :n_loc].reshape(s_loc.shape)
    p_ctx = p[..., n_loc:]
    o = jnp.einsum('bhrnqkc,brknchd->brnqhd', p_loc, vg) + jnp.einsum('bhrnqs,bshd->brnqhd', p_ctx, v_ctx)
    return o.reshape(b, n, h * d)


def s5_discretise(lam_re, lam_im, log_step, b_re, b_im):
    f32 = jnp.float32
    lam_re, lam_im = lam_re.astype(f32), lam_im.astype(f32)
    step = jnp.exp(log_step.astype(f32))[:, None]
    mag = jnp.exp(lam_re * step)
    a_re, a_im = mag * jnp.cos(lam_im * step), mag * jnp.sin(lam_im * step)
    den = lam_re * lam_re + lam_im * lam_im
    nr, ni = a_re - 1.0, a_im
    f_re = (nr * lam_re + ni * lam_im) / den
    f_im = (ni * lam_re - nr * lam_im) / den
    b_re, b_im = b_re.astype(f32), b_im.astype(f32)
    bb_re = f_re[..., None] * b_re - f_im[..., None] * b_im
    bb_im = f_re[..., None] * b_im + f_im[..., None] * b_re
    return a_re, a_im, bb_re, bb_im


def s5_scan(a_re, a_im, bu_re, bu_im, reverse):
    A_re = jnp.broadcast_to(a_re, bu_re.shape)
    A_im = jnp.broadcast_to(a_im, bu_re.shape)

    def combine(e1, e2):
        a1r, a1i, b1r, b1i = e1
        a2r, a2i, b2r, b2i = e2
        return (a2r * a1r - a2i * a1i, a2r * a1i + a2i * a1r,
                a2r * b1r - a2i * b1i + b2r, a2r * b1i + a2i * b1r + b2i)

    _, _, x_re, x_im = lax.associative_scan(combine, (A_re, A_im, bu_re, bu_im), axis=1, reverse=reverse)
    return x_re, x_im


def s5_mixer(u, lam_re, lam_im, log_step, b_re, b_im, c_re, c_im, d_skip, w_glu, h0_re, h0_im, return_state):
    f32 = jnp.float32
    bsz, length, _ = u.shape
    uf = u.astype(f32)
    ug = uf.reshape(bsz, length, SSM_GROUPS, SSM_GROUP)
    y = ug * d_skip.astype(f32).reshape(SSM_GROUPS, SSM_GROUP)
    fin_re, fin_im = [], []
    for dr in range(N_DIR):
        reverse = dr == 1
        a_re, a_im, bb_re, bb_im = s5_discretise(lam_re[dr], lam_im[dr], log_step[dr], b_re[dr], b_im[dr])
        bu_re = jnp.einsum('gpc,blgc->blgp', bb_re, ug)
        bu_im = jnp.einsum('gpc,blgc->blgp', bb_im, ug)
        if h0_re is not None:
            pos = length - 1 if reverse else 0
            s_re = h0_re[:, dr].astype(f32)
            s_im = h0_im[:, dr].astype(f32)
            bu_re = bu_re.at[:, pos].add(a_re * s_re - a_im * s_im)
            bu_im = bu_im.at[:, pos].add(a_re * s_im + a_im * s_re)
        x_re, x_im = s5_scan(a_re, a_im, bu_re, bu_im, reverse)
        y = y + jnp.einsum('gcp,blgp->blgc', c_re[dr].astype(f32), x_re) \
              - jnp.einsum('gcp,blgp->blgc', c_im[dr].astype(f32), x_im)
        if return_state:
            end = 0 if reverse else length - 1
            fin_re.append(x_re[:, end])
            fin_im.append(x_im[:, end])
    z = jax.nn.gelu(y.reshape(bsz, length, D_MODEL)).astype(u.dtype)
    val, gate = jnp.split(z @ w_glu, 2, axis=-1)
    out = val * jax.nn.sigmoid(gate)
    if return_state:
        return out, jnp.stack(fin_re, axis=1).astype(u.dtype), jnp.stack(fin_im, axis=1).astype(u.dtype)
    return out


def setup_inputs(seed: int = 0) -> dict:
    key = jax.random.key(seed)
    ks = jax.random.split(key, 32)
    f32 = jnp.float32

    def nrm(i, shape, s):
        return jax.random.normal(ks[i], shape, f32) * s

    d = D_MODEL
    lam_im_base = jnp.pi * jnp.arange(SSM_STATE, dtype=f32)
    return {
        "x_prompt": nrm(0, (BATCH, SEQ, d), 1.0),
        "x_sample": nrm(1, (DEC_BATCH, DEC_SEQ, d), 1.0),
        "cache_k": nrm(2, (DEC_BATCH, N_NA_LAYERS, PAST_LEN, NA_HEADS, HEAD_DIM), 1.0),
        "cache_v": nrm(3, (DEC_BATCH, N_NA_LAYERS, PAST_LEN, NA_HEADS, HEAD_DIM), 1.0),
        "state_ssm_re": nrm(4, (DEC_BATCH, N_SSM_LAYERS, N_DIR, SSM_GROUPS, SSM_STATE), 0.3),
        "state_ssm_im": nrm(5, (DEC_BATCH, N_SSM_LAYERS, N_DIR, SSM_GROUPS, SSM_STATE), 0.3),
        "c": nrm(6, (DEC_BATCH, d), 1.0),
        "c_ctx": nrm(7, (d,), 1.0),
        "norm_mix": 1.0 + nrm(8, (DEPTH, d), 0.02),
        "norm_ffn": 1.0 + nrm(9, (DEPTH, d), 0.02),
        "ada_w": nrm(10, (DEPTH, d, N_MOD * d), 0.5 * d ** -0.5),
        "ada_b": nrm(11, (DEPTH, N_MOD * d), 0.02),
        "na_w_qkv": nrm(12, (N_NA_LAYERS, d, 3 * d), d ** -0.5),
        "na_w_o": nrm(13, (N_NA_LAYERS, d, d), d ** -0.5),
        "na_q_gain": 1.0 + nrm(14, (N_NA_LAYERS, HEAD_DIM), 0.02),
        "na_k_gain": 1.0 + nrm(15, (N_NA_LAYERS, HEAD_DIM), 0.02),
        "na_rpb": nrm(16, (N_NA_LAYERS, NA_HEADS, 2 * WIN_R - 1, 2 * WIN_C - 1), 0.02),
        "ssm_lambda_re": -0.5 + nrm(17, (N_SSM_LAYERS, N_DIR, SSM_GROUPS, SSM_STATE), 0.01),
        "ssm_lambda_im": lam_im_base + nrm(18, (N_SSM_LAYERS, N_DIR, SSM_GROUPS, SSM_STATE), 0.01),
        "ssm_log_step": jax.random.uniform(ks[19], (N_SSM_LAYERS, N_DIR, SSM_GROUPS), f32,
                                           minval=math.log(1e-3), maxval=math.log(1e-1)),
        "ssm_b_re": nrm(20, (N_SSM_LAYERS, N_DIR, SSM_GROUPS, SSM_STATE, SSM_GROUP), SSM_GROUP ** -0.5),
        "ssm_b_im": nrm(21, (N_SSM_LAYERS, N_DIR, SSM_GROUPS, SSM_STATE, SSM_GROUP), SSM_GROUP ** -0.5),
        "ssm_c_re": nrm(22, (N_SSM_LAYERS, N_DIR, SSM_GROUPS, SSM_GROUP, SSM_STATE), SSM_STATE ** -0.5),
        "ssm_c_im": nrm(23, (N_SSM_LAYERS, N_DIR, SSM_GROUPS, SSM_GROUP, SSM_STATE), SSM_STATE ** -0.5),
        "ssm_d": nrm(24, (N_SSM_LAYERS, d), 1.0),
        "ssm_w_glu": nrm(25, (N_SSM_LAYERS, d, 2 * d), d ** -0.5),
        "ffn_w1": nrm(26, (DEPTH, d, D_FF), d ** -0.5),
        "ffn_w3": nrm(27, (DEPTH, d, D_FF), d ** -0.5),
        "ffn_w2": nrm(28, (DEPTH, D_FF, d), D_FF ** -0.5),
    }


def reference(x_prompt, x_sample, cache_k, cache_v, state_ssm_re, state_ssm_im, c, c_ctx,
              norm_mix, norm_ffn, ada_w, ada_b,
              na_w_qkv, na_w_o, na_q_gain, na_k_gain, na_rpb,
              ssm_lambda_re, ssm_lambda_im, ssm_log_step, ssm_b_re, ssm_b_im, ssm_c_re, ssm_c_im,
              ssm_d, ssm_w_glu, ffn_w1, ffn_w3, ffn_w2):
    xp, xs = x_prompt, x_sample
    new_k, new_v, new_sre, new_sim = [], [], [], []
    for i in range(DEPTH):
        j = i // N_MIXERS
        m_ctx = ada_modulation(c_ctx, ada_w[i], ada_b[i])
        m_lat = ada_modulation(c, ada_w[i], ada_b[i])
        hp = modulate(rms_norm(xp, norm_mix[i]), m_ctx[0], m_ctx[1])
        hs = modulate(rms_norm(xs, norm_mix[i]), m_lat[0], m_lat[1])
        if i % N_MIXERS == 0:
            qp, kp, vp = na_qkv(hp, na_w_qkv[j], na_q_gain[j], na_k_gain[j])
            op = context_attention(qp, kp, vp) @ na_w_o[j]
            new_k.append(kp)
            new_v.append(vp)
            qs, ks_, vs = na_qkv(hs, na_w_qkv[j], na_q_gain[j], na_k_gain[j])
            os_ = neighbourhood_attention(qs, ks_, vs, cache_k[:, j], cache_v[:, j], na_rpb[j]) @ na_w_o[j]
        else:
            op, sre, sim = s5_mixer(hp, ssm_lambda_re[j], ssm_lambda_im[j], ssm_log_step[j],
                                    ssm_b_re[j], ssm_b_im[j], ssm_c_re[j], ssm_c_im[j],
                                    ssm_d[j], ssm_w_glu[j], None, None, True)
            new_sre.append(sre)
            new_sim.append(sim)
            os_ = s5_mixer(hs, ssm_lambda_re[j], ssm_lambda_im[j], ssm_log_step[j],
                           ssm_b_re[j], ssm_b_im[j], ssm_c_re[j], ssm_c_im[j],
                           ssm_d[j], ssm_w_glu[j], state_ssm_re[:, j], state_ssm_im[:, j], False)
        xp = xp + m_ctx[2] * op
        xs = xs + m_lat[2] * os_
        hp = modulate(rms_norm(xp, norm_ffn[i]), m_ctx[3], m_ctx[4])
        hs = modulate(rms_norm(xs, norm_ffn[i]), m_lat[3], m_lat[4])
        xp = xp + m_ctx[5] * swiglu(hp, ffn_w1[i], ffn_w3[i], ffn_w2[i])
        xs = xs + m_lat[5] * swiglu(hs, ffn_w1[i], ffn_w3[i], ffn_w2[i])
    new_cache_k = jnp.stack(new_k, axis=1)
    new_cache_v = jnp.stack(new_v, axis=1)
    new_state_ssm_re = jnp.stack(new_sre, axis=1)
    new_state_ssm_im = jnp.stack(new_sim, axis=1)
    return (xp, xs, new_cache_k, new_cache_v, new_state_ssm_re, new_state_ssm_im)
```

```python
import contextlib
import math
import numpy as np
import concourse.bass as bass
import concourse.mybir as mybir
from concourse.bass_utils import run_bass_kernel_spmd

F32 = mybir.dt.float32
BF16 = mybir.dt.bfloat16
AF = mybir.ActivationFunctionType
ALU = mybir.AluOpType

D = 1024
NCH = 8
NT = 2048
NP_TOK = 1024
DFF = 2816
NFF = 22
DEPTH = 4
EPS = 1e-6
NDMA_SEMS = 24


class Res:
    __slots__ = ("name", "w", "r")

    def __init__(self, name):
        self.name = name
        self.w = None
        self.r = {}


class Prog:
    def __init__(self, nc):
        self.nc = nc
        self.eng = {"pe": nc.tensor, "act": nc.scalar, "dve": nc.vector, "pool": nc.gpsimd, "sp": nc.sync}
        self.sem = {e: nc.alloc_semaphore(name=f"sem_{e}") for e in self.eng}
        self.cnt = {e: 0 for e in self.eng}
        self.seen = {e: {} for e in self.eng}
        self.ops = {e: [] for e in self.eng}
        self.dsem = [nc.alloc_semaphore(name=f"sem_dma{i}") for i in range(NDMA_SEMS)]
        self.dval = [0] * NDMA_SEMS
        self.drr = 0
        self.drr_sw = 0
        self.out_tickets = []

    def _sem_of(self, key):
        return self.sem[key] if isinstance(key, str) else self.dsem[key[1]]

    def _deps(self, eng, reads, writes, extra=()):
        need = {}

        def add(t):
            if t is None:
                return
            k, v = t
            if need.get(k, 0) < v:
                need[k] = v

        for r in reads:
            add(r.w)
        for w in writes:
            add(w.w)
            for k, v in w.r.items():
                add((k, v))
        for t in extra:
            add(t)
        waits = []
        for k, v in need.items():
            if eng == "pe" and k == "pe":
                continue
            if self.seen[eng].get(k, 0) < v:
                self.seen[eng][k] = v
                waits.append((k, v))
        return waits

    def _commit(self, tk, reads, writes):
        for r in reads:
            k, v = tk
            if r.r.get(k, 0) < v:
                r.r[k] = v
        for w in writes:
            w.w = tk
            w.r = {}

    def op(self, eng, fn, reads=(), writes=()):
        waits = self._deps(eng, reads, writes)
        self.cnt[eng] += 1
        tk = (eng, self.cnt[eng])
        self.ops[eng].append((waits, fn, self.sem[eng], 1))
        self._commit(tk, reads, writes)
        return tk

    def dma(self, q, out, in_, reads=(), writes=(), is_output=False):
        half = NDMA_SEMS // 2
        if q == "pool":
            i = self.drr_sw
            self.drr_sw = (self.drr_sw + 1) % half
        else:
            i = half + self.drr
            self.drr = (self.drr + 1) % half
        prev = self.dval[i]
        self.dval[i] = prev + 16
        key = ("dma", i)
        extra = [(key, prev)] if prev > 0 else []
        waits = self._deps(q, reads, writes, extra)
        e = self.eng[q]
        fn = lambda: e.dma_start(out=out, in_=in_)
        self.ops[q].append((waits, fn, self.dsem[i], 16))
        tk = (key, prev + 16)
        self._commit(tk, reads, writes)
        if is_output:
            self.out_tickets.append(tk)
        return tk

    def finish(self):
        need = {}
        for k, v in self.out_tickets:
            need[k] = max(need.get(k, 0), v)
        for i in range(NDMA_SEMS):
            if self.dval[i] > 0:
                need[("dma", i)] = self.dval[i]
        for e in self.eng:
            if e != "sp" and self.cnt[e] > 0:
                need[e] = self.cnt[e]
        waits = list(need.items())
        self.ops["sp"].append((waits, None, None, 0))

    def emit(self, block):
        def make(ename):
            def body(_e):
                e = self.eng[ename]
                for waits, fn, sem, inc in self.ops[ename]:
                    for k, v in waits:
                        e.wait_ge(self._sem_of(k), v)
                    if fn is not None:
                        ins = fn()
                        ins.then_inc(sem, inc)
            return body

        block.tensor(make("pe"))
        block.scalar(make("act"))
        block.vector(make("dve"))
        block.gpsimd(make("pool"))
        block.sync(make("sp"))


def _barrier(self):
    comp = [e for e in self.eng if e != "sp"]
    for e in self.eng:
        waits = []
        for o in comp:
            if o == e or self.cnt[o] == 0:
                continue
            if self.seen[e].get(o, 0) < self.cnt[o]:
                self.seen[e][o] = self.cnt[o]
                waits.append((o, self.cnt[o]))
        for i in range(NDMA_SEMS):
            k = ("dma", i)
            if self.dval[i] > 0 and self.seen[e].get(k, 0) < self.dval[i] and k in self.store_keys:
                self.seen[e][k] = self.dval[i]
                waits.append((k, self.dval[i]))
        if waits:
            self.ops[e].append((waits, None, None, 0))


Prog.barrier = _barrier


def build_program(stages=None):
    stages = ("na", "s5", "ffn") if stages is None else stages
    nc = bass.Bass("TRN2", target_bir_lowering=False)

    def din(name, shape, dt=F32):
        return nc.dram_tensor(name, list(shape), dt, kind="ExternalInput").ap()

    def dout(name, shape, dt=F32):
        return nc.dram_tensor(name, list(shape), dt, kind="ExternalOutput").ap()

    x_tok = din("x_tok", [NT, D])
    ident_in = din("ident", [128, 128])
    cond_in = din("cond", [128, NCH, 2])
    gmix_in = din("gmix", [128, DEPTH, NCH])
    gffn_in = din("gffn", [128, DEPTH, NCH])
    adab_in = din("adab", [128, DEPTH, 48])
    ada_w = din("ada_w", [DEPTH, D, 6 * D])
    ffn_w1 = din("ffn_w1", [DEPTH, D, DFF])
    ffn_w3 = din("ffn_w3", [DEPTH, D, DFF])
    ffn_w2 = din("ffn_w2", [DEPTH, DFF, D])
    na_w_qkv = din("na_w_qkv", [2, D, 3 * D])
    na_w_o = din("na_w_o", [2, D, D])
    qkg_in = din("qkg", [128, 2, 2])
    bones_in = din("bones", [128, 128])
    ck_in = din("ck", [2, 512, D])
    cv_in = din("cv", [2, 512, D])
    ebias_in = din("ebias", [2, 16, 128, 960])
    s5lam_in = din("s5lam", [2, 128, 3, 64])
    s5b_in = din("s5b", [2, 128, 2, 64, 16])
    s5c_in = din("s5c", [2, 128, 2, 64, 16])
    s5h0_in = din("s5h0", [2, 128, 2, 64])
    dvec_in = din("dvec", [128, 2, 64])
    masks_in = din("masks", [128, 256])
    glu_w = din("glu_w", [2, D, 2 * D])
    cW1 = nc.dram_tensor("cache_w1t", [2, 4, 128, 2048], BF16, kind="Internal").ap()
    cW2 = nc.dram_tensor("cache_w2", [2, 128, 8192], BF16, kind="Internal").ap()
    cM = nc.dram_tensor("cache_m", [2, 128, 4096], BF16, kind="Internal").ap()
    y_tok = dout("y_tok", [NT, D])
    st_out = dout("st_out", [4, 2, 2, 2, 32, 128])
    k_out = dout("k_out", [4, 2, 256, D])
    v_out = dout("v_out", [4, 2, 256, D])

    P = Prog(nc)
    P.store_keys = set()
    with contextlib.ExitStack() as es:
        def sb(name, shape, dt=F32):
            return es.enter_context(nc.sbuf_tensor(name, list(shape), dt))

        def ps(name, shape, dt=F32):
            return es.enter_context(nc.psum_tensor(name, list(shape), dt))

        xT = sb("xT", [128, NCH, NT])
        hT = sb("hT", [128, NCH, 1024], BF16)
        R1 = sb("R1", [128, 12288])
        R2 = sb("R2", [128, 10240])
        wada = [sb(f"wada{i}", [128, NCH, 128], BF16) for i in range(2)]
        ident = sb("ident_sb", [128, 128])
        ones = sb("ones_sb", [128, 128])
        cond = sb("cond_sb", [128, NCH, 2])
        scond = sb("scond_sb", [128, NCH, 2], BF16)
        gmix = sb("gmix_sb", [128, DEPTH, NCH])
        gffn = sb("gffn_sb", [128, DEPTH, NCH])
        adab = sb("adab_sb", [128, DEPTH, 48])
        mod = sb("mod_sb", [128, DEPTH, 48, 2])
        Amod = sb("Amod_sb", [128, DEPTH, 2, NCH, 2])
        sq = [sb(f"sq{i}", [128, 512]) for i in range(2)]
        rt = sb("rt_sb", [128, 512])
        rstd = sb("rstd_sb", [128, 512])
        tmp = [sb(f"tmp{i}", [128, 512]) for i in range(2)]
        EB = sb("EB_sb", [128, 2, 960])
        qkg = sb("qkg_sb", [128, 2, 2])
        bones = sb("bones_sb", [128, 128])
        ones_bf = sb("ones_bf_sb", [128, 64], BF16)
        identb = sb("identb_sb", [128, 128], BF16)
        epsc = sb("epsc_sb", [128, 1])
        masks = sb("masks_sb", [128, 256])
        dvec = sb("dvec_sb", [128, 2, 64])
        banks = [ps(f"bank{i}", [128, 512]) for i in range(8)]

        r_const = Res("const")
        r_bank = [Res(f"bank{i}") for i in range(8)]
        r_xT = [Res(f"xT{t}") for t in range(16)]
        r_hT = [Res(f"hT{t}") for t in range(2)]
        r_sq = [Res("sq0"), Res("sq1")]
        r_rt = Res("rt")
        r_rstd = Res("rstd")
        r_tmp = [Res("tmp0"), Res("tmp1")]
        r_wada = [Res("wada0"), Res("wada1")]
        r_mod = [Res(f"mod{i}") for i in range(DEPTH)]
        STAT_BANK, ADA_BANK = 6, 7
        state = {"bi": 0}

        def nb():
            b = state["bi"] % 6
            state["bi"] += 1
            return b

        def xres(t0, n):
            return r_xT[t0 // 128:(t0 + n + 127) // 128]

        def mm(out, lhsT, rhs, start, stop, reads, writes):
            P.op("pe", lambda: nc.tensor.matmul(out, lhsT, rhs, start=start, stop=stop), reads, writes)

        def tr(out, in_, idt, reads, writes):
            P.op("pe", lambda: nc.tensor.transpose(out=out, in_=in_, identity=idt), reads, writes)

        def act(out, in_, func, reads, writes, bias=None, scale=None):
            kw = {}
            if bias is not None:
                kw["bias"] = bias
            if scale is not None:
                kw["scale"] = scale
            P.op("act", lambda: nc.scalar.activation(out=out, in_=in_, func=func, **kw), reads, writes)

        def tt(eng, out, in0, in1, op, reads, writes):
            e = nc.vector if eng == "dve" else nc.gpsimd
            P.op(eng, lambda: e.tensor_tensor(out=out, in0=in0, in1=in1, op=op), reads, writes)

        def ts(eng, out, in0, s1, s2, op0, op1, reads, writes):
            e = nc.vector if eng == "dve" else nc.gpsimd
            if op1 is None:
                P.op(eng, lambda: e.tensor_scalar(out=out, in0=in0, scalar1=s1, scalar2=None, op0=op0), reads, writes)
            else:
                P.op(eng, lambda: e.tensor_scalar(out=out, in0=in0, scalar1=s1, scalar2=s2, op0=op0, op1=op1), reads, writes)

        def stt(out, in0, scalar, in1, op0, op1, reads, writes):
            P.op("dve", lambda: nc.vector.scalar_tensor_tensor(out=out, in0=in0, scalar=scalar, in1=in1, op0=op0, op1=op1),
                 reads, writes)

        def cp(eng, out, in_, reads, writes):
            if eng == "act":
                P.op("act", lambda: nc.scalar.copy(out=out, in_=in_), reads, writes)
            elif eng == "dve":
                P.op("dve", lambda: nc.vector.tensor_copy(out=out, in_=in_), reads, writes)
            else:
                P.op("pool", lambda: nc.gpsimd.tensor_copy(out=out, in_=in_), reads, writes)

        P.dma("sp", ident[:], ident_in[:, :], writes=[r_const])
        P.dma("sp", cond[:], cond_in[:, :, :], writes=[r_const])
        P.dma("sp", gmix[:], gmix_in[:, :, :], writes=[r_const])
        P.dma("sp", gffn[:], gffn_in[:, :, :], writes=[r_const])
        P.dma("sp", adab[:], adab_in[:, :, :], writes=[r_const])
        P.op("pool", lambda: nc.gpsimd.memset(ones[:], 1.0), writes=[r_const])
        P.op("pool", lambda: nc.gpsimd.memset(ones_bf[:], 1.0), writes=[r_const])
        P.op("pool", lambda: nc.gpsimd.memset(epsc[:], EPS), writes=[r_const])
        P.dma("sp", qkg[:], qkg_in[:, :, :], writes=[r_const])
        P.dma("sp", bones[:], bones_in[:, :], writes=[r_const])
        act(scond[:], cond[:], AF.Silu, [r_const], [r_const])
        P.dma("sp", masks[:], masks_in[:, :], writes=[r_const])
        P.dma("sp", dvec[:], dvec_in[:, :, :], writes=[r_const])
        cp("dve", identb[:], ident[:], [r_const], [r_const])
        stst = [rt[:, 0:128], rstd[:, 0:128]]
        r_stst = [r_rt, r_rstd]

        pending_ada = []

        def ada_step():
            for _ in range(4):
                if pending_ada:
                    pending_ada.pop(0)()

        def ada_flush():
            while pending_ada:
                pending_ada.pop(0)()

        def ada_layer(i):
            pending_ada.append(lambda: ada_dma(i, 0))
            pending_ada.append(lambda: ada_dma(i, 1))
            for cb in range(48):
                pending_ada.append(lambda cb=cb: ada_mm(i, cb))
                if cb + 2 < 48:
                    pending_ada.append(lambda cb=cb: ada_dma(i, cb + 2))
            pending_ada.append(lambda: ada_fin(i))

        def ada_dma(i, cb):
            s = cb % 2
            P.dma("pool", wada[s][:], ada_w[i, :, cb * 128:(cb + 1) * 128].rearrange("(k p) n -> p k n", p=128),
                  writes=[r_wada[s]])

        def ada_mm(i, cb):
            mps = banks[ADA_BANK][:, 0:96].rearrange("p (a b) -> p a b", b=2)
            if True:
                s = cb % 2
                for j2 in range(1):
                    j = cb
                    for kc in range(NCH):
                        mm(mps[:, j, :], wada[s][:, kc, :], scond[:, kc, :],
                           kc == 0, kc == NCH - 1, [r_wada[s], r_const], [r_bank[ADA_BANK]])
        def ada_fin(i):
            mps = banks[ADA_BANK][:, 0:96].rearrange("p (a b) -> p a b", b=2)
            for jc in range(2):
                tt("dve", mod[:, i, :, jc], mps[:, :, jc], adab[:, i, :], ALU.add, [r_bank[ADA_BANK], r_const], [r_mod[i]])
            for jc in range(2):
                stt(Amod[:, i, 0, :, jc], mod[:, i, 8:16, jc], 1.0, gmix[:, i, :], ALU.add, ALU.mult, [r_mod[i], r_const], [r_mod[i]])
                stt(Amod[:, i, 1, :, jc], mod[:, i, 32:40, jc], 1.0, gffn[:, i, :], ALU.add, ALU.mult, [r_mod[i], r_const], [r_mod[i]])

        def norm_mod(i, which, sbk):
            jc = sbk
            sh0 = 0 if which == 0 else 24
            for tb in range(2):
                t0 = sbk * 1024 + tb * 512
                xr = xres(t0, 512)
                for c in range(NCH):
                    s = c % 2
                    act(sq[s][:], xT[:, c, t0:t0 + 512], AF.Square, xr, [r_sq[s]])
                    mm(banks[STAT_BANK][:, :], ones[:], sq[s][:], c == 0, c == NCH - 1, [r_sq[s], r_const], [r_bank[STAT_BANK]])
                act(rt[:], banks[STAT_BANK][:, :], AF.Ln, [r_bank[STAT_BANK], r_const], [r_rt], scale=1.0 / D, bias=epsc[:, 0:1])
                act(rstd[:], rt[:], AF.Exp, [r_rt], [r_rstd], scale=-0.5)
                for c in range(NCH):
                    s = c % 2
                    stt(tmp[s][:], xT[:, c, t0:t0 + 512], Amod[:, i, which, c, jc:jc + 1], rstd[:], ALU.mult, ALU.mult,
                        xr + [r_rstd, r_mod[i]], [r_tmp[s]])
                    act(hT[:, c, tb * 512:(tb + 1) * 512], tmp[s][:], AF.Identity, [r_tmp[s], r_mod[i]], [r_hT[tb]],
                        bias=mod[:, i, sh0 + c, jc:jc + 1])

        gT = R1[:, 0:11264].bitcast(BF16).rearrange("p (j n) -> p j n", j=NFF)
        w13 = [[R2[:, (s * 2 + w) * 1024:(s * 2 + w + 1) * 1024].bitcast(BF16).rearrange("p (k n) -> p k n", k=NCH)
                for w in range(2)] for s in range(2)]
        w2b = [R2[:, 4096 + s * 2816:4096 + (s + 1) * 2816].bitcast(BF16).rearrange("p (j n) -> p j n", j=NFF)
               for s in range(2)]
        r_gT = [Res("gT0"), Res("gT1")]
        r_w13 = [Res("w13_0"), Res("w13_1")]
        r_w2 = [Res("w2_0"), Res("w2_1")]
        wcount = {"a": 0, "b": 0}

        def ffn(i, sbk, between=None):
            jc = sbk
            for wb in range(11):
                s = wcount["a"] % 2
                wcount["a"] += 1
                P.dma("pool", w13[s][0], ffn_w1[i, :, wb * 256:(wb + 1) * 256].rearrange("(k p) n -> p k n", p=128), writes=[r_w13[s]])
                P.dma("pool", w13[s][1], ffn_w3[i, :, wb * 256:(wb + 1) * 256].rearrange("(k p) n -> p k n", p=128), writes=[r_w13[s]])
                for j2 in range(2):
                    j = wb * 2 + j2
                    for tb in range(2):
                        b1, b3 = nb(), nb()
                        for kc in range(NCH):
                            mm(banks[b1][:, :], w13[s][0][:, kc, j2 * 128:(j2 + 1) * 128], hT[:, kc, tb * 512:(tb + 1) * 512],
                               kc == 0, kc == NCH - 1, [r_w13[s], r_hT[tb]], [r_bank[b1]])
                        for kc in range(NCH):
                            mm(banks[b3][:, :], w13[s][1][:, kc, j2 * 128:(j2 + 1) * 128], hT[:, kc, tb * 512:(tb + 1) * 512],
                               kc == 0, kc == NCH - 1, [r_w13[s], r_hT[tb]], [r_bank[b3]])
                        ti = (j * 2 + tb) % 2
                        act(tmp[ti][:], banks[b1][:, :], AF.Silu, [r_bank[b1]], [r_tmp[ti]])
                        tt("dve", gT[:, j, tb * 512:(tb + 1) * 512], tmp[ti][:], banks[b3][:, :], ALU.mult,
                           [r_tmp[ti], r_bank[b3]], [r_gT[tb]])
                ada_step()
            if between is not None:
                between()
            for cb in range(4):
                ada_step()
                s = wcount["b"] % 2
                wcount["b"] += 1
                P.dma("pool", w2b[s], ffn_w2[i, :, cb * 256:(cb + 1) * 256].rearrange("(j p) n -> p j n", p=128), writes=[r_w2[s]])
                for tb in range(2):
                    t0 = sbk * 1024 + tb * 512
                    for i2 in range(2):
                        dc = cb * 2 + i2
                        b = nb()
                        for j in range(NFF):
                            mm(banks[b][:, :], w2b[s][:, j, i2 * 128:(i2 + 1) * 128], gT[:, j, tb * 512:(tb + 1) * 512],
                               j == 0, j == NFF - 1, [r_w2[s], r_gT[tb]], [r_bank[b]])
                        stt(xT[:, dc, t0:t0 + 512], banks[b][:, :], mod[:, i, 40 + dc, jc:jc + 1], xT[:, dc, t0:t0 + 512],
                            ALU.mult, ALU.add, [r_bank[b], r_mod[i]] + xres(t0, 512), xres(t0, 512))


        qT = R1[:, 0:4096].bitcast(BF16).rearrange("p (c n) -> p c n", c=NCH)
        kT = R1[:, 4096:8192].bitcast(BF16).rearrange("p (c n) -> p c n", c=NCH)
        Vt = R1[:, 8192:12288].bitcast(BF16).rearrange("p (t n) -> p t n", t=8)
        Vs = R2[:, 0:3584].bitcast(BF16).rearrange("p (t n) -> p t n", t=7)
        ckst = [R2[:, s_ * 1024:(s_ + 1) * 1024] for s_ in range(2)]
        kstage = [R2[:, s_ * 512:(s_ + 1) * 512].rearrange("p (t n) -> p t n", t=4) for s_ in range(2)]
        vstage = [R2[:, 1024 + s_ * 256:1024 + (s_ + 1) * 256] for s_ in range(2)]
        kctxT = R2[:, 3584:5632].bitcast(BF16).rearrange("p (c n) -> p c n", c=NCH)
        vctx = R2[:, 5632:7680].bitcast(BF16).rearrange("p (t n) -> p t n", t=4)
        wblk = [R2[:, 7680 + s_ * 1024:7680 + (s_ + 1) * 1024].bitcast(BF16).rearrange("p (k n) -> p k n", k=NCH) for s_ in range(2)]
        PTc = [R2[:, 9728 + s_ * 256:9728 + (s_ + 1) * 256].bitcast(BF16) for s_ in range(2)]
        PTl = [sq[s_][:, 0:128].bitcast(BF16) for s_ in range(2)]
        oT = hT
        PTc3 = [PTc[0], PTc[1], tmp[0][:, 256:512].bitcast(BF16)]
        r_pt3 = [Res("ptc0"), Res("ptc1"), Res("ptc2")]
        Tl3 = [tmp[0][:, 0:256], tmp[1][:, 0:256], rt[:, 0:256]]
        r_tl3 = [r_tmp[0], r_tmp[1], r_rt]
        PTl3 = [PTl[0], PTl[1], rstd[:, 0:128].bitcast(BF16)]
        r_ptl3 = [r_sq[0], r_sq[1], r_rstd]
        nast = {"w": 0, "pt": 0, "ks": 0, "vs": 0, "eb": 0, "lt": 0}

        parts = {"ctx", "qk", "v", "attp", "atts", "o"}

        def na_layer(i):
            j = i // 2
            for sbk in range(2):
                jc = sbk
                P.barrier()
                r_q, r_k, r_v, r_vs = Res("q"), Res("k"), Res("v"), Res("vs")
                r_wb = [Res("wb0"), Res("wb1")]
                r_pt = [Res("pt0"), Res("pt1")]
                r_kst = [Res("kst0"), Res("kst1")]
                r_vst = [Res("vst0"), Res("vst1")]
                r_ckst = [Res("ckst0"), Res("ckst1")]
                r_kctx, r_vctx = Res("kctx"), Res("vctx")
                r_eb = [Res("eb0"), Res("eb1")]
                norm_mod(i, 0, sbk)
                if sbk == 1 and "ctx" in parts:
                    for t in range(4):
                        s_ = t % 2
                        P.dma("sp", ckst[s_], ck_in[j, t * 128:(t + 1) * 128, :], writes=[r_ckst[s_]])
                        for half in range(2):
                            b = nb()
                            for c4 in range(4):
                                c = half * 4 + c4
                                tr(banks[b][:, c4 * 128:(c4 + 1) * 128], ckst[s_][:, c * 128:(c + 1) * 128], ident[:],
                                   [r_ckst[s_], r_const], [r_bank[b]])
                            cp("act", kctxT[:, half * 4:half * 4 + 4, t * 128:(t + 1) * 128],
                               banks[b][:, :].rearrange("p (c n) -> p c n", c=4), [r_bank[b]], [r_kctx])
                    P.dma("pool", vctx, cv_in[j, :, :].rearrange("(t p) n -> p t n", p=128), writes=[r_vctx])
                pend_qk = []
                for cb in range(8 if "qk" in parts else 0):
                    s_ = nast["w"] % 2
                    nast["w"] += 1
                    P.dma("pool", wblk[s_], na_w_qkv[j, :, cb * 256:(cb + 1) * 256].rearrange("(k p) n -> p k n", p=128),
                          writes=[r_wb[s_]])
                    isk = cb >= 4
                    for m2 in range(2):
                        m = (cb % 4) * 2 + m2
                        for tb in range(2):
                            b = nb()
                            for kc in range(NCH):
                                mm(banks[b][:, :], wblk[s_][:, kc, m2 * 128:(m2 + 1) * 128], hT[:, kc, tb * 512:(tb + 1) * 512],
                                   kc == 0, kc == NCH - 1, [r_wb[s_], r_hT[tb]], [r_bank[b]])
                            q_ = (m * 2 + tb) % 2
                            act(sq[q_][:], banks[b][:, :], AF.Square, [r_bank[b]], [r_sq[q_]])

                            def _stage_b(b=b, q_=q_, m=m, tb=tb, isk=isk):
                                mm(banks[STAT_BANK][:, :], bones[:], sq[q_][:], True, True, [r_sq[q_], r_const], [r_bank[STAT_BANK]])
                                act(rt[:], banks[STAT_BANK][:, :], AF.Ln, [r_bank[STAT_BANK], r_const], [r_rt], scale=1.0 / 64, bias=epsc[:, 0:1])
                                act(rstd[:], rt[:], AF.Exp, [r_rt], [r_rstd], scale=-0.5)
                                gsc = qkg[:, j, (1 if isk else 0):(2 if isk else 1)]
                                if not isk:
                                    stt(qT[:, m, tb * 512:(tb + 1) * 512], banks[b][:, :], gsc, rstd[:], ALU.mult, ALU.mult,
                                        [r_bank[b], r_rstd, r_const], [r_q])
                                elif sbk == 1:
                                    stt(kT[:, m, tb * 512:(tb + 1) * 512], banks[b][:, :], gsc, rstd[:], ALU.mult, ALU.mult,
                                        [r_bank[b], r_rstd, r_const], [r_k])
                                else:
                                    stt(tmp[q_][:], banks[b][:, :], gsc, rstd[:], ALU.mult, ALU.mult,
                                        [r_bank[b], r_rstd, r_const], [r_tmp[q_]])
                                    cp("act", kT[:, m, tb * 512:(tb + 1) * 512], tmp[q_][:], [r_tmp[q_]], [r_k])
                                    b2 = nb()
                                    for t4 in range(4):
                                        tr(banks[b2][:, t4 * 128:(t4 + 1) * 128], tmp[q_][:, t4 * 128:(t4 + 1) * 128], ident[:],
                                           [r_tmp[q_], r_const], [r_bank[b2]])
                                    ks = nast["ks"] % 2
                                    nast["ks"] += 1
                                    cp("act", kstage[ks], banks[b2][:, :].rearrange("p (t n) -> p t n", t=4), [r_bank[b2]], [r_kst[ks]])
                                    for sq_ in range(2):
                                        seq = tb * 2 + sq_
                                        tk = P.dma("sp", k_out[seq, j, :, m * 128:(m + 1) * 128].rearrange("(t p) n -> p t n", p=128),
                                                   kstage[ks][:, sq_ * 2:sq_ * 2 + 2, :], reads=[r_kst[ks]], is_output=True)
                                        P.store_keys.add(tk[0])

                            pend_qk.append(_stage_b)
                            if len(pend_qk) > 1:
                                pend_qk.pop(0)()
                while pend_qk:
                    pend_qk.pop(0)()
                for cb in (range(8, 12) if "v" in parts else []):
                    s_ = nast["w"] % 2
                    nast["w"] += 1
                    P.dma("pool", wblk[s_], na_w_qkv[j, :, cb * 256:(cb + 1) * 256].rearrange("(k p) n -> p k n", p=128),
                          writes=[r_wb[s_]])
                    vc0 = (cb - 8) * 256
                    ntile = 15 if sbk == 1 else 8
                    for tt_ in range(ntile):
                        shifted = tt_ >= 8
                        tok0 = (tt_ - 8) * 128 + 64 if shifted else tt_ * 128
                        b = nb()
                        for kc in range(NCH):
                            mm(banks[b][:, 0:256], hT[:, kc, tok0:tok0 + 128], wblk[s_][:, kc, :],
                               kc == 0, kc == NCH - 1, [r_wb[s_], r_hT[0], r_hT[1]], [r_bank[b]])
                        if shifted:
                            cp("act", Vs[:, tt_ - 8, vc0:vc0 + 256], banks[b][:, 0:256], [r_bank[b]], [r_vs])
                        elif sbk == 1:
                            cp("act", Vt[:, tt_, vc0:vc0 + 256], banks[b][:, 0:256], [r_bank[b]], [r_v])
                        else:
                            if True:
                                vs_ = nast["vs"] % 2
                                nast["vs"] += 1
                                cp("act", vstage[vs_], banks[b][:, 0:256], [r_bank[b]], [r_vst[vs_]])
                                cp("dve", Vt[:, tt_, vc0:vc0 + 256], vstage[vs_], [r_vst[vs_]], [r_v])
                                tk = P.dma("sp", v_out[tt_ // 2, j, (tt_ % 2) * 128:(tt_ % 2) * 128 + 128, vc0:vc0 + 256],
                                           vstage[vs_], reads=[r_vst[vs_]], is_output=True)
                                P.store_keys.add(tk[0])
                if sbk == 0 and "attp" in parts:
                    pend_p = []
                    for seq in range(4):
                        for m in range(NCH):
                            bo = nb()
                            for hh in range(2):
                                h = 2 * m + hh
                                pb = 64 * hh
                                bs = nb()
                                for kt in range(2):
                                    mm(banks[bs][:, kt * 256:(kt + 1) * 256],
                                       kT[pb:pb + 64, m, seq * 256 + kt * 128:seq * 256 + kt * 128 + 128],
                                       qT[pb:pb + 64, m, seq * 256:seq * 256 + 256], True, True, [r_q, r_k], [r_bank[bs]])
                                k3 = nast["pt"] % 3
                                nast["pt"] += 1
                                act(PTc3[k3], banks[bs][:, :], AF.Exp, [r_bank[bs]], [r_pt3[k3]], scale=0.125)

                                def _pv(bo=bo, k3=k3, pb=pb, h=h, hh=hh, seq=seq, m=m):
                                    for kt in range(2):
                                        mm(banks[bo][pb:pb + 64, 0:256], Vt[:, seq * 2 + kt, h * 64:(h + 1) * 64],
                                           PTc3[k3][:, kt * 256:(kt + 1) * 256], kt == 0, kt == 1, [r_v, r_pt3[k3]], [r_bank[bo]])
                                    for kt in range(2):
                                        mm(banks[bo][pb:pb + 64, 256:512], ones_bf[:, :],
                                           PTc3[k3][:, kt * 256:(kt + 1) * 256], kt == 0, kt == 1, [r_const, r_pt3[k3]], [r_bank[bo]])
                                    if hh == 1:
                                        P.op("dve", lambda: nc.vector.reciprocal(out=rstd[:, 0:256], in_=banks[bo][:, 256:512]),
                                             [r_bank[bo]], [r_rstd])
                                        tt("dve", oT[:, m, seq * 256:seq * 256 + 256], banks[bo][:, 0:256], rstd[:, 0:256], ALU.mult,
                                           [r_bank[bo], r_rstd], [r_hT[seq // 2]])

                                pend_p.append(_pv)
                                if len(pend_p) > 2:
                                    pend_p.pop(0)()
                    while pend_p:
                        pend_p.pop(0)()
                elif sbk == 1 and "atts" in parts:
                    for m in range(NCH):
                        for hh in range(2):
                            h = 2 * m + hh
                            pb = 64 * hh
                            e_ = nast["eb"] % 2
                            nast["eb"] += 1
                            P.dma("sp", EB[:, e_, :], ebias_in[j, h, :, :], writes=[r_eb[e_]])
                            ebv = EB[:, e_, :].rearrange("p (d q) -> p d q", q=64)
                            first = [True, True]
                            units = [("c", qb, kt) for qb in range(2) for kt in range(4)] + [("l", r) for r in range(16)]

                            def stage_a(u, slot):
                                bs = 4 + slot % 3
                                k3 = slot % 3
                                if u[0] == "c":
                                    _, qb, kt = u
                                    mm(banks[bs][:, :], kctxT[pb:pb + 64, m, kt * 128:(kt + 1) * 128],
                                       qT[pb:pb + 64, m, qb * 512:(qb + 1) * 512], True, True, [r_q, r_kctx], [r_bank[bs]])
                                    act(PTc3[k3], banks[bs][:, :], AF.Exp, [r_bank[bs]], [r_pt3[k3]], scale=0.125)
                                else:
                                    r = u[1]
                                    rs = min(max(r - 4, 0), 8)
                                    d0 = rs - r + 7
                                    for m4 in range(4):
                                        kr0 = rs + 2 * m4
                                        mm(banks[bs][:, m4 * 64:(m4 + 1) * 64], kT[pb:pb + 64, m, kr0 * 64:kr0 * 64 + 128],
                                           qT[pb:pb + 64, m, r * 64:(r + 1) * 64], True, True, [r_q, r_k], [r_bank[bs]])
                                    stt(Tl3[k3].rearrange("p (a q) -> p a q", q=64),
                                        banks[bs][:, 0:256].rearrange("p (a q) -> p a q", q=64), 0.125,
                                        ebv[:, d0:d0 + 7:2, :], ALU.mult, ALU.add, [r_bank[bs], r_eb[e_]], [r_tl3[k3]])
                                    act(PTl3[k3], Tl3[k3], AF.Exp, [r_tl3[k3]], [r_ptl3[k3]])

                            def stage_b(u, slot):
                                k3 = slot % 3
                                if u[0] == "c":
                                    _, qb, kt = u
                                    mm(banks[qb][pb:pb + 64, :], vctx[:, kt, h * 64:(h + 1) * 64], PTc3[k3], first[qb], False,
                                       [r_vctx, r_pt3[k3]], [r_bank[qb]])
                                    mm(banks[2 + qb][pb:pb + 64, :], ones_bf[:, :], PTc3[k3], first[qb], False,
                                       [r_const, r_pt3[k3]], [r_bank[2 + qb]])
                                    first[qb] = False
                                else:
                                    r = u[1]
                                    rs = min(max(r - 4, 0), 8)
                                    qb = r // 8
                                    qc0 = (r % 8) * 64
                                    for m4 in range(4):
                                        kr0 = rs + 2 * m4
                                        vsrc = Vt[:, kr0 // 2, h * 64:(h + 1) * 64] if kr0 % 2 == 0 else Vs[:, (kr0 - 1) // 2, h * 64:(h + 1) * 64]
                                        last = (r % 8 == 7) and m4 == 3
                                        mm(banks[qb][pb:pb + 64, qc0:qc0 + 64], vsrc, PTl3[k3][:, m4 * 64:(m4 + 1) * 64], False, last,
                                           [r_v, r_vs, r_ptl3[k3]], [r_bank[qb]])
                                        mm(banks[2 + qb][pb:pb + 64, qc0:qc0 + 64], ones_bf[:, :], PTl3[k3][:, m4 * 64:(m4 + 1) * 64],
                                           False, last, [r_const, r_ptl3[k3]], [r_bank[2 + qb]])

                            g0 = nast["lt"]
                            for t_ in range(len(units) + 2):
                                if t_ < len(units):
                                    stage_a(units[t_], g0 + t_)
                                if t_ >= 2:
                                    stage_b(units[t_ - 2], g0 + t_ - 2)
                            nast["lt"] += len(units)
                        for qb in range(2):
                            P.op("dve", lambda qb=qb: nc.vector.reciprocal(out=rt[:], in_=banks[2 + qb][:, :]),
                                 [r_bank[2 + qb]], [r_rt])
                            tt("dve", oT[:, m, qb * 512:(qb + 1) * 512], banks[qb][:, :], rt[:], ALU.mult,
                               [r_bank[qb], r_rt], [r_hT[qb]])
                for cb in range(4 if "o" in parts else 0):
                    s_ = nast["w"] % 2
                    nast["w"] += 1
                    P.dma("pool", wblk[s_], na_w_o[j, :, cb * 256:(cb + 1) * 256].rearrange("(k p) n -> p k n", p=128),
                          writes=[r_wb[s_]])
                    for tb in range(2):
                        t0 = sbk * 1024 + tb * 512
                        for i2 in range(2):
                            dc = cb * 2 + i2
                            b = nb()
                            for kc in range(NCH):
                                mm(banks[b][:, :], wblk[s_][:, kc, i2 * 128:(i2 + 1) * 128], oT[:, kc, tb * 512:(tb + 1) * 512],
                                   kc == 0, kc == NCH - 1, [r_wb[s_], r_hT[tb]], [r_bank[b]])
                            stt(xT[:, dc, t0:t0 + 512], banks[b][:, :], mod[:, i, 16 + dc, jc:jc + 1], xT[:, dc, t0:t0 + 512],
                                ALU.mult, ALU.add, [r_bank[b], r_mod[i]] + xres(t0, 512), xres(t0, 512))
                ada_step()
            P.barrier()

        S5B = sb("S5B_sb", [128, 2304])
        R1t, R2t, hTt, EBt, S5t = R1, R2, hT, EB, S5B

        def raw(t, psz, off, dims):
            return bass.AP(t, off, [[psz, 128]] + [list(d) for d in dims])

        TM8 = R1[:, 8192:12288].bitcast(BF16)
        TM8t = TM8.tensor
        W2 = R2[:, 0:4096].bitcast(BF16).rearrange("p (d g r n) -> p d g r n", d=2, g=16, r=2)
        Msb = R2[:, 4096:6144].bitcast(BF16).rearrange("p (g n) -> p g n", g=32)
        Z1 = R2[:, 6144:7168].bitcast(BF16).rearrange("p (d g r n) -> p d g r n", d=2, g=4, r=2)
        Z2 = R2[:, 7168:8192].bitcast(BF16).rearrange("p (d g r n) -> p d g r n", d=2, g=4, r=2)
        W1T = R2[:, 8192:9216].bitcast(BF16).rearrange("p (d g r n) -> p d g r n", d=2, g=4, r=2)
        T1 = R2[:, 9216:9728]
        T2 = R2[:, 9728:10240]
        Xin = hT[:, :, :].rearrange("p c n -> p (c n)").rearrange("p (d f k) -> p d f k", d=2, f=32)
        Ust = [sq[s_][:, 0:256].bitcast(BF16) for s_ in range(2)]
        ebf = EB[:, :, :].rearrange("p a n -> p (a n)")
        LAM = ebf[:, 0:192].rearrange("p (a f) -> p a f", a=3)
        MISC = ebf[:, 192:1216].rearrange("p (a f) -> p a f", f=64)
        BRAW = ebf[:, 1216:1472]
        CRAW = ebf[:, 1472:1728]
        H0T = ebf[:, 1728:1792]
        TS = ebf[:, 1856:1920]
        APOW = S5B[:, 0:1152]
        AINV = S5B[:, 1152:2304]
        FIN = T2[:, 256:512]
        r_s5c = Res("s5c")
        r_tm8, r_S, r_W2, r_M, r_Z1, r_Z2, r_W1T = Res("tm8"), Res("S"), Res("W2"), Res("M"), Res("Z1"), Res("Z2"), Res("W1T")
        r_T1, r_T2, r_braw, r_craw, r_xin, r_fin, r_h0 = Res("T1"), Res("T2"), Res("braw"), Res("craw"), Res("xin"), Res("fin"), Res("h0")
        r_ust = [Res("ust0"), Res("ust1")]
        s5st = {"u": 0, "w": 0}
        r_cW1 = [[Res(f"cw1_{a}_{c}") for c in range(4)] for a in range(2)]
        r_cW2 = [Res("cw2_0"), Res("cw2_1")]
        r_cM = [Res("cm_0"), Res("cm_1")]
        W1Tflat = R2[:, 8192:9216].bitcast(BF16)
        W2flat = R2[:, 0:4096].bitcast(BF16)
        Mflat = R2[:, 4096:6144].bitcast(BF16)

        def mi(k):
            return MISC[:, k, :]

        def s5_discretise(j):
            P.dma("sp", LAM, s5lam_in[j, :, :, :], writes=[r_s5c])
            rc = [r_s5c]
            lre, lim, lst = LAM[:, 0, :], LAM[:, 1, :], LAM[:, 2, :]
            step, mag, th2, kk, are, aim, den, nr, t_a, t_b, fre, fim = [mi(k) for k in range(12)]
            act(step, lst, AF.Exp, rc, rc)
            tt("dve", t_a, lre, step, ALU.mult, rc, rc)
            act(mag, t_a, AF.Exp, rc, rc)
            TH = MISC[:, 12:14, :]
            SC = MISC[:, 14:16, :]
            tt("dve", TH[:, 0, :], lim, step, ALU.mult, rc, rc)
            ts("dve", TH[:, 1, :], TH[:, 0, :], math.pi / 2, None, ALU.add, None, rc, rc)
            KI = T1[:, 0:128].bitcast(mybir.dt.int32).rearrange("p (a f) -> p a f", a=2)
            KF = T2[:, 0:128].rearrange("p (a f) -> p a f", a=2)
            ts("dve", KF, TH, 1.0 / (2 * math.pi), None, ALU.mult, None, rc, [r_T2])
            cp("dve", KI, KF, [r_T2], [r_T1])
            cp("dve", KF, KI, [r_T1], [r_T2])
            stt(TH, KF, -2 * math.pi, TH, ALU.mult, ALU.add, [r_T2] + rc, rc)
            ts("dve", KF, TH, math.pi, -2 * math.pi, ALU.is_gt, ALU.mult, rc, [r_T2])
            tt("dve", TH, TH, KF, ALU.add, [r_T2] + rc, rc)
            ts("dve", KF, TH, -math.pi, 2 * math.pi, ALU.is_lt, ALU.mult, rc, [r_T2])
            tt("dve", TH, TH, KF, ALU.add, [r_T2] + rc, rc)
            ts("dve", TH, TH, math.pi, -math.pi, ALU.min, ALU.max, rc, rc)
            act(SC, TH, AF.Sin, rc, rc)
            tt("dve", aim, mag, SC[:, 0, :], ALU.mult, rc, rc)
            tt("dve", are, mag, SC[:, 1, :], ALU.mult, rc, rc)
            tt("dve", den, lre, lre, ALU.mult, rc, rc)
            tt("dve", t_a, lim, lim, ALU.mult, rc, rc)
            tt("dve", den, den, t_a, ALU.add, rc, rc)
            P.op("dve", lambda: nc.vector.reciprocal(out=den, in_=den), rc, rc)
            ts("dve", nr, are, -1.0, None, ALU.add, None, rc, rc)
            tt("dve", t_a, nr, lre, ALU.mult, rc, rc)
            tt("dve", t_b, aim, lim, ALU.mult, rc, rc)
            tt("dve", t_a, t_a, t_b, ALU.add, rc, rc)
            tt("dve", fre, t_a, den, ALU.mult, rc, rc)
            tt("dve", t_a, aim, lre, ALU.mult, rc, rc)
            tt("dve", t_b, nr, lim, ALU.mult, rc, rc)
            tt("dve", t_a, t_a, t_b, ALU.subtract, rc, rc)
            tt("dve", fim, t_a, den, ALU.mult, rc, rc)
            ire, iim = mi(3), mi(6)
            tt("dve", t_a, mag, mag, ALU.mult, rc, rc)
            P.op("dve", lambda: nc.vector.reciprocal(out=t_a, in_=t_a), rc, rc)
            tt("dve", ire, are, t_a, ALU.mult, rc, rc)
            stt(iim, aim, -1.0, t_a, ALU.mult, ALU.mult, rc, rc)

            def pw(tab, e):
                return tab[:, e * 128:e * 128 + 64], tab[:, e * 128 + 64:e * 128 + 128]

            def cmul(o_re, o_im, x_re, x_im, y_re, y_im):
                tt("dve", t_a, x_re, y_re, ALU.mult, rc, rc)
                tt("dve", t_b, x_im, y_im, ALU.mult, rc, rc)
                tt("dve", TS, x_re, y_im, ALU.mult, rc, rc)
                tt("dve", o_re, t_a, t_b, ALU.subtract, rc, rc)
                tt("dve", t_a, x_im, y_re, ALU.mult, rc, rc)
                tt("dve", o_im, TS, t_a, ALU.add, rc, rc)

            for tab, (b_re, b_im) in ((APOW, (are, aim)), (AINV, (ire, iim))):
                r0, i0 = pw(tab, 0)
                P.op("pool", lambda r0=r0: nc.gpsimd.memset(r0, 1.0), rc, rc)
                P.op("pool", lambda i0=i0: nc.gpsimd.memset(i0, 0.0), rc, rc)
                r1, i1 = pw(tab, 1)
                cp("dve", r1, b_re, rc, rc)
                cp("dve", i1, b_im, rc, rc)
                for e in range(2, 9):
                    pr, pi_ = pw(tab, e - 1)
                    cr, ci = pw(tab, e)
                    cmul(cr, ci, pr, pi_, b_re, b_im)
            P.op("pool", lambda: nc.gpsimd.memset(TS, 0.0), rc, rc)
            return fre, fim

        def cprod(o_view, tab, e0, estep, xraw, d, gp0, neg_im, dstr=1024):
            fo = d * 32 + gp0
            er = raw(S5t, 2304, (0 if tab is APOW else 1152) + e0 * 128 + fo, [(estep * 128, 8), (1, 4), (0, 16)])
            ei = raw(S5t, 2304, (0 if tab is APOW else 1152) + e0 * 128 + 64 + fo, [(estep * 128, 8), (1, 4), (0, 16)])
            xoff = 1216 if xraw is BRAW else 1472
            xr = raw(EBt, 1920, xoff + (0 * 2 + d) * 64, [(0, 8), (16, 4), (1, 16)])
            xi = raw(EBt, 1920, xoff + (1 * 2 + d) * 64, [(0, 8), (16, 4), (1, 16)])
            t1 = raw(R2t, 10240, 9216, [(16, 8), (128, 4), (1, 16)])
            t2 = raw(R2t, 10240, 9728, [(16, 8), (128, 4), (1, 16)])
            ovt = o_view.tensor
            obase = o_view.offset
            o_re = raw(ovt, 20480, obase + d * dstr, [(16, 8), (256, 4), (1, 16)])
            o_im = raw(ovt, 20480, obase + d * dstr + 128, [(16, 8), (256, 4), (1, 16)])
            rx = [r_s5c, r_braw if xraw is BRAW else r_craw]
            ro = [r_Z1 if o_view is Z1 else (r_Z2 if o_view is Z2 else r_W2)]
            tt("dve", t1, er, xr, ALU.mult, rx, [r_T1])
            tt("pool", t2, ei, xi, ALU.mult, rx, [r_T2])
            tt("dve", o_re, t1, t2, ALU.subtract, [r_T1, r_T2], ro)
            tt("dve", t1, er, xi, ALU.mult, rx, [r_T1])
            tt("pool", t2, ei, xr, ALU.mult, rx, [r_T2])
            if neg_im:
                tt("dve", t1, t1, t2, ALU.add, [r_T1, r_T2], [r_T1])
                tt("dve", o_im, raw(EBt, 1920, 1856, [(0, 8), (0, 4), (0, 16)]), t1, ALU.subtract, [r_T1, r_s5c], ro)
            else:
                tt("dve", o_im, t1, t2, ALU.add, [r_T1, r_T2], ro)

        def s5_layer(i):
            j = i // 2
            P.barrier()
            fre, fim = s5_discretise(j)
            rc = [r_s5c]
            for sbk in range(2):
                jc = sbk
                P.barrier()
                norm_mod(i, 0, sbk)
                for s_ in range(8):
                    b = nb()
                    pb16 = banks[b][:, :].bitcast(BF16)
                    for dc in range(NCH):
                        hsl = raw(hTt, 8192, dc * 1024 + s_, [(8, 128)])
                        tr(pb16[:, dc * 128:(dc + 1) * 128], hsl, identb[:], [r_hT[0], r_hT[1], r_const], [r_bank[b]])
                    cp("act", raw(TM8t, 24576, TM8.offset + s_ * 16, [(128, 64), (1, 16)]),
                       pb16.rearrange("p (g c) -> p g c", c=16), [r_bank[b]], [r_tm8])
                for gb in range(2):
                    P.barrier()
                    if sbk == 1:
                        P.dma("sp", H0T, s5h0_in[j, :, gb, :], writes=[r_h0])
                    for sub in range(4):
                        gp0 = gb * 16 + sub * 4
                        if sbk == 1:
                            P.dma("sp", W1Tflat, cW1[gb, sub, :, :], reads=[r_cW1[gb][sub]], writes=[r_W1T])
                            if sub == 0:
                                P.dma("sp", W2flat, cW2[gb, :, :], reads=[r_cW2[gb]], writes=[r_W2])
                                P.dma("sp", Mflat, cM[gb, :, :], reads=[r_cM[gb]], writes=[r_M])
                        else:
                            for d in range(2):
                                P.dma("sp", BRAW.rearrange("p (r d g c) -> p r d g c", r=2, d=2, g=4)[:, :, d, :, :],
                                      s5b_in[j, :, :, d * 32 + gp0:d * 32 + gp0 + 4, :], writes=[r_braw])
                                P.dma("sp", CRAW.rearrange("p (r d g c) -> p r d g c", r=2, d=2, g=4)[:, :, d, :, :],
                                      s5c_in[j, :, :, d * 32 + gp0:d * 32 + gp0 + 4, :], writes=[r_craw])
                            bv = BRAW.rearrange("p (r f c) -> p r f c", r=2, f=8)
                            for d in range(2):
                                fr_ = raw(EBt, 1920, 192 + 10 * 64 + d * 32 + gp0, [(1, 4), (0, 16)])
                                fi_ = raw(EBt, 1920, 192 + 11 * 64 + d * 32 + gp0, [(1, 4), (0, 16)])
                                br = bv[:, 0, d * 4:(d + 1) * 4, :]
                                bi_ = bv[:, 1, d * 4:(d + 1) * 4, :]
                                u1 = T1[:, 0:64].rearrange("p (g c) -> p g c", c=16)
                                u2 = T2[:, 0:64].rearrange("p (g c) -> p g c", c=16)
                                u3 = T1[:, 64:128].rearrange("p (g c) -> p g c", c=16)
                                u4 = T2[:, 64:128].rearrange("p (g c) -> p g c", c=16)
                                tt("dve", u1, fr_, br, ALU.mult, [r_s5c, r_braw], [r_T1])
                                tt("dve", u2, fi_, bi_, ALU.mult, [r_s5c, r_braw], [r_T2])
                                tt("dve", u3, fr_, bi_, ALU.mult, [r_s5c, r_braw], [r_T1])
                                tt("dve", u4, fi_, br, ALU.mult, [r_s5c, r_braw], [r_T2])
                                tt("dve", br, u1, u2, ALU.subtract, [r_T1, r_T2], [r_braw])
                                tt("dve", bi_, u3, u4, ALU.add, [r_T1, r_T2], [r_braw])
                            for d in range(2):
                                cprod(Z1, APOW, 7 if d == 0 else 0, -1 if d == 0 else 1, BRAW, d, gp0, False)
                                cprod(Z2, AINV, 7 if d == 0 else 0, -1 if d == 0 else 1, CRAW, d, gp0, True)
                            for hb in range(2):
                                b = nb()
                                pb16 = banks[b][:, :].bitcast(BF16)
                                for q4 in range(8):
                                    idx = hb * 8 + q4
                                    d, gl, r = idx // 8, (idx // 2) % 4, idx % 2
                                    tr(pb16[:, q4 * 128:(q4 + 1) * 128], Z1[:, d, gl, r, :], identb[:], [r_Z1, r_const], [r_bank[b]])
                                cp("act", W1T[:, hb, :, :, :].rearrange("p g r n -> p (g r n)"), pb16, [r_bank[b]], [r_W1T])
                            for gl in range(4):
                                for g2 in range(2):
                                    g = 2 * (gp0 + gl) + g2
                                    glb = 2 * (sub * 4 + gl) + g2
                                    b = nb()
                                    for d in range(2):
                                        for r in range(2):
                                            mm(banks[b][:, d * 128:(d + 1) * 128], Z1[g2 * 64:(g2 + 1) * 64, d, gl, r, :],
                                               Z2[g2 * 64:(g2 + 1) * 64, d, gl, r, :], r == 0, r == 1, [r_Z1, r_Z2], [r_bank[b]])
                                    tt("dve", T1[:, 0:256], banks[b][:, 0:256], masks[:, :], ALU.mult, [r_bank[b], r_const], [r_T1])
                                    tt("dve", T1[:, 0:128], T1[:, 0:128], T1[:, 128:256], ALU.add, [r_T1], [r_T1])
                                    stt(Msb[:, glb, :], ident[:], dvec[:, j, g:g + 1], T1[:, 0:128], ALU.mult, ALU.add,
                                        [r_T1, r_const], [r_M])
                            for d in range(2):
                                cprod(W2[:, :, sub * 4:(sub + 1) * 4, :, :], APOW, 1 if d == 0 else 8, 1 if d == 0 else -1, CRAW, d, gp0, True, dstr=4096)
                            tk = P.dma("sp", cW1[gb, sub, :, :], W1Tflat, reads=[r_W1T], writes=[r_cW1[gb][sub]])
                            P.store_keys.add(tk[0])
                            if sub == 3:
                                tk = P.dma("sp", cW2[gb, :, :], W2flat, reads=[r_W2], writes=[r_cW2[gb]])
                                P.store_keys.add(tk[0])
                                tk = P.dma("sp", cM[gb, :, :], Mflat, reads=[r_M], writes=[r_cM[gb]])
                                P.store_keys.add(tk[0])
                        for gl in range(4):
                            gpl = sub * 4 + gl
                            u_ = s5st["u"] % 2
                            s5st["u"] += 1
                            bu = nb()
                            pu16 = banks[bu][:, :].bitcast(BF16)
                            for g2 in range(2):
                                g = 2 * (gp0 + gl) + g2
                                tr(pu16[:, g2 * 128:(g2 + 1) * 128], TM8[:, g * 128:(g + 1) * 128],
                                   identb[:], [r_tm8, r_const], [r_bank[bu]])
                            cp("act", Ust[u_][:, 0:256], pu16[:, 0:256], [r_bank[bu]], [r_ust[u_]])
                            b = nb()
                            for d in range(2):
                                for r in range(2):
                                    for g2 in range(2):
                                        mm(banks[b][g2 * 64:(g2 + 1) * 64, (d * 2 + r) * 128:(d * 2 + r + 1) * 128],
                                           W1T[:, d, gl, r, g2 * 64:(g2 + 1) * 64], Ust[u_][:, g2 * 128:(g2 + 1) * 128],
                                           True, True, [r_W1T, r_ust[u_]], [r_bank[b]])
                            cp("act", raw(R1t, 12288, gpl * 256, [(4096, 2), (128, 2), (1, 128)]),
                               banks[b][:, :].rearrange("p (d r k) -> p d r k", d=2, r=2), [r_bank[b]], [r_S])
                        ada_step()
                    nseq = 4 if sbk == 0 else 1
                    kl = 32 if sbk == 0 else 128

                    def sview(kk, sel):
                        dstr = 4096 + (kl - 1) - 2 * kk
                        if sel is None:
                            dims = [(dstr, 2), (128, 32)]
                        else:
                            dims = [(dstr, 2), (256, 16)]
                        off = kk + (0 if sel is None else sel * 128)
                        if nseq > 1:
                            dims = dims + [(32, nseq)]
                        return raw(R1t, 12288, off, dims)

                    sq_dims = [(0, nseq)] if nseq > 1 else []
                    tS = raw(R2t, 10240, 9216, [(32 * nseq, 2), (nseq, 32)] + ([(1, nseq)] if nseq > 1 else []))
                    uS = raw(R2t, 10240, 9728, [(32 * nseq, 2), (nseq, 32)] + ([(1, nseq)] if nseq > 1 else []))
                    uS_re = raw(R2t, 10240, 9728, [(32 * nseq, 2), (2 * nseq, 16)] + ([(1, nseq)] if nseq > 1 else []))
                    uS_im = raw(R2t, 10240, 9728 + nseq, [(32 * nseq, 2), (2 * nseq, 16)] + ([(1, nseq)] if nseq > 1 else []))

                    def scan_step(prv_all, prv_re, prv_im, cur):
                        tt("dve", tS, A8R, prv_all, ALU.mult, [r_S, r_s5c, r_h0], [r_T1])
                        tt("dve", uS_re, A8IN, prv_im, ALU.mult, [r_S, r_s5c, r_h0], [r_T2])
                        tt("dve", uS_im, A8I, prv_re, ALU.mult, [r_S, r_s5c, r_h0], [r_T2])
                        tt("dve", tS, tS, uS, ALU.add, [r_T1, r_T2], [r_T1])
                        tt("dve", cur, cur, tS, ALU.add, [r_T1, r_S], [r_S])

                    cof = raw(EBt, 1920, 192 + 12 * 64, [(32, 2), (2, 16), (1, 2)])
                    P.op("dve", lambda cof=cof, gb=gb: nc.vector.tensor_copy(
                        out=cof, in_=raw(S5t, 2304, 8 * 128 + gb * 16, [(32, 2), (1, 16), (0, 2)])), [r_s5c], [r_s5c])
                    coi = raw(EBt, 1920, 192 + 13 * 64, [(16, 2), (1, 16)])
                    P.op("dve", lambda coi=coi, gb=gb: nc.vector.tensor_copy(
                        out=coi, in_=raw(S5t, 2304, 8 * 128 + 64 + gb * 16, [(32, 2), (1, 16)])), [r_s5c], [r_s5c])
                    con = raw(EBt, 1920, 192 + 14 * 64, [(16, 2), (1, 16)])
                    P.op("dve", lambda con=con, coi=coi: nc.vector.tensor_scalar(
                        out=con, in0=coi, scalar1=-1.0, scalar2=None, op0=ALU.mult), [r_s5c], [r_s5c])
                    A8R = raw(EBt, 1920, 192 + 12 * 64, [(32, 2), (1, 32)] + sq_dims)
                    A8I = raw(EBt, 1920, 192 + 13 * 64, [(16, 2), (1, 16)] + sq_dims)
                    A8IN = raw(EBt, 1920, 192 + 14 * 64, [(16, 2), (1, 16)] + sq_dims)
                    if sbk == 1:
                        h_all = raw(EBt, 1920, 1728, [(32, 2), (1, 32)])
                        h_re = raw(EBt, 1920, 1728, [(32, 2), (2, 16)])
                        h_im = raw(EBt, 1920, 1729, [(32, 2), (2, 16)])
                        scan_step(h_all, h_re, h_im, sview(0, None))
                    for kk in range(1, kl):
                        scan_step(sview(kk - 1, None), sview(kk - 1, 0), sview(kk - 1, 1), sview(kk, None))
                    if sbk == 0:
                        for seq in range(4):
                            cp("act", FIN.rearrange("p (s r d g) -> p s r d g", s=4, r=2, d=2)[:, seq, :, :, :],
                               raw(R1t, 12288, seq * 32 + 31, [(128, 2), (4096 - 31, 2), (256, 16)]), [r_S], [r_T2])
                        for seq in range(4):
                            for r in range(2):
                                b = nb()
                                tr(banks[b][0:32, 0:128], FIN.rearrange("p (s r f) -> p s r f", s=4, r=2)[:, seq, r, :], ident[:],
                                   [r_T2, r_const], [r_bank[b]])
                                o_ = s5st["w"] % 2
                                s5st["w"] += 1
                                cp("act", stst[o_][0:32, :], banks[b][0:32, 0:128], [r_bank[b]], [r_stst[o_]])
                                for d in range(2):
                                    tk = P.dma("sp", st_out[seq, j, r, d, gb * 16:(gb + 1) * 16, :],
                                               stst[o_][d * 16:(d + 1) * 16, :], reads=[r_stst[o_]], is_output=True)
                                    P.store_keys.add(tk[0])
                    for quad in range(8):
                        gq0 = gb * 32 + quad * 4
                        u_ = s5st["u"] % 2
                        s5st["u"] += 1
                        bu = nb()
                        pu16 = banks[bu][:, :].bitcast(BF16)
                        for g4 in range(4):
                            g = gq0 + g4
                            tr(pu16[:, g4 * 128:(g4 + 1) * 128], TM8[:, g * 128:(g + 1) * 128],
                               identb[:], [r_tm8, r_const], [r_bank[bu]])
                        cp("act", Ust[u_][:, :], pu16[:, 0:512], [r_bank[bu]], [r_ust[u_]])
                        xq_t = T1 if quad % 2 == 0 else T2
                        r_xq = r_T1 if quad % 2 == 0 else r_T2
                        Xq = xq_t[:, :].bitcast(BF16).rearrange("p (d f k) -> p d f k", d=2, f=4)
                        fr0 = quad * 4
                        if sbk == 1:
                            cp("act", Xq[:, 0, :, 1:128], raw(R1t, 12288, fr0 * 128, [(128, 4), (1, 127)]), [r_S], [r_xq])
                            cp("act", Xq[:, 1, :, 0:127], raw(R1t, 12288, 4096 + fr0 * 128 + 1, [(128, 4), (1, 127)]), [r_S], [r_xq])
                            cp("pool", Xq[:, 0, :, 0:1], raw(EBt, 1920, 1728 + fr0, [(1, 4), (1, 1)]), [r_h0], [r_xq])
                            cp("pool", Xq[:, 1, :, 127:128], raw(EBt, 1920, 1728 + 32 + fr0, [(1, 4), (1, 1)]), [r_h0], [r_xq])
                        else:
                            xv = Xq.rearrange("p d f (s k) -> p d f s k", s=4)
                            cp("act", xv[:, 0, :, :, 1:32], raw(R1t, 12288, fr0 * 128, [(128, 4), (32, 4), (1, 31)]), [r_S], [r_xq])
                            cp("act", xv[:, 1, :, :, 0:31], raw(R1t, 12288, 4096 + fr0 * 128 + 1, [(128, 4), (32, 4), (1, 31)]), [r_S], [r_xq])
                            P.op("pool", lambda xv=xv: nc.gpsimd.memset(xv[:, 0, :, :, 0:1], 0.0), [], [r_xq])
                            P.op("pool", lambda xv=xv: nc.gpsimd.memset(xv[:, 1, :, :, 31:32], 0.0), [], [r_xq])
                        b = nb()
                        for g4 in range(4):
                            g = gq0 + g4
                            glb = quad * 4 + g4
                            gpl, g2 = glb // 2, glb % 2
                            mm(banks[b][:, g4 * 128:(g4 + 1) * 128], Ust[u_][:, g4 * 128:(g4 + 1) * 128], Msb[:, glb, :],
                               True, False, [r_ust[u_], r_M], [r_bank[b]])
                            for d in range(2):
                                for r in range(2):
                                    mm(banks[b][:, g4 * 128:(g4 + 1) * 128], Xq[g2 * 64:(g2 + 1) * 64, d, (g4 // 2) * 2 + r, :],
                                       W2[g2 * 64:(g2 + 1) * 64, d, gpl, r, :], False, d == 1 and r == 1,
                                       [r_xq, r_W2], [r_bank[b]])
                        t_ = tmp[quad % 2]
                        rt_ = r_tmp[quad % 2]
                        act(t_[:], banks[b][:, :], AF.Square, [r_bank[b]], [rt_])
                        ts("dve", t_[:], t_[:], 0.044715, 1.0, ALU.mult, ALU.add, [rt_], [rt_])
                        tt("dve", t_[:], t_[:], banks[b][:, :], ALU.mult, [rt_, r_bank[b]], [rt_])
                        act(t_[:], t_[:], AF.Sigmoid, [rt_], [rt_], scale=1.5957691216057308)
                        tt("dve", raw(hTt, 8192, gq0 * 16, [(16, 4), (1024, 8), (1, 16)]),
                           t_[:].rearrange("p (g t c) -> p g t c", g=4, t=8), banks[b][:, :].rearrange("p (g t c) -> p g t c", g=4, t=8),
                           ALU.mult, [rt_, r_bank[b]], [r_hT[0], r_hT[1]])
                    ada_step()
                P.barrier()
                Yv = hT[:, :, :].rearrange("p c n -> p (c n)")
                zT = TM8.rearrange("p (c n) -> p c n", c=NCH)
                for dc in range(NCH):
                    b = nb()
                    pb16 = banks[b][:, :].bitcast(BF16)
                    for t8 in range(8):
                        tr(pb16[:, t8 * 128:(t8 + 1) * 128], Yv[:, t8 * 1024 + dc * 128:t8 * 1024 + (dc + 1) * 128], identb[:],
                           [r_hT[0], r_hT[1], r_const], [r_bank[b]])
                    cp("act", raw(TM8t, 24576, TM8.offset + dc * 1024, [(1, 8), (8, 128)]), pb16.rearrange("p (t k) -> p t k", t=8),
                       [r_bank[b]], [r_tm8])
                r_wg = [Res("wg0"), Res("wg1")]
                wgb = [[R2[:, (s_ * 2 + w_) * 1024:(s_ * 2 + w_ + 1) * 1024].bitcast(BF16).rearrange("p (k n) -> p k n", k=NCH)
                        for w_ in range(2)] for s_ in range(2)]
                for cb in range(4):
                    s_ = cb % 2
                    P.dma("pool", wgb[s_][0], glu_w[j, :, cb * 256:(cb + 1) * 256].rearrange("(k p) n -> p k n", p=128), writes=[r_wg[s_]])
                    P.dma("pool", wgb[s_][1], glu_w[j, :, D + cb * 256:D + (cb + 1) * 256].rearrange("(k p) n -> p k n", p=128), writes=[r_wg[s_]])
                    for i2 in range(2):
                        dc = cb * 2 + i2
                        for tb in range(2):
                            t0 = sbk * 1024 + tb * 512
                            bv_, bg_ = nb(), nb()
                            for kc in range(NCH):
                                mm(banks[bv_][:, :], wgb[s_][0][:, kc, i2 * 128:(i2 + 1) * 128], zT[:, kc, tb * 512:(tb + 1) * 512],
                                   kc == 0, kc == NCH - 1, [r_wg[s_], r_tm8], [r_bank[bv_]])
                            for kc in range(NCH):
                                mm(banks[bg_][:, :], wgb[s_][1][:, kc, i2 * 128:(i2 + 1) * 128], zT[:, kc, tb * 512:(tb + 1) * 512],
                                   kc == 0, kc == NCH - 1, [r_wg[s_], r_tm8], [r_bank[bg_]])
                            ti = (dc * 2 + tb) % 2
                            act(tmp[ti][:], banks[bg_][:, :], AF.Sigmoid, [r_bank[bg_]], [r_tmp[ti]])
                            tt("dve", tmp[ti][:], tmp[ti][:], banks[bv_][:, :], ALU.mult, [r_tmp[ti], r_bank[bv_]], [r_tmp[ti]])
                            stt(xT[:, dc, t0:t0 + 512], tmp[ti][:], mod[:, i, 16 + dc, jc:jc + 1], xT[:, dc, t0:t0 + 512],
                                ALU.mult, ALU.add, [r_tmp[ti], r_mod[i]] + xres(t0, 512), xres(t0, 512))
                ada_step()
            P.barrier()

        stage = [R1[:, s * 1024:(s + 1) * 1024] for s in range(2)]
        r_stage = [Res("stage0"), Res("stage1")]
        for t in range(16):
            s = t % 2
            P.dma("sp", stage[s], x_tok[t * 128:(t + 1) * 128, :], writes=[r_stage[s]])
            for half in range(2):
                b = nb()
                for j in range(4):
                    c = half * 4 + j
                    tr(banks[b][:, j * 128:(j + 1) * 128], stage[s][:, c * 128:(c + 1) * 128], ident[:],
                       [r_stage[s], r_const], [r_bank[b]])
                cp("act", xT[:, half * 4:half * 4 + 4, t * 128:(t + 1) * 128],
                   banks[b][:, :].rearrange("p (c n) -> p c n", c=4), [r_bank[b]], [r_xT[t]])
        wbig = [R2[:, s_ * 4096:(s_ + 1) * 4096].bitcast(BF16).rearrange("p (k n) -> p k n", k=NCH) for s_ in range(2)]
        r_wbig = [Res("wbig0"), Res("wbig1")]
        mps0 = banks[ADA_BANK][:, 0:96].rearrange("p (a b) -> p a b", b=2)
        for cbig in range(6):
            s_ = cbig % 2
            P.dma("pool", wbig[s_], ada_w[0, :, cbig * 1024:(cbig + 1) * 1024].rearrange("(k p) n -> p k n", p=128),
                  writes=[r_wbig[s_]])
            for j8 in range(8):
                jj = cbig * 8 + j8
                for kc in range(NCH):
                    mm(mps0[:, jj, :], wbig[s_][:, kc, j8 * 128:(j8 + 1) * 128], scond[:, kc, :],
                       kc == 0, kc == NCH - 1, [r_wbig[s_], r_const], [r_bank[ADA_BANK]])
        ada_fin(0)
        P.barrier()

        for i in range(DEPTH):
            if i + 1 < DEPTH:
                ada_layer(i + 1)
            if "na" in stages and i % 2 == 0:
                na_layer(i)
            if "s5" in stages and i % 2 == 1:
                s5_layer(i)
            if "ffn" in stages:
                for sbk in range(2):
                    if sbk == 0:
                        norm_mod(i, 1, 0)
                        ffn(i, 0, between=lambda i=i: norm_mod(i, 1, 1))
                    else:
                        ffn(i, 1)
            ada_flush()

        P.barrier()
        r_stage = [Res("ostage0"), Res("ostage1")]
        for t in range(16):
            s = t % 2
            for half in range(2):
                b = nb()
                for j in range(4):
                    c = half * 4 + j
                    tr(banks[b][:, j * 128:(j + 1) * 128], xT[:, c, t * 128:(t + 1) * 128], ident[:],
                       [r_xT[t], r_const], [r_bank[b]])
                cp("dve", stage[s][:, half * 512:(half + 1) * 512], banks[b][:, :], [r_bank[b]], [r_stage[s]])
            tk = P.dma("sp", y_tok[t * 128:(t + 1) * 128, :], stage[s], reads=[r_stage[s]], is_output=True)
            P.store_keys.add(tk[0])

        P.finish()
        with nc.Block() as block:
            P.emit(block)
    return nc


_NC_CACHE = {}


def _build_ebias(rpb):
    L, H = rpb.shape[0], rpb.shape[1]
    pad = np.concatenate([rpb.reshape(L, H, -1), np.full((L, H, 1), -30000.0, np.float32)], -1)
    u = np.arange(2)[:, None, None, None]
    kc = np.arange(64)[None, :, None, None]
    d = np.arange(15)[None, None, :, None]
    qc = np.arange(64)[None, None, None, :]
    dr = d + u - 7
    cs = np.clip(qc - 8, 0, 48)
    ok = (kc >= cs) & (kc < cs + 16) & (dr <= 7)
    dc = np.clip(kc - qc, -15, 15)
    idx = np.where(ok, (np.clip(dr, -7, 7) + 7) * 31 + dc + 15, 15 * 31)
    idx = np.broadcast_to(idx, (2, 64, 15, 64)).reshape(128, 960)
    return np.ascontiguousarray(pad[:, :, idx])


def _qf(a):
    lead = a.shape[:-3]
    a = a.reshape(lead + (2, 32, 2, 64))
    nl = len(lead)
    a = np.moveaxis(a, (nl + 2, nl + 3, nl + 0, nl + 1), (nl + 0, nl + 1, nl + 2, nl + 3))
    return a.reshape(lead + (128, 64))


def _build_s5(lre, lim, lst, bre, bim, cre, cim, dsk):
    L = lre.shape[0]
    lam = np.stack([_qf(lre), _qf(lim), _qf(np.broadcast_to(lst[..., None], lre.shape))], 2)
    def qfc(a):
        a = a.reshape(L, 2, 32, 2, 64, 16).transpose(0, 3, 4, 1, 2, 5)
        return a.reshape(L, 128, 64, 16)
    b = np.stack([qfc(bre), qfc(bim)], 2)
    ct = lambda a: np.swapaxes(a, -1, -2)
    c = np.stack([qfc(ct(cre)), qfc(ct(cim))], 2)
    dvec = np.ascontiguousarray(np.tile(dsk.reshape(L, 64, 1, 16), (1, 1, 8, 1)).reshape(L, 64, 128).transpose(2, 0, 1))
    s_ = np.arange(128)[:, None] // 16
    t_ = np.arange(128)[None, :] // 16
    masks = np.concatenate([(t_ >= s_), (t_ <= s_)], 1).astype(np.float32)
    return {"lam": np.ascontiguousarray(lam), "b": np.ascontiguousarray(b), "c": np.ascontiguousarray(c),
            "dvec": dvec, "masks": masks}


def _build_h0(hre, him):
    a = np.stack([hre, him], -1)
    L = a.shape[0]
    a = a.reshape(L, 2, 2, 16, 2, 64, 2)
    a = a.transpose(0, 4, 5, 2, 1, 3, 6)
    return np.ascontiguousarray(a.reshape(L, 128, 2, 64))


def _fm(v):
    return np.ascontiguousarray(v.reshape(NCH, 128).T)


def kernel(x_prompt, x_sample, cache_k, cache_v, state_ssm_re, state_ssm_im, c, c_ctx,
           norm_mix, norm_ffn, ada_w, ada_b, na_w_qkv, na_w_o, na_q_gain, na_k_gain, na_rpb,
           ssm_lambda_re, ssm_lambda_im, ssm_log_step, ssm_b_re, ssm_b_im, ssm_c_re, ssm_c_im,
           ssm_d, ssm_w_glu, ffn_w1, ffn_w3, ffn_w2):
    n = 8
    f = lambda a: np.ascontiguousarray(np.asarray(a, np.float32))
    x_prompt, x_sample, c, c_ctx = f(x_prompt), f(x_sample), f(c), f(c_ctx)
    norm_mix, norm_ffn, ada_w, ada_b = f(norm_mix), f(norm_ffn), f(ada_w), f(ada_b)
    ffn_w1, ffn_w3, ffn_w2 = f(ffn_w1), f(ffn_w3), f(ffn_w2)
    na_w_qkv, na_w_o, na_q_gain, na_k_gain, na_rpb = f(na_w_qkv), f(na_w_o), f(na_q_gain), f(na_k_gain), f(na_rpb)
    cache_k, cache_v = f(cache_k), f(cache_v)
    s5in = _build_s5(f(ssm_lambda_re), f(ssm_lambda_im), f(ssm_log_step), f(ssm_b_re), f(ssm_b_im), f(ssm_c_re), f(ssm_c_im), f(ssm_d))
    state_ssm_re, state_ssm_im, ssm_w_glu = f(state_ssm_re), f(state_ssm_im), f(ssm_w_glu)
    qkg = np.ascontiguousarray(np.stack([np.tile(na_q_gain, (1, 2)).T, np.tile(na_k_gain, (1, 2)).T], -1))
    bones = np.kron(np.eye(2, dtype=np.float32), np.ones((64, 64), np.float32))
    ebias = _build_ebias(na_rpb)
    if "nc" not in _NC_CACHE:
        _NC_CACHE["nc"] = build_program()
    nc = _NC_CACHE["nc"]
    ident = np.eye(128, dtype=np.float32)
    gmix = np.ascontiguousarray(norm_mix.reshape(DEPTH, NCH, 128).transpose(2, 0, 1))
    gffn = np.ascontiguousarray(norm_ffn.reshape(DEPTH, NCH, 128).transpose(2, 0, 1))
    adab = np.ascontiguousarray(ada_b.reshape(DEPTH, 48, 128).transpose(2, 0, 1))
    in_maps = []
    for core in range(n):
        xp = x_prompt[4 * core:4 * core + 4].reshape(NP_TOK, D)
        xs = x_sample[core % 2]
        cond = np.ascontiguousarray(np.stack([_fm(c_ctx), _fm(c[core % 2])], -1))
        in_maps.append({"x_tok": np.ascontiguousarray(np.concatenate([xp, xs], 0)), "ident": ident, "cond": cond,
                        "gmix": gmix, "gffn": gffn, "adab": adab, "ada_w": ada_w,
                        "ffn_w1": ffn_w1, "ffn_w3": ffn_w3, "ffn_w2": ffn_w2,
                        "na_w_qkv": na_w_qkv, "na_w_o": na_w_o, "qkg": qkg, "bones": bones, "ebias": ebias,
                        "s5lam": s5in["lam"], "s5b": s5in["b"], "s5c": s5in["c"], "dvec": s5in["dvec"], "masks": s5in["masks"],
                        "s5h0": _build_h0(state_ssm_re[core % 2], state_ssm_im[core % 2]), "glu_w": ssm_w_glu,
                        "ck": np.ascontiguousarray(cache_k[core % 2].reshape(2, 512, D)),
                        "cv": np.ascontiguousarray(cache_v[core % 2].reshape(2, 512, D))})
    res = run_bass_kernel_spmd(nc, in_maps, core_ids=list(range(n)))
    outs = res.results
    y_prompt = np.concatenate([outs[c_]["y_tok"][:NP_TOK].reshape(4, 256, D) for c_ in range(n)], 0)
    y_sample = np.stack([outs[0]["y_tok"][NP_TOK:], outs[1]["y_tok"][NP_TOK:]], 0)
    new_k = np.concatenate([outs[c_]["k_out"] for c_ in range(n)], 0).reshape(32, 2, 256, 16, 64)
    new_v = np.concatenate([outs[c_]["v_out"] for c_ in range(n)], 0).reshape(32, 2, 256, 16, 64)
    st = np.concatenate([outs[c_]["st_out"] for c_ in range(n)], 0).reshape(32, 2, 2, 2, 64, 64)
    st_re = np.ascontiguousarray(st[:, :, 0])
    st_im = np.ascontiguousarray(st[:, :, 1])
    return (y_prompt, y_sample, new_k, new_v, st_re, st_im)
```

```python
import contextlib
import math
import numpy as np
import concourse.bass as bass
import concourse.mybir as mybir
from concourse.bass_utils import run_bass_kernel_spmd

F32 = mybir.dt.float32
BF16 = mybir.dt.bfloat16
AF = mybir.ActivationFunctionType
ALU = mybir.AluOpType

D = 1024
NCH = 8
NT = 2048
NP_TOK = 1024
DFF = 2816
NFF = 22
DEPTH = 4
EPS = 1e-6
NDMA_SEMS = 40


class Res:
    __slots__ = ("name", "w", "r")

    def __init__(self, name):
        self.name = name
        self.w = None
        self.r = {}


class Prog:
    def __init__(self, nc):
        self.nc = nc
        self.eng = {"pe": nc.tensor, "act": nc.scalar, "dve": nc.vector, "pool": nc.gpsimd, "sp": nc.sync}
        self.sem = {e: nc.alloc_semaphore(name=f"sem_{e}") for e in self.eng}
        self.cnt = {e: 0 for e in self.eng}
        self.seen = {e: {} for e in self.eng}
        self.ops = {e: [] for e in self.eng}
        self.dsem = [nc.alloc_semaphore(name=f"sem_dma{i}") for i in range(NDMA_SEMS)]
        self.dval = [0] * NDMA_SEMS
        self.drr = 0
        self.drr_sw = 0
        self.out_tickets = []

    def _sem_of(self, key):
        return self.sem[key] if isinstance(key, str) else self.dsem[key[1]]

    def _deps(self, eng, reads, writes, extra=()):
        need = {}

        def add(t):
            if t is None:
                return
            k, v = t
            if need.get(k, 0) < v:
                need[k] = v

        for r in reads:
            add(r.w)
        for w in writes:
            add(w.w)
            for k, v in w.r.items():
                add((k, v))
        for t in extra:
            add(t)
        waits = []
        for k, v in need.items():
            if eng == "pe" and k == "pe":
                continue
            if self.seen[eng].get(k, 0) < v:
                self.seen[eng][k] = v
                waits.append((k, v))
        return waits

    def _commit(self, tk, reads, writes):
        for r in reads:
            k, v = tk
            if r.r.get(k, 0) < v:
                r.r[k] = v
        for w in writes:
            w.w = tk
            w.r = {}

    def op(self, eng, fn, reads=(), writes=()):
        waits = self._deps(eng, reads, writes)
        self.cnt[eng] += 1
        tk = (eng, self.cnt[eng])
        self.ops[eng].append((waits, fn, self.sem[eng], 1))
        self._commit(tk, reads, writes)
        return tk

    def dma(self, q, out, in_, reads=(), writes=(), is_output=False):
        half = NDMA_SEMS // 2
        if q == "pool":
            i = self.drr_sw
            self.drr_sw = (self.drr_sw + 1) % half
        else:
            i = half + self.drr
            self.drr = (self.drr + 1) % half
        prev = self.dval[i]
        self.dval[i] = prev + 16
        key = ("dma", i)
        extra = [(key, prev)] if prev > 0 else []
        waits = self._deps(q, reads, writes, extra)
        e = self.eng[q]
        fn = lambda: e.dma_start(out=out, in_=in_)
        self.ops[q].append((waits, fn, self.dsem[i], 16))
        tk = (key, prev + 16)
        self._commit(tk, reads, writes)
        if is_output:
            self.out_tickets.append(tk)
        return tk

    def finish(self):
        need = {}
        for k, v in self.out_tickets:
            need[k] = max(need.get(k, 0), v)
        for i in range(NDMA_SEMS):
            if self.dval[i] > 0:
                need[("dma", i)] = self.dval[i]
        for e in self.eng:
            if e != "sp" and self.cnt[e] > 0:
                need[e] = self.cnt[e]
        waits = list(need.items())
        self.ops["sp"].append((waits, None, None, 0))

    def emit(self, block):
        def make(ename):
            def body(_e):
                e = self.eng[ename]
                for waits, fn, sem, inc in self.ops[ename]:
                    for k, v in waits:
                        e.wait_ge(self._sem_of(k), v)
                    if fn is not None:
                        ins = fn()
                        ins.then_inc(sem, inc)
            return body

        block.tensor(make("pe"))
        block.scalar(make("act"))
        block.vector(make("dve"))
        block.gpsimd(make("pool"))
        block.sync(make("sp"))


def _barrier(self):
    comp = [e for e in self.eng if e != "sp"]
    for e in self.eng:
        waits = []
        for o in comp:
            if o == e or self.cnt[o] == 0:
                continue
            if self.seen[e].get(o, 0) < self.cnt[o]:
                self.seen[e][o] = self.cnt[o]
                waits.append((o, self.cnt[o]))
        for i in range(NDMA_SEMS):
            k = ("dma", i)
            if self.dval[i] > 0 and self.seen[e].get(k, 0) < self.dval[i] and k in self.store_keys:
                self.seen[e][k] = self.dval[i]
                waits.append((k, self.dval[i]))
        if waits:
            self.ops[e].append((waits, None, None, 0))


Prog.barrier = _barrier


def build_program(stages=None):
    stages = ("na", "s5", "ffn") if stages is None else stages
    nc = bass.Bass("TRN2", target_bir_lowering=False)

    def din(name, shape, dt=F32):
        return nc.dram_tensor(name, list(shape), dt, kind="ExternalInput").ap()

    def dout(name, shape, dt=F32):
        return nc.dram_tensor(name, list(shape), dt, kind="ExternalOutput").ap()

    x_tok = din("x_tok", [NT, D])
    ident_in = din("ident", [128, 128])
    cond_in = din("cond", [128, NCH, 2])
    gmix_in = din("gmix", [128, DEPTH, NCH])
    gffn_in = din("gffn", [128, DEPTH, NCH])
    adab_in = din("adab", [128, DEPTH, 48])
    ada_w = din("ada_w", [DEPTH, D, 6 * D])
    ffn_w1 = din("ffn_w1", [DEPTH, D, DFF])
    ffn_w3 = din("ffn_w3", [DEPTH, D, DFF])
    ffn_w2 = din("ffn_w2", [DEPTH, DFF, D])
    na_w_qkv = din("na_w_qkv", [2, D, 3 * D])
    na_w_o = din("na_w_o", [2, D, D])
    qkg_in = din("qkg", [128, 2, 2])
    bones_in = din("bones", [128, 128])
    ck_in = din("ck", [2, 512, D])
    cv_in = din("cv", [2, 512, D])
    ebias_in = din("ebias", [2, 16, 128, 960])
    s5lam_in = din("s5lam", [2, 128, 3, 64])
    s5b_in = din("s5b", [2, 128, 2, 64, 16])
    s5c_in = din("s5c", [2, 128, 2, 64, 16])
    s5h0_in = din("s5h0", [2, 128, 2, 64])
    dvec_in = din("dvec", [128, 2, 64])
    masks_in = din("masks", [128, 256])
    glu_w = din("glu_w", [2, D, 2 * D])
    cW1 = nc.dram_tensor("cache_w1t", [2, 4, 128, 2048], BF16, kind="Internal").ap()
    cW2 = nc.dram_tensor("cache_w2", [2, 128, 8192], BF16, kind="Internal").ap()
    cM = nc.dram_tensor("cache_m", [2, 128, 4096], BF16, kind="Internal").ap()
    y_tok = dout("y_tok", [NT, D])
    st_out = dout("st_out", [4, 2, 2, 2, 32, 128])
    k_out = dout("k_out", [4, 2, 256, D])
    v_out = dout("v_out", [4, 2, 256, D])

    P = Prog(nc)
    P.store_keys = set()
    with contextlib.ExitStack() as es:
        def sb(name, shape, dt=F32):
            return es.enter_context(nc.sbuf_tensor(name, list(shape), dt))

        def ps(name, shape, dt=F32):
            return es.enter_context(nc.psum_tensor(name, list(shape), dt))

        xT = sb("xT", [128, NCH, NT])
        hT = sb("hT", [128, NCH, 1024], BF16)
        R1 = sb("R1", [128, 12288])
        R2 = sb("R2", [128, 10240])
        wada = [sb(f"wada{i}", [128, NCH, 128], BF16) for i in range(2)]
        ident = sb("ident_sb", [128, 128])
        ones = sb("ones_sb", [128, 128])
        cond = sb("cond_sb", [128, NCH, 2])
        scond = sb("scond_sb", [128, NCH, 2], BF16)
        gmix = sb("gmix_sb", [128, DEPTH, NCH])
        gffn = sb("gffn_sb", [128, DEPTH, NCH])
        adab = sb("adab_sb", [128, DEPTH, 48])
        mod = sb("mod_sb", [128, DEPTH, 48, 2])
        Amod = sb("Amod_sb", [128, DEPTH, 2, NCH, 2])
        sq = [sb(f"sq{i}", [128, 512]) for i in range(2)]
        rt = sb("rt_sb", [128, 512])
        rstd = sb("rstd_sb", [128, 512])
        tmp = [sb(f"tmp{i}", [128, 512]) for i in range(2)]
        EB = sb("EB_sb", [128, 2, 960])
        qkg = sb("qkg_sb", [128, 2, 2])
        bones = sb("bones_sb", [128, 128])
        ones_bf = sb("ones_bf_sb", [128, 64], BF16)
        identb = sb("identb_sb", [128, 128], BF16)
        epsc = sb("epsc_sb", [128, 1])
        masks = sb("masks_sb", [128, 256])
        dvec = sb("dvec_sb", [128, 2, 64])
        banks = [ps(f"bank{i}", [128, 512]) for i in range(8)]

        r_const = Res("const")
        r_bank = [Res(f"bank{i}") for i in range(8)]
        r_xT = [Res(f"xT{t}") for t in range(16)]
        r_hT = [Res(f"hT{t}") for t in range(2)]
        r_sq = [Res("sq0"), Res("sq1")]
        r_rt = Res("rt")
        r_rstd = Res("rstd")
        r_tmp = [Res("tmp0"), Res("tmp1")]
        r_wada = [Res("wada0"), Res("wada1")]
        r_mod = [Res(f"mod{i}") for i in range(DEPTH)]
        STAT_BANK, ADA_BANK = 6, 7
        state = {"bi": 0}

        def nb():
            b = state["bi"] % 6
            state["bi"] += 1
            return b

        def xres(t0, n):
            return r_xT[t0 // 128:(t0 + n + 127) // 128]

        def mm(out, lhsT, rhs, start, stop, reads, writes):
            P.op("pe", lambda: nc.tensor.matmul(out, lhsT, rhs, start=start, stop=stop), reads, writes)

        def tr(out, in_, idt, reads, writes):
            P.op("pe", lambda: nc.tensor.transpose(out=out, in_=in_, identity=idt), reads, writes)

        def act(out, in_, func, reads, writes, bias=None, scale=None):
            kw = {}
            if bias is not None:
                kw["bias"] = bias
            if scale is not None:
                kw["scale"] = scale
            P.op("act", lambda: nc.scalar.activation(out=out, in_=in_, func=func, **kw), reads, writes)

        def tt(eng, out, in0, in1, op, reads, writes):
            e = nc.vector if eng == "dve" else nc.gpsimd
            P.op(eng, lambda: e.tensor_tensor(out=out, in0=in0, in1=in1, op=op), reads, writes)

        def ts(eng, out, in0, s1, s2, op0, op1, reads, writes):
            e = nc.vector if eng == "dve" else nc.gpsimd
            if op1 is None:
                P.op(eng, lambda: e.tensor_scalar(out=out, in0=in0, scalar1=s1, scalar2=None, op0=op0), reads, writes)
            else:
                P.op(eng, lambda: e.tensor_scalar(out=out, in0=in0, scalar1=s1, scalar2=s2, op0=op0, op1=op1), reads, writes)

        def stt(out, in0, scalar, in1, op0, op1, reads, writes):
            P.op("dve", lambda: nc.vector.scalar_tensor_tensor(out=out, in0=in0, scalar=scalar, in1=in1, op0=op0, op1=op1),
                 reads, writes)

        def cp(eng, out, in_, reads, writes):
            if eng == "act":
                P.op("act", lambda: nc.scalar.copy(out=out, in_=in_), reads, writes)
            elif eng == "dve":
                P.op("dve", lambda: nc.vector.tensor_copy(out=out, in_=in_), reads, writes)
            else:
                P.op("pool", lambda: nc.gpsimd.tensor_copy(out=out, in_=in_), reads, writes)

        P.dma("sp", ident[:], ident_in[:, :], writes=[r_const])
        P.dma("sp", cond[:], cond_in[:, :, :], writes=[r_const])
        P.dma("sp", gmix[:], gmix_in[:, :, :], writes=[r_const])
        P.dma("sp", gffn[:], gffn_in[:, :, :], writes=[r_const])
        P.dma("sp", adab[:], adab_in[:, :, :], writes=[r_const])
        P.op("pool", lambda: nc.gpsimd.memset(ones[:], 1.0), writes=[r_const])
        P.op("pool", lambda: nc.gpsimd.memset(ones_bf[:], 1.0), writes=[r_const])
        P.op("pool", lambda: nc.gpsimd.memset(epsc[:], EPS), writes=[r_const])
        P.dma("sp", qkg[:], qkg_in[:, :, :], writes=[r_const])
        P.dma("sp", bones[:], bones_in[:, :], writes=[r_const])
        act(scond[:], cond[:], AF.Silu, [r_const], [r_const])
        P.dma("sp", masks[:], masks_in[:, :], writes=[r_const])
        P.dma("sp", dvec[:], dvec_in[:, :, :], writes=[r_const])
        cp("dve", identb[:], ident[:], [r_const], [r_const])
        stst = [rt[:, 0:128], rstd[:, 0:128]]
        r_stst = [r_rt, r_rstd]

        pending_ada = []

        def ada_step():
            for _ in range(2):
                if pending_ada:
                    pending_ada.pop(0)()

        def ada_flush():
            while pending_ada:
                pending_ada.pop(0)()

        def ada_layer(i):
            for cb in range(48):
                pending_ada.append(lambda cb=cb: ada_piece(i, cb))
            pending_ada.append(lambda: ada_fin(i))

        def ada_piece(i, cb):
            mps = banks[ADA_BANK][:, 0:96].rearrange("p (a b) -> p a b", b=2)
            if True:
                s = cb % 2
                P.dma("pool", wada[s][:], ada_w[i, :, cb * 128:(cb + 1) * 128].rearrange("(k p) n -> p k n", p=128),
                      writes=[r_wada[s]])
                for j2 in range(1):
                    j = cb
                    for kc in range(NCH):
                        mm(mps[:, j, :], wada[s][:, kc, :], scond[:, kc, :],
                           kc == 0, kc == NCH - 1, [r_wada[s], r_const], [r_bank[ADA_BANK]])
        def ada_fin(i):
            mps = banks[ADA_BANK][:, 0:96].rearrange("p (a b) -> p a b", b=2)
            for jc in range(2):
                tt("dve", mod[:, i, :, jc], mps[:, :, jc], adab[:, i, :], ALU.add, [r_bank[ADA_BANK], r_const], [r_mod[i]])
            for jc in range(2):
                stt(Amod[:, i, 0, :, jc], mod[:, i, 8:16, jc], 1.0, gmix[:, i, :], ALU.add, ALU.mult, [r_mod[i], r_const], [r_mod[i]])
                stt(Amod[:, i, 1, :, jc], mod[:, i, 32:40, jc], 1.0, gffn[:, i, :], ALU.add, ALU.mult, [r_mod[i], r_const], [r_mod[i]])

        def norm_mod(i, which, sbk):
            jc = sbk
            sh0 = 0 if which == 0 else 24
            for tb in range(2):
                t0 = sbk * 1024 + tb * 512
                xr = xres(t0, 512)
                for c in range(NCH):
                    s = c % 2
                    act(sq[s][:], xT[:, c, t0:t0 + 512], AF.Square, xr, [r_sq[s]])
                    mm(banks[STAT_BANK][:, :], ones[:], sq[s][:], c == 0, c == NCH - 1, [r_sq[s], r_const], [r_bank[STAT_BANK]])
                act(rt[:], banks[STAT_BANK][:, :], AF.Ln, [r_bank[STAT_BANK], r_const], [r_rt], scale=1.0 / D, bias=epsc[:, 0:1])
                act(rstd[:], rt[:], AF.Exp, [r_rt], [r_rstd], scale=-0.5)
                for c in range(NCH):
                    s = c % 2
                    stt(tmp[s][:], xT[:, c, t0:t0 + 512], Amod[:, i, which, c, jc:jc + 1], rstd[:], ALU.mult, ALU.mult,
                        xr + [r_rstd, r_mod[i]], [r_tmp[s]])
                    act(hT[:, c, tb * 512:(tb + 1) * 512], tmp[s][:], AF.Identity, [r_tmp[s], r_mod[i]], [r_hT[tb]],
                        bias=mod[:, i, sh0 + c, jc:jc + 1])

        gT = R1[:, 0:11264].bitcast(BF16).rearrange("p (j n) -> p j n", j=NFF)
        w13 = [[R2[:, (s * 2 + w) * 1024:(s * 2 + w + 1) * 1024].bitcast(BF16).rearrange("p (k n) -> p k n", k=NCH)
                for w in range(2)] for s in range(2)]
        w2b = [R2[:, 4096 + s * 2816:4096 + (s + 1) * 2816].bitcast(BF16).rearrange("p (j n) -> p j n", j=NFF)
               for s in range(2)]
        r_gT = [Res("gT0"), Res("gT1")]
        r_w13 = [Res("w13_0"), Res("w13_1")]
        r_w2 = [Res("w2_0"), Res("w2_1")]
        wcount = {"a": 0, "b": 0}

        def ffn(i, sbk, between=None):
            jc = sbk
            for wb in range(11):
                s = wcount["a"] % 2
                wcount["a"] += 1
                P.dma("pool", w13[s][0], ffn_w1[i, :, wb * 256:(wb + 1) * 256].rearrange("(k p) n -> p k n", p=128), writes=[r_w13[s]])
                P.dma("pool", w13[s][1], ffn_w3[i, :, wb * 256:(wb + 1) * 256].rearrange("(k p) n -> p k n", p=128), writes=[r_w13[s]])
                for j2 in range(2):
                    j = wb * 2 + j2
                    for tb in range(2):
                        b1, b3 = nb(), nb()
                        for kc in range(NCH):
                            mm(banks[b1][:, :], w13[s][0][:, kc, j2 * 128:(j2 + 1) * 128], hT[:, kc, tb * 512:(tb + 1) * 512],
                               kc == 0, kc == NCH - 1, [r_w13[s], r_hT[tb]], [r_bank[b1]])
                        for kc in range(NCH):
                            mm(banks[b3][:, :], w13[s][1][:, kc, j2 * 128:(j2 + 1) * 128], hT[:, kc, tb * 512:(tb + 1) * 512],
                               kc == 0, kc == NCH - 1, [r_w13[s], r_hT[tb]], [r_bank[b3]])
                        ti = (j * 2 + tb) % 2
                        act(tmp[ti][:], banks[b1][:, :], AF.Silu, [r_bank[b1]], [r_tmp[ti]])
                        tt("dve", gT[:, j, tb * 512:(tb + 1) * 512], tmp[ti][:], banks[b3][:, :], ALU.mult,
                           [r_tmp[ti], r_bank[b3]], [r_gT[tb]])
                ada_step()
            if between is not None:
                between()
            for cb in range(4):
                ada_step()
                s = wcount["b"] % 2
                wcount["b"] += 1
                P.dma("pool", w2b[s], ffn_w2[i, :, cb * 256:(cb + 1) * 256].rearrange("(j p) n -> p j n", p=128), writes=[r_w2[s]])
                for tb in range(2):
                    t0 = sbk * 1024 + tb * 512
                    for i2 in range(2):
                        dc = cb * 2 + i2
                        b = nb()
                        for j in range(NFF):
                            mm(banks[b][:, :], w2b[s][:, j, i2 * 128:(i2 + 1) * 128], gT[:, j, tb * 512:(tb + 1) * 512],
                               j == 0, j == NFF - 1, [r_w2[s], r_gT[tb]], [r_bank[b]])
                        stt(xT[:, dc, t0:t0 + 512], banks[b][:, :], mod[:, i, 40 + dc, jc:jc + 1], xT[:, dc, t0:t0 + 512],
                            ALU.mult, ALU.add, [r_bank[b], r_mod[i]] + xres(t0, 512), xres(t0, 512))


        qT = R1[:, 0:4096].bitcast(BF16).rearrange("p (c n) -> p c n", c=NCH)
        kT = R1[:, 4096:8192].bitcast(BF16).rearrange("p (c n) -> p c n", c=NCH)
        Vt = R1[:, 8192:12288].bitcast(BF16).rearrange("p (t n) -> p t n", t=8)
        Vs = R2[:, 0:3584].bitcast(BF16).rearrange("p (t n) -> p t n", t=7)
        ckst = [R2[:, s_ * 1024:(s_ + 1) * 1024] for s_ in range(2)]
        kstage = [R2[:, s_ * 512:(s_ + 1) * 512].rearrange("p (t n) -> p t n", t=4) for s_ in range(2)]
        vstage = [R2[:, 1024 + s_ * 256:1024 + (s_ + 1) * 256] for s_ in range(2)]
        kctxT = R2[:, 3584:5632].bitcast(BF16).rearrange("p (c n) -> p c n", c=NCH)
        vctx = R2[:, 5632:7680].bitcast(BF16).rearrange("p (t n) -> p t n", t=4)
        wblk = [R2[:, 7680 + s_ * 1024:7680 + (s_ + 1) * 1024].bitcast(BF16).rearrange("p (k n) -> p k n", k=NCH) for s_ in range(2)]
        PTc = [R2[:, 9728 + s_ * 256:9728 + (s_ + 1) * 256].bitcast(BF16) for s_ in range(2)]
        PTl = [sq[s_][:, 0:128].bitcast(BF16) for s_ in range(2)]
        oT = hT
        PTc3 = [PTc[0], PTc[1], tmp[0][:, 256:512].bitcast(BF16)]
        r_pt3 = [Res("ptc0"), Res("ptc1"), Res("ptc2")]
        Tl3 = [tmp[0][:, 0:256], tmp[1][:, 0:256], rt[:, 0:256]]
        r_tl3 = [r_tmp[0], r_tmp[1], r_rt]
        PTl3 = [PTl[0], PTl[1], rstd[:, 0:128].bitcast(BF16)]
        r_ptl3 = [r_sq[0], r_sq[1], r_rstd]
        nast = {"w": 0, "pt": 0, "ks": 0, "vs": 0, "eb": 0, "lt": 0}

        parts = {"ctx", "qk", "v", "attp", "atts", "o"}

        def na_layer(i):
            j = i // 2
            for sbk in range(2):
                jc = sbk
                P.barrier()
                r_q, r_k, r_v, r_vs = Res("q"), Res("k"), Res("v"), Res("vs")
                r_wb = [Res("wb0"), Res("wb1")]
                r_pt = [Res("pt0"), Res("pt1")]
                r_kst = [Res("kst0"), Res("kst1")]
                r_vst = [Res("vst0"), Res("vst1")]
                r_ckst = [Res("ckst0"), Res("ckst1")]
                r_kctx, r_vctx = Res("kctx"), Res("vctx")
                r_eb = [Res("eb0"), Res("eb1")]
                norm_mod(i, 0, sbk)
                if sbk == 1 and "ctx" in parts:
                    for t in range(4):
                        s_ = t % 2
                        P.dma("sp", ckst[s_], ck_in[j, t * 128:(t + 1) * 128, :], writes=[r_ckst[s_]])
                        for half in range(2):
                            b = nb()
                            for c4 in range(4):
                                c = half * 4 + c4
                                tr(banks[b][:, c4 * 128:(c4 + 1) * 128], ckst[s_][:, c * 128:(c + 1) * 128], ident[:],
                                   [r_ckst[s_], r_const], [r_bank[b]])
                            cp("act", kctxT[:, half * 4:half * 4 + 4, t * 128:(t + 1) * 128],
                               banks[b][:, :].rearrange("p (c n) -> p c n", c=4), [r_bank[b]], [r_kctx])
                    P.dma("pool", vctx, cv_in[j, :, :].rearrange("(t p) n -> p t n", p=128), writes=[r_vctx])
                pend_qk = []
                for cb in range(8 if "qk" in parts else 0):
                    s_ = nast["w"] % 2
                    nast["w"] += 1
                    P.dma("pool", wblk[s_], na_w_qkv[j, :, cb * 256:(cb + 1) * 256].rearrange("(k p) n -> p k n", p=128),
                          writes=[r_wb[s_]])
                    isk = cb >= 4
                    for m2 in range(2):
                        m = (cb % 4) * 2 + m2
                        for tb in range(2):
                            b = nb()
                            for kc in range(NCH):
                                mm(banks[b][:, :], wblk[s_][:, kc, m2 * 128:(m2 + 1) * 128], hT[:, kc, tb * 512:(tb + 1) * 512],
                                   kc == 0, kc == NCH - 1, [r_wb[s_], r_hT[tb]], [r_bank[b]])
                            q_ = (m * 2 + tb) % 2
                            act(sq[q_][:], banks[b][:, :], AF.Square, [r_bank[b]], [r_sq[q_]])

                            def _stage_b(b=b, q_=q_, m=m, tb=tb, isk=isk):
                                mm(banks[STAT_BANK][:, :], bones[:], sq[q_][:], True, True, [r_sq[q_], r_const], [r_bank[STAT_BANK]])
                                act(rt[:], banks[STAT_BANK][:, :], AF.Ln, [r_bank[STAT_BANK], r_const], [r_rt], scale=1.0 / 64, bias=epsc[:, 0:1])
                                act(rstd[:], rt[:], AF.Exp, [r_rt], [r_rstd], scale=-0.5)
                                gsc = qkg[:, j, (1 if isk else 0):(2 if isk else 1)]
                                if not isk:
                                    stt(qT[:, m, tb * 512:(tb + 1) * 512], banks[b][:, :], gsc, rstd[:], ALU.mult, ALU.mult,
                                        [r_bank[b], r_rstd, r_const], [r_q])
                                elif sbk == 1:
                                    stt(kT[:, m, tb * 512:(tb + 1) * 512], banks[b][:, :], gsc, rstd[:], ALU.mult, ALU.mult,
                                        [r_bank[b], r_rstd, r_const], [r_k])
                                else:
                                    stt(tmp[q_][:], banks[b][:, :], gsc, rstd[:], ALU.mult, ALU.mult,
                                        [r_bank[b], r_rstd, r_const], [r_tmp[q_]])
                                    cp("act", kT[:, m, tb * 512:(tb + 1) * 512], tmp[q_][:], [r_tmp[q_]], [r_k])
                                    b2 = nb()
                                    for t4 in range(4):
                                        tr(banks[b2][:, t4 * 128:(t4 + 1) * 128], tmp[q_][:, t4 * 128:(t4 + 1) * 128], ident[:],
                                           [r_tmp[q_], r_const], [r_bank[b2]])
                                    ks = nast["ks"] % 2
                                    nast["ks"] += 1
                                    cp("act", kstage[ks], banks[b2][:, :].rearrange("p (t n) -> p t n", t=4), [r_bank[b2]], [r_kst[ks]])
                                    for sq_ in range(2):
                                        seq = tb * 2 + sq_
                                        tk = P.dma("sp", k_out[seq, j, :, m * 128:(m + 1) * 128].rearrange("(t p) n -> p t n", p=128),
                                                   kstage[ks][:, sq_ * 2:sq_ * 2 + 2, :], reads=[r_kst[ks]], is_output=True)
                                        P.store_keys.add(tk[0])

                            pend_qk.append(_stage_b)
                            if len(pend_qk) > 1:
                                pend_qk.pop(0)()
                while pend_qk:
                    pend_qk.pop(0)()
                for cb in (range(8, 12) if "v" in parts else []):
                    s_ = nast["w"] % 2
                    nast["w"] += 1
                    P.dma("pool", wblk[s_], na_w_qkv[j, :, cb * 256:(cb + 1) * 256].rearrange("(k p) n -> p k n", p=128),
                          writes=[r_wb[s_]])
                    vc0 = (cb - 8) * 256
                    ntile = 15 if sbk == 1 else 8
                    for tt_ in range(ntile):
                        shifted = tt_ >= 8
                        tok0 = (tt_ - 8) * 128 + 64 if shifted else tt_ * 128
                        b = nb()
                        for kc in range(NCH):
                            mm(banks[b][:, 0:256], hT[:, kc, tok0:tok0 + 128], wblk[s_][:, kc, :],
                               kc == 0, kc == NCH - 1, [r_wb[s_], r_hT[0], r_hT[1]], [r_bank[b]])
                        if shifted:
                            cp("act", Vs[:, tt_ - 8, vc0:vc0 + 256], banks[b][:, 0:256], [r_bank[b]], [r_vs])
                        elif sbk == 1:
                            cp("act", Vt[:, tt_, vc0:vc0 + 256], banks[b][:, 0:256], [r_bank[b]], [r_v])
                        else:
                            if True:
                                vs_ = nast["vs"] % 2
                                nast["vs"] += 1
                                cp("act", vstage[vs_], banks[b][:, 0:256], [r_bank[b]], [r_vst[vs_]])
                                cp("dve", Vt[:, tt_, vc0:vc0 + 256], vstage[vs_], [r_vst[vs_]], [r_v])
                                tk = P.dma("sp", v_out[tt_ // 2, j, (tt_ % 2) * 128:(tt_ % 2) * 128 + 128, vc0:vc0 + 256],
                                           vstage[vs_], reads=[r_vst[vs_]], is_output=True)
                                P.store_keys.add(tk[0])
                if sbk == 0 and "attp" in parts:
                    pend_p = []
                    for seq in range(4):
                        for m in range(NCH):
                            bo = nb()
                            for hh in range(2):
                                h = 2 * m + hh
                                pb = 64 * hh
                                bs = nb()
                                for kt in range(2):
                                    mm(banks[bs][:, kt * 256:(kt + 1) * 256],
                                       kT[pb:pb + 64, m, seq * 256 + kt * 128:seq * 256 + kt * 128 + 128],
                                       qT[pb:pb + 64, m, seq * 256:seq * 256 + 256], True, True, [r_q, r_k], [r_bank[bs]])
                                k3 = nast["pt"] % 3
                                nast["pt"] += 1
                                act(PTc3[k3], banks[bs][:, :], AF.Exp, [r_bank[bs]], [r_pt3[k3]], scale=0.125)

                                def _pv(bo=bo, k3=k3, pb=pb, h=h, hh=hh, seq=seq, m=m):
                                    for kt in range(2):
                                        mm(banks[bo][pb:pb + 64, 0:256], Vt[:, seq * 2 + kt, h * 64:(h + 1) * 64],
                                           PTc3[k3][:, kt * 256:(kt + 1) * 256], kt == 0, kt == 1, [r_v, r_pt3[k3]], [r_bank[bo]])
                                    for kt in range(2):
                                        mm(banks[bo][pb:pb + 64, 256:512], ones_bf[:, :],
                                           PTc3[k3][:, kt * 256:(kt + 1) * 256], kt == 0, kt == 1, [r_const, r_pt3[k3]], [r_bank[bo]])
                                    if hh == 1:
                                        P.op("dve", lambda: nc.vector.reciprocal(out=rstd[:, 0:256], in_=banks[bo][:, 256:512]),
                                             [r_bank[bo]], [r_rstd])
                                        tt("dve", oT[:, m, seq * 256:seq * 256 + 256], banks[bo][:, 0:256], rstd[:, 0:256], ALU.mult,
                                           [r_bank[bo], r_rstd], [r_hT[seq // 2]])

                                pend_p.append(_pv)
                                if len(pend_p) > 2:
                                    pend_p.pop(0)()
                    while pend_p:
                        pend_p.pop(0)()
                elif sbk == 1 and "atts" in parts:
                    for m in range(NCH):
                        for hh in range(2):
                            h = 2 * m + hh
                            pb = 64 * hh
                            e_ = nast["eb"] % 2
                            nast["eb"] += 1
                            P.dma("sp", EB[:, e_, :], ebias_in[j, h, :, :], writes=[r_eb[e_]])
                            ebv = EB[:, e_, :].rearrange("p (d q) -> p d q", q=64)
                            first = [True, True]
                            units = [("c", qb, kt) for qb in range(2) for kt in range(4)] + [("l", r) for r in range(16)]

                            def stage_a(u, slot):
                                bs = 4 + slot % 3
                                k3 = slot % 3
                                if u[0] == "c":
                                    _, qb, kt = u
                                    mm(banks[bs][:, :], kctxT[pb:pb + 64, m, kt * 128:(kt + 1) * 128],
                                       qT[pb:pb + 64, m, qb * 512:(qb + 1) * 512], True, True, [r_q, r_kctx], [r_bank[bs]])
                                    act(PTc3[k3], banks[bs][:, :], AF.Exp, [r_bank[bs]], [r_pt3[k3]], scale=0.125)
                                else:
                                    r = u[1]
                                    rs = min(max(r - 4, 0), 8)
                                    d0 = rs - r + 7
                                    for m4 in range(4):
                                        kr0 = rs + 2 * m4
                                        mm(banks[bs][:, m4 * 64:(m4 + 1) * 64], kT[pb:pb + 64, m, kr0 * 64:kr0 * 64 + 128],
                                           qT[pb:pb + 64, m, r * 64:(r + 1) * 64], True, True, [r_q, r_k], [r_bank[bs]])
                                    stt(Tl3[k3].rearrange("p (a q) -> p a q", q=64),
                                        banks[bs][:, 0:256].rearrange("p (a q) -> p a q", q=64), 0.125,
                                        ebv[:, d0:d0 + 7:2, :], ALU.mult, ALU.add, [r_bank[bs], r_eb[e_]], [r_tl3[k3]])
                                    act(PTl3[k3], Tl3[k3], AF.Exp, [r_tl3[k3]], [r_ptl3[k3]])

                            def stage_b(u, slot):
                                k3 = slot % 3
                                if u[0] == "c":
                                    _, qb, kt = u
                                    mm(banks[qb][pb:pb + 64, :], vctx[:, kt, h * 64:(h + 1) * 64], PTc3[k3], first[qb], False,
                                       [r_vctx, r_pt3[k3]], [r_bank[qb]])
                                    mm(banks[2 + qb][pb:pb + 64, :], ones_bf[:, :], PTc3[k3], first[qb], False,
                                       [r_const, r_pt3[k3]], [r_bank[2 + qb]])
                                    first[qb] = False
                                else:
                                    r = u[1]
                                    rs = min(max(r - 4, 0), 8)
                                    qb = r // 8
                                    qc0 = (r % 8) * 64
                                    for m4 in range(4):
                                        kr0 = rs + 2 * m4
                                        vsrc = Vt[:, kr0 // 2, h * 64:(h + 1) * 64] if kr0 % 2 == 0 else Vs[:, (kr0 - 1) // 2, h * 64:(h + 1) * 64]
                                        last = (r % 8 == 7) and m4 == 3
                                        mm(banks[qb][pb:pb + 64, qc0:qc0 + 64], vsrc, PTl3[k3][:, m4 * 64:(m4 + 1) * 64], False, last,
                                           [r_v, r_vs, r_ptl3[k3]], [r_bank[qb]])
                                        mm(banks[2 + qb][pb:pb + 64, qc0:qc0 + 64], ones_bf[:, :], PTl3[k3][:, m4 * 64:(m4 + 1) * 64],
                                           False, last, [r_const, r_ptl3[k3]], [r_bank[2 + qb]])

                            g0 = nast["lt"]
                            for t_ in range(len(units) + 2):
                                if t_ < len(units):
                                    stage_a(units[t_], g0 + t_)
                                if t_ >= 2:
                                    stage_b(units[t_ - 2], g0 + t_ - 2)
                            nast["lt"] += len(units)
                        for qb in range(2):
                            P.op("dve", lambda qb=qb: nc.vector.reciprocal(out=rt[:], in_=banks[2 + qb][:, :]),
                                 [r_bank[2 + qb]], [r_rt])
                            tt("dve", oT[:, m, qb * 512:(qb + 1) * 512], banks[qb][:, :], rt[:], ALU.mult,
                               [r_bank[qb], r_rt], [r_hT[qb]])
                for cb in range(4 if "o" in parts else 0):
                    s_ = nast["w"] % 2
                    nast["w"] += 1
                    P.dma("pool", wblk[s_], na_w_o[j, :, cb * 256:(cb + 1) * 256].rearrange("(k p) n -> p k n", p=128),
                          writes=[r_wb[s_]])
                    for tb in range(2):
                        t0 = sbk * 1024 + tb * 512
                        for i2 in range(2):
                            dc = cb * 2 + i2
                            b = nb()
                            for kc in range(NCH):
                                mm(banks[b][:, :], wblk[s_][:, kc, i2 * 128:(i2 + 1) * 128], oT[:, kc, tb * 512:(tb + 1) * 512],
                                   kc == 0, kc == NCH - 1, [r_wb[s_], r_hT[tb]], [r_bank[b]])
                            stt(xT[:, dc, t0:t0 + 512], banks[b][:, :], mod[:, i, 16 + dc, jc:jc + 1], xT[:, dc, t0:t0 + 512],
                                ALU.mult, ALU.add, [r_bank[b], r_mod[i]] + xres(t0, 512), xres(t0, 512))
                ada_step()
            P.barrier()

        S5B = sb("S5B_sb", [128, 2304])
        R1t, R2t, hTt, EBt, S5t = R1, R2, hT, EB, S5B

        def raw(t, psz, off, dims):
            return bass.AP(t, off, [[psz, 128]] + [list(d) for d in dims])

        TM8 = R1[:, 8192:12288].bitcast(BF16)
        TM8t = TM8.tensor
        W2 = R2[:, 0:4096].bitcast(BF16).rearrange("p (d g r n) -> p d g r n", d=2, g=16, r=2)
        Msb = R2[:, 4096:6144].bitcast(BF16).rearrange("p (g n) -> p g n", g=32)
        Z1 = R2[:, 6144:7168].bitcast(BF16).rearrange("p (d g r n) -> p d g r n", d=2, g=4, r=2)
        Z2 = R2[:, 7168:8192].bitcast(BF16).rearrange("p (d g r n) -> p d g r n", d=2, g=4, r=2)
        W1T = R2[:, 8192:9216].bitcast(BF16).rearrange("p (d g r n) -> p d g r n", d=2, g=4, r=2)
        T1 = R2[:, 9216:9728]
        T2 = R2[:, 9728:10240]
        Xin = hT[:, :, :].rearrange("p c n -> p (c n)").rearrange("p (d f k) -> p d f k", d=2, f=32)
        Ust = [sq[s_][:, 0:256].bitcast(BF16) for s_ in range(2)]
        ebf = EB[:, :, :].rearrange("p a n -> p (a n)")
        LAM = ebf[:, 0:192].rearrange("p (a f) -> p a f", a=3)
        MISC = ebf[:, 192:1216].rearrange("p (a f) -> p a f", f=64)
        BRAW = ebf[:, 1216:1472]
        CRAW = ebf[:, 1472:1728]
        H0T = ebf[:, 1728:1792]
        TS = ebf[:, 1856:1920]
        APOW = S5B[:, 0:1152]
        AINV = S5B[:, 1152:2304]
        FIN = T2[:, 256:512]
        r_s5c = Res("s5c")
        r_tm8, r_S, r_W2, r_M, r_Z1, r_Z2, r_W1T = Res("tm8"), Res("S"), Res("W2"), Res("M"), Res("Z1"), Res("Z2"), Res("W1T")
        r_T1, r_T2, r_braw, r_craw, r_xin, r_fin, r_h0 = Res("T1"), Res("T2"), Res("braw"), Res("craw"), Res("xin"), Res("fin"), Res("h0")
        r_ust = [Res("ust0"), Res("ust1")]
        s5st = {"u": 0, "w": 0}
        r_cW1 = [[Res(f"cw1_{a}_{c}") for c in range(4)] for a in range(2)]
        r_cW2 = [Res("cw2_0"), Res("cw2_1")]
        r_cM = [Res("cm_0"), Res("cm_1")]
        W1Tflat = R2[:, 8192:9216].bitcast(BF16)
        W2flat = R2[:, 0:4096].bitcast(BF16)
        Mflat = R2[:, 4096:6144].bitcast(BF16)

        def mi(k):
            return MISC[:, k, :]

        def s5_discretise(j):
            P.dma("sp", LAM, s5lam_in[j, :, :, :], writes=[r_s5c])
            rc = [r_s5c]
            lre, lim, lst = LAM[:, 0, :], LAM[:, 1, :], LAM[:, 2, :]
            step, mag, th2, kk, are, aim, den, nr, t_a, t_b, fre, fim = [mi(k) for k in range(12)]
            act(step, lst, AF.Exp, rc, rc)
            tt("dve", t_a, lre, step, ALU.mult, rc, rc)
            act(mag, t_a, AF.Exp, rc, rc)
            TH = MISC[:, 12:14, :]
            SC = MISC[:, 14:16, :]
            tt("dve", TH[:, 0, :], lim, step, ALU.mult, rc, rc)
            ts("dve", TH[:, 1, :], TH[:, 0, :], math.pi / 2, None, ALU.add, None, rc, rc)
            KI = T1[:, 0:128].bitcast(mybir.dt.int32).rearrange("p (a f) -> p a f", a=2)
            KF = T2[:, 0:128].rearrange("p (a f) -> p a f", a=2)
            ts("dve", KF, TH, 1.0 / (2 * math.pi), None, ALU.mult, None, rc, [r_T2])
            cp("dve", KI, KF, [r_T2], [r_T1])
            cp("dve", KF, KI, [r_T1], [r_T2])
            stt(TH, KF, -2 * math.pi, TH, ALU.mult, ALU.add, [r_T2] + rc, rc)
            ts("dve", KF, TH, math.pi, -2 * math.pi, ALU.is_gt, ALU.mult, rc, [r_T2])
            tt("dve", TH, TH, KF, ALU.add, [r_T2] + rc, rc)
            ts("dve", KF, TH, -math.pi, 2 * math.pi, ALU.is_lt, ALU.mult, rc, [r_T2])
            tt("dve", TH, TH, KF, ALU.add, [r_T2] + rc, rc)
            ts("dve", TH, TH, math.pi, -math.pi, ALU.min, ALU.max, rc, rc)
            act(SC, TH, AF.Sin, rc, rc)
            tt("dve", aim, mag, SC[:, 0, :], ALU.mult, rc, rc)
            tt("dve", are, mag, SC[:, 1, :], ALU.mult, rc, rc)
            tt("dve", den, lre, lre, ALU.mult, rc, rc)
            tt("dve", t_a, lim, lim, ALU.mult, rc, rc)
            tt("dve", den, den, t_a, ALU.add, rc, rc)
            P.op("dve", lambda: nc.vector.reciprocal(out=den, in_=den), rc, rc)
            ts("dve", nr, are, -1.0, None, ALU.add, None, rc, rc)
            tt("dve", t_a, nr, lre, ALU.mult, rc, rc)
            tt("dve", t_b, aim, lim, ALU.mult, rc, rc)
            tt("dve", t_a, t_a, t_b, ALU.add, rc, rc)
            tt("dve", fre, t_a, den, ALU.mult, rc, rc)
            tt("dve", t_a, aim, lre, ALU.mult, rc, rc)
            tt("dve", t_b, nr, lim, ALU.mult, rc, rc)
            tt("dve", t_a, t_a, t_b, ALU.subtract, rc, rc)
            tt("dve", fim, t_a, den, ALU.mult, rc, rc)
            ire, iim = mi(3), mi(6)
            tt("dve", t_a, mag, mag, ALU.mult, rc, rc)
            P.op("dve", lambda: nc.vector.reciprocal(out=t_a, in_=t_a), rc, rc)
            tt("dve", ire, are, t_a, ALU.mult, rc, rc)
            stt(iim, aim, -1.0, t_a, ALU.mult, ALU.mult, rc, rc)

            def pw(tab, e):
                return tab[:, e * 128:e * 128 + 64], tab[:, e * 128 + 64:e * 128 + 128]

            def cmul(o_re, o_im, x_re, x_im, y_re, y_im):
                tt("dve", t_a, x_re, y_re, ALU.mult, rc, rc)
                tt("dve", t_b, x_im, y_im, ALU.mult, rc, rc)
                tt("dve", TS, x_re, y_im, ALU.mult, rc, rc)
                tt("dve", o_re, t_a, t_b, ALU.subtract, rc, rc)
                tt("dve", t_a, x_im, y_re, ALU.mult, rc, rc)
                tt("dve", o_im, TS, t_a, ALU.add, rc, rc)

            for tab, (b_re, b_im) in ((APOW, (are, aim)), (AINV, (ire, iim))):
                r0, i0 = pw(tab, 0)
                P.op("pool", lambda r0=r0: nc.gpsimd.memset(r0, 1.0), rc, rc)
                P.op("pool", lambda i0=i0: nc.gpsimd.memset(i0, 0.0), rc, rc)
                r1, i1 = pw(tab, 1)
                cp("dve", r1, b_re, rc, rc)
                cp("dve", i1, b_im, rc, rc)
                for e in range(2, 9):
                    pr, pi_ = pw(tab, e - 1)
                    cr, ci = pw(tab, e)
                    cmul(cr, ci, pr, pi_, b_re, b_im)
            P.op("pool", lambda: nc.gpsimd.memset(TS, 0.0), rc, rc)
            return fre, fim

        def cprod(o_view, tab, e0, estep, xraw, d, gp0, neg_im, dstr=1024):
            fo = d * 32 + gp0
            er = raw(S5t, 2304, (0 if tab is APOW else 1152) + e0 * 128 + fo, [(estep * 128, 8), (1, 4), (0, 16)])
            ei = raw(S5t, 2304, (0 if tab is APOW else 1152) + e0 * 128 + 64 + fo, [(estep * 128, 8), (1, 4), (0, 16)])
            xoff = 1216 if xraw is BRAW else 1472
            xr = raw(EBt, 1920, xoff + (0 * 2 + d) * 64, [(0, 8), (16, 4), (1, 16)])
            xi = raw(EBt, 1920, xoff + (1 * 2 + d) * 64, [(0, 8), (16, 4), (1, 16)])
            t1 = raw(R2t, 10240, 9216, [(16, 8), (128, 4), (1, 16)])
            t2 = raw(R2t, 10240, 9728, [(16, 8), (128, 4), (1, 16)])
            ovt = o_view.tensor
            obase = o_view.offset
            o_re = raw(ovt, 20480, obase + d * dstr, [(16, 8), (256, 4), (1, 16)])
            o_im = raw(ovt, 20480, obase + d * dstr + 128, [(16, 8), (256, 4), (1, 16)])
            rx = [r_s5c, r_braw if xraw is BRAW else r_craw]
            ro = [r_Z1 if o_view is Z1 else (r_Z2 if o_view is Z2 else r_W2)]
            tt("dve", t1, er, xr, ALU.mult, rx, [r_T1])
            tt("pool", t2, ei, xi, ALU.mult, rx, [r_T2])
            tt("dve", o_re, t1, t2, ALU.subtract, [r_T1, r_T2], ro)
            tt("dve", t1, er, xi, ALU.mult, rx, [r_T1])
            tt("pool", t2, ei, xr, ALU.mult, rx, [r_T2])
            if neg_im:
                tt("dve", t1, t1, t2, ALU.add, [r_T1, r_T2], [r_T1])
                tt("dve", o_im, raw(EBt, 1920, 1856, [(0, 8), (0, 4), (0, 16)]), t1, ALU.subtract, [r_T1, r_s5c], ro)
            else:
                tt("dve", o_im, t1, t2, ALU.add, [r_T1, r_T2], ro)

        def s5_layer(i):
            j = i // 2
            P.barrier()
            fre, fim = s5_discretise(j)
            rc = [r_s5c]
            for sbk in range(2):
                jc = sbk
                P.barrier()
                norm_mod(i, 0, sbk)
                for s_ in range(8):
                    b = nb()
                    pb16 = banks[b][:, :].bitcast(BF16)
                    for dc in range(NCH):
                        hsl = raw(hTt, 8192, dc * 1024 + s_, [(8, 128)])
                        tr(pb16[:, dc * 128:(dc + 1) * 128], hsl, identb[:], [r_hT[0], r_hT[1], r_const], [r_bank[b]])
                    cp("act", raw(TM8t, 24576, TM8.offset + s_ * 16, [(128, 64), (1, 16)]),
                       pb16.rearrange("p (g c) -> p g c", c=16), [r_bank[b]], [r_tm8])
                for gb in range(2):
                    P.barrier()
                    if sbk == 1:
                        P.dma("sp", H0T, s5h0_in[j, :, gb, :], writes=[r_h0])
                    for sub in range(4):
                        gp0 = gb * 16 + sub * 4
                        if sbk == 1:
                            P.dma("sp", W1Tflat, cW1[gb, sub, :, :], reads=[r_cW1[gb][sub]], writes=[r_W1T])
                            if sub == 0:
                                P.dma("sp", W2flat, cW2[gb, :, :], reads=[r_cW2[gb]], writes=[r_W2])
                                P.dma("sp", Mflat, cM[gb, :, :], reads=[r_cM[gb]], writes=[r_M])
                        else:
                            for d in range(2):
                                P.dma("sp", BRAW.rearrange("p (r d g c) -> p r d g c", r=2, d=2, g=4)[:, :, d, :, :],
                                      s5b_in[j, :, :, d * 32 + gp0:d * 32 + gp0 + 4, :], writes=[r_braw])
                                P.dma("sp", CRAW.rearrange("p (r d g c) -> p r d g c", r=2, d=2, g=4)[:, :, d, :, :],
                                      s5c_in[j, :, :, d * 32 + gp0:d * 32 + gp0 + 4, :], writes=[r_craw])
                            bv = BRAW.rearrange("p (r f c) -> p r f c", r=2, f=8)
                            for d in range(2):
                                fr_ = raw(EBt, 1920, 192 + 10 * 64 + d * 32 + gp0, [(1, 4), (0, 16)])
                                fi_ = raw(EBt, 1920, 192 + 11 * 64 + d * 32 + gp0, [(1, 4), (0, 16)])
                                br = bv[:, 0, d * 4:(d + 1) * 4, :]
                                bi_ = bv[:, 1, d * 4:(d + 1) * 4, :]
                                u1 = T1[:, 0:64].rearrange("p (g c) -> p g c", c=16)
                                u2 = T2[:, 0:64].rearrange("p (g c) -> p g c", c=16)
                                u3 = T1[:, 64:128].rearrange("p (g c) -> p g c", c=16)
                                u4 = T2[:, 64:128].rearrange("p (g c) -> p g c", c=16)
                                tt("dve", u1, fr_, br, ALU.mult, [r_s5c, r_braw], [r_T1])
                                tt("dve", u2, fi_, bi_, ALU.mult, [r_s5c, r_braw], [r_T2])
                                tt("dve", u3, fr_, bi_, ALU.mult, [r_s5c, r_braw], [r_T1])
                                tt("dve", u4, fi_, br, ALU.mult, [r_s5c, r_braw], [r_T2])
                                tt("dve", br, u1, u2, ALU.subtract, [r_T1, r_T2], [r_braw])
                                tt("dve", bi_, u3, u4, ALU.add, [r_T1, r_T2], [r_braw])
                            for d in range(2):
                                cprod(Z1, APOW, 7 if d == 0 else 0, -1 if d == 0 else 1, BRAW, d, gp0, False)
                                cprod(Z2, AINV, 7 if d == 0 else 0, -1 if d == 0 else 1, CRAW, d, gp0, True)
                            for hb in range(2):
                                b = nb()
                                pb16 = banks[b][:, :].bitcast(BF16)
                                for q4 in range(8):
                                    idx = hb * 8 + q4
                                    d, gl, r = idx // 8, (idx // 2) % 4, idx % 2
                                    tr(pb16[:, q4 * 128:(q4 + 1) * 128], Z1[:, d, gl, r, :], identb[:], [r_Z1, r_const], [r_bank[b]])
                                cp("act", W1T[:, hb, :, :, :].rearrange("p g r n -> p (g r n)"), pb16, [r_bank[b]], [r_W1T])
                            for gl in range(4):
                                for g2 in range(2):
                                    g = 2 * (gp0 + gl) + g2
                                    glb = 2 * (sub * 4 + gl) + g2
                                    b = nb()
                                    for d in range(2):
                                        for r in range(2):
                                            mm(banks[b][:, d * 128:(d + 1) * 128], Z1[g2 * 64:(g2 + 1) * 64, d, gl, r, :],
                                               Z2[g2 * 64:(g2 + 1) * 64, d, gl, r, :], r == 0, r == 1, [r_Z1, r_Z2], [r_bank[b]])
                                    tt("dve", T1[:, 0:256], banks[b][:, 0:256], masks[:, :], ALU.mult, [r_bank[b], r_const], [r_T1])
                                    tt("dve", T1[:, 0:128], T1[:, 0:128], T1[:, 128:256], ALU.add, [r_T1], [r_T1])
                                    stt(Msb[:, glb, :], ident[:], dvec[:, j, g:g + 1], T1[:, 0:128], ALU.mult, ALU.add,
                                        [r_T1, r_const], [r_M])
                            for d in range(2):
                                cprod(W2[:, :, sub * 4:(sub + 1) * 4, :, :], APOW, 1 if d == 0 else 8, 1 if d == 0 else -1, CRAW, d, gp0, True, dstr=4096)
                            tk = P.dma("sp", cW1[gb, sub, :, :], W1Tflat, reads=[r_W1T], writes=[r_cW1[gb][sub]])
                            P.store_keys.add(tk[0])
                            if sub == 3:
                                tk = P.dma("sp", cW2[gb, :, :], W2flat, reads=[r_W2], writes=[r_cW2[gb]])
                                P.store_keys.add(tk[0])
                                tk = P.dma("sp", cM[gb, :, :], Mflat, reads=[r_M], writes=[r_cM[gb]])
                                P.store_keys.add(tk[0])
                        for gl in range(4):
                            gpl = sub * 4 + gl
                            u_ = s5st["u"] % 2
                            s5st["u"] += 1
                            bu = nb()
                            pu16 = banks[bu][:, :].bitcast(BF16)
                            for g2 in range(2):
                                g = 2 * (gp0 + gl) + g2
                                tr(pu16[:, g2 * 128:(g2 + 1) * 128], TM8[:, g * 128:(g + 1) * 128],
                                   identb[:], [r_tm8, r_const], [r_bank[bu]])
                            cp("act", Ust[u_][:, 0:256], pu16[:, 0:256], [r_bank[bu]], [r_ust[u_]])
                            b = nb()
                            for d in range(2):
                                for r in range(2):
                                    for g2 in range(2):
                                        mm(banks[b][g2 * 64:(g2 + 1) * 64, (d * 2 + r) * 128:(d * 2 + r + 1) * 128],
                                           W1T[:, d, gl, r, g2 * 64:(g2 + 1) * 64], Ust[u_][:, g2 * 128:(g2 + 1) * 128],
                                           True, True, [r_W1T, r_ust[u_]], [r_bank[b]])
                            cp("act", raw(R1t, 12288, gpl * 256, [(4096, 2), (128, 2), (1, 128)]),
                               banks[b][:, :].rearrange("p (d r k) -> p d r k", d=2, r=2), [r_bank[b]], [r_S])
                        ada_step()
                    nseq = 4 if sbk == 0 else 1
                    kl = 32 if sbk == 0 else 128

                    def sview(kk, sel):
                        dstr = 4096 + (kl - 1) - 2 * kk
                        if sel is None:
                            dims = [(dstr, 2), (128, 32)]
                        else:
                            dims = [(dstr, 2), (256, 16)]
                        off = kk + (0 if sel is None else sel * 128)
                        if nseq > 1:
                            dims = dims + [(32, nseq)]
                        return raw(R1t, 12288, off, dims)

                    sq_dims = [(0, nseq)] if nseq > 1 else []
                    tS = raw(R2t, 10240, 9216, [(32 * nseq, 2), (nseq, 32)] + ([(1, nseq)] if nseq > 1 else []))
                    uS = raw(R2t, 10240, 9728, [(32 * nseq, 2), (nseq, 32)] + ([(1, nseq)] if nseq > 1 else []))
                    uS_re = raw(R2t, 10240, 9728, [(32 * nseq, 2), (2 * nseq, 16)] + ([(1, nseq)] if nseq > 1 else []))
                    uS_im = raw(R2t, 10240, 9728 + nseq, [(32 * nseq, 2), (2 * nseq, 16)] + ([(1, nseq)] if nseq > 1 else []))

                    def scan_step(prv_all, prv_re, prv_im, cur):
                        tt("dve", tS, A8R, prv_all, ALU.mult, [r_S, r_s5c, r_h0], [r_T1])
                        tt("dve", uS_re, A8IN, prv_im, ALU.mult, [r_S, r_s5c, r_h0], [r_T2])
                        tt("dve", uS_im, A8I, prv_re, ALU.mult, [r_S, r_s5c, r_h0], [r_T2])
                        tt("dve", tS, tS, uS, ALU.add, [r_T1, r_T2], [r_T1])
                        tt("dve", cur, cur, tS, ALU.add, [r_T1, r_S], [r_S])

                    cof = raw(EBt, 1920, 192 + 12 * 64, [(32, 2), (2, 16), (1, 2)])
                    P.op("dve", lambda cof=cof, gb=gb: nc.vector.tensor_copy(
                        out=cof, in_=raw(S5t, 2304, 8 * 128 + gb * 16, [(32, 2), (1, 16), (0, 2)])), [r_s5c], [r_s5c])
                    coi = raw(EBt, 1920, 192 + 13 * 64, [(16, 2), (1, 16)])
                    P.op("dve", lambda coi=coi, gb=gb: nc.vector.tensor_copy(
                        out=coi, in_=raw(S5t, 2304, 8 * 128 + 64 + gb * 16, [(32, 2), (1, 16)])), [r_s5c], [r_s5c])
                    con = raw(EBt, 1920, 192 + 14 * 64, [(16, 2), (1, 16)])
                    P.op("dve", lambda con=con, coi=coi: nc.vector.tensor_scalar(
                        out=con, in0=coi, scalar1=-1.0, scalar2=None, op0=ALU.mult), [r_s5c], [r_s5c])
                    A8R = raw(EBt, 1920, 192 + 12 * 64, [(32, 2), (1, 32)] + sq_dims)
                    A8I = raw(EBt, 1920, 192 + 13 * 64, [(16, 2), (1, 16)] + sq_dims)
                    A8IN = raw(EBt, 1920, 192 + 14 * 64, [(16, 2), (1, 16)] + sq_dims)
                    if sbk == 1:
                        h_all = raw(EBt, 1920, 1728, [(32, 2), (1, 32)])
                        h_re = raw(EBt, 1920, 1728, [(32, 2), (2, 16)])
                        h_im = raw(EBt, 1920, 1729, [(32, 2), (2, 16)])
                        scan_step(h_all, h_re, h_im, sview(0, None))
                    for kk in range(1, kl):
                        scan_step(sview(kk - 1, None), sview(kk - 1, 0), sview(kk - 1, 1), sview(kk, None))
                    if sbk == 0:
                        for seq in range(4):
                            cp("act", FIN.rearrange("p (s r d g) -> p s r d g", s=4, r=2, d=2)[:, seq, :, :, :],
                               raw(R1t, 12288, seq * 32 + 31, [(128, 2), (4096 - 31, 2), (256, 16)]), [r_S], [r_T2])
                        for seq in range(4):
                            for r in range(2):
                                b = nb()
                                tr(banks[b][0:32, 0:128], FIN.rearrange("p (s r f) -> p s r f", s=4, r=2)[:, seq, r, :], ident[:],
                                   [r_T2, r_const], [r_bank[b]])
                                o_ = s5st["w"] % 2
                                s5st["w"] += 1
                                cp("act", stst[o_][0:32, :], banks[b][0:32, 0:128], [r_bank[b]], [r_stst[o_]])
                                for d in range(2):
                                    tk = P.dma("sp", st_out[seq, j, r, d, gb * 16:(gb + 1) * 16, :],
                                               stst[o_][d * 16:(d + 1) * 16, :], reads=[r_stst[o_]], is_output=True)
                                    P.store_keys.add(tk[0])
                    for quad in range(8):
                        gq0 = gb * 32 + quad * 4
                        u_ = s5st["u"] % 2
                        s5st["u"] += 1
                        bu = nb()
                        pu16 = banks[bu][:, :].bitcast(BF16)
                        for g4 in range(4):
                            g = gq0 + g4
                            tr(pu16[:, g4 * 128:(g4 + 1) * 128], TM8[:, g * 128:(g + 1) * 128],
                               identb[:], [r_tm8, r_const], [r_bank[bu]])
                        cp("act", Ust[u_][:, :], pu16[:, 0:512], [r_bank[bu]], [r_ust[u_]])
                        xq_t = T1 if quad % 2 == 0 else T2
                        r_xq = r_T1 if quad % 2 == 0 else r_T2
                        Xq = xq_t[:, :].bitcast(BF16).rearrange("p (d f k) -> p d f k", d=2, f=4)
                        fr0 = quad * 4
                        if sbk == 1:
                            cp("act", Xq[:, 0, :, 1:128], raw(R1t, 12288, fr0 * 128, [(128, 4), (1, 127)]), [r_S], [r_xq])
                            cp("act", Xq[:, 1, :, 0:127], raw(R1t, 12288, 4096 + fr0 * 128 + 1, [(128, 4), (1, 127)]), [r_S], [r_xq])
                            cp("pool", Xq[:, 0, :, 0:1], raw(EBt, 1920, 1728 + fr0, [(1, 4), (1, 1)]), [r_h0], [r_xq])
                            cp("pool", Xq[:, 1, :, 127:128], raw(EBt, 1920, 1728 + 32 + fr0, [(1, 4), (1, 1)]), [r_h0], [r_xq])
                        else:
                            xv = Xq.rearrange("p d f (s k) -> p d f s k", s=4)
                            cp("act", xv[:, 0, :, :, 1:32], raw(R1t, 12288, fr0 * 128, [(128, 4), (32, 4), (1, 31)]), [r_S], [r_xq])
                            cp("act", xv[:, 1, :, :, 0:31], raw(R1t, 12288, 4096 + fr0 * 128 + 1, [(128, 4), (32, 4), (1, 31)]), [r_S], [r_xq])
                            P.op("pool", lambda xv=xv: nc.gpsimd.memset(xv[:, 0, :, :, 0:1], 0.0), [], [r_xq])
                            P.op("pool", lambda xv=xv: nc.gpsimd.memset(xv[:, 1, :, :, 31:32], 0.0), [], [r_xq])
                        b = nb()
                        for g4 in range(4):
                            g = gq0 + g4
                            glb = quad * 4 + g4
                            gpl, g2 = glb // 2, glb % 2
                            mm(banks[b][:, g4 * 128:(g4 + 1) * 128], Ust[u_][:, g4 * 128:(g4 + 1) * 128], Msb[:, glb, :],
                               True, False, [r_ust[u_], r_M], [r_bank[b]])
                            for d in range(2):
                                for r in range(2):
                                    mm(banks[b][:, g4 * 128:(g4 + 1) * 128], Xq[g2 * 64:(g2 + 1) * 64, d, (g4 // 2) * 2 + r, :],
                                       W2[g2 * 64:(g2 + 1) * 64, d, gpl, r, :], False, d == 1 and r == 1,
                                       [r_xq, r_W2], [r_bank[b]])
                        t_ = tmp[quad % 2]
                        rt_ = r_tmp[quad % 2]
                        act(t_[:], banks[b][:, :], AF.Square, [r_bank[b]], [rt_])
                        ts("dve", t_[:], t_[:], 0.044715, 1.0, ALU.mult, ALU.add, [rt_], [rt_])
                        tt("dve", t_[:], t_[:], banks[b][:, :], ALU.mult, [rt_, r_bank[b]], [rt_])
                        act(t_[:], t_[:], AF.Sigmoid, [rt_], [rt_], scale=1.5957691216057308)
                        tt("dve", raw(hTt, 8192, gq0 * 16, [(16, 4), (1024, 8), (1, 16)]),
                           t_[:].rearrange("p (g t c) -> p g t c", g=4, t=8), banks[b][:, :].rearrange("p (g t c) -> p g t c", g=4, t=8),
                           ALU.mult, [rt_, r_bank[b]], [r_hT[0], r_hT[1]])
                    ada_step()
                P.barrier()
                Yv = hT[:, :, :].rearrange("p c n -> p (c n)")
                zT = TM8.rearrange("p (c n) -> p c n", c=NCH)
                for dc in range(NCH):
                    b = nb()
                    pb16 = banks[b][:, :].bitcast(BF16)
                    for t8 in range(8):
                        tr(pb16[:, t8 * 128:(t8 + 1) * 128], Yv[:, t8 * 1024 + dc * 128:t8 * 1024 + (dc + 1) * 128], identb[:],
                           [r_hT[0], r_hT[1], r_const], [r_bank[b]])
                    cp("act", raw(TM8t, 24576, TM8.offset + dc * 1024, [(1, 8), (8, 128)]), pb16.rearrange("p (t k) -> p t k", t=8),
                       [r_bank[b]], [r_tm8])
                r_wg = [Res("wg0"), Res("wg1")]
                wgb = [[R2[:, (s_ * 2 + w_) * 1024:(s_ * 2 + w_ + 1) * 1024].bitcast(BF16).rearrange("p (k n) -> p k n", k=NCH)
                        for w_ in range(2)] for s_ in range(2)]
                for cb in range(4):
                    s_ = cb % 2
                    P.dma("pool", wgb[s_][0], glu_w[j, :, cb * 256:(cb + 1) * 256].rearrange("(k p) n -> p k n", p=128), writes=[r_wg[s_]])
                    P.dma("pool", wgb[s_][1], glu_w[j, :, D + cb * 256:D + (cb + 1) * 256].rearrange("(k p) n -> p k n", p=128), writes=[r_wg[s_]])
                    for i2 in range(2):
                        dc = cb * 2 + i2
                        for tb in range(2):
                            t0 = sbk * 1024 + tb * 512
                            bv_, bg_ = nb(), nb()
                            for kc in range(NCH):
                                mm(banks[bv_][:, :], wgb[s_][0][:, kc, i2 * 128:(i2 + 1) * 128], zT[:, kc, tb * 512:(tb + 1) * 512],
                                   kc == 0, kc == NCH - 1, [r_wg[s_], r_tm8], [r_bank[bv_]])
                            for kc in range(NCH):
                                mm(banks[bg_][:, :], wgb[s_][1][:, kc, i2 * 128:(i2 + 1) * 128], zT[:, kc, tb * 512:(tb + 1) * 512],
                                   kc == 0, kc == NCH - 1, [r_wg[s_], r_tm8], [r_bank[bg_]])
                            ti = (dc * 2 + tb) % 2
                            act(tmp[ti][:], banks[bg_][:, :], AF.Sigmoid, [r_bank[bg_]], [r_tmp[ti]])
                            tt("dve", tmp[ti][:], tmp[ti][:], banks[bv_][:, :], ALU.mult, [r_tmp[ti], r_bank[bv_]], [r_tmp[ti]])
                            stt(xT[:, dc, t0:t0 + 512], tmp[ti][:], mod[:, i, 16 + dc, jc:jc + 1], xT[:, dc, t0:t0 + 512],
                                ALU.mult, ALU.add, [r_tmp[ti], r_mod[i]] + xres(t0, 512), xres(t0, 512))
                ada_step()
            P.barrier()

        stage = [R1[:, s * 1024:(s + 1) * 1024] for s in range(2)]
        r_stage = [Res("stage0"), Res("stage1")]
        for t in range(16):
            s = t % 2
            P.dma("sp", stage[s], x_tok[t * 128:(t + 1) * 128, :], writes=[r_stage[s]])
            for half in range(2):
                b = nb()
                for j in range(4):
                    c = half * 4 + j
                    tr(banks[b][:, j * 128:(j + 1) * 128], stage[s][:, c * 128:(c + 1) * 128], ident[:],
                       [r_stage[s], r_const], [r_bank[b]])
                cp("act", xT[:, half * 4:half * 4 + 4, t * 128:(t + 1) * 128],
                   banks[b][:, :].rearrange("p (c n) -> p c n", c=4), [r_bank[b]], [r_xT[t]])
        wbig = [R2[:, s_ * 4096:(s_ + 1) * 4096].bitcast(BF16).rearrange("p (k n) -> p k n", k=NCH) for s_ in range(2)]
        r_wbig = [Res("wbig0"), Res("wbig1")]
        mps0 = banks[ADA_BANK][:, 0:96].rearrange("p (a b) -> p a b", b=2)
        for cbig in range(6):
            s_ = cbig % 2
            P.dma("pool", wbig[s_], ada_w[0, :, cbig * 1024:(cbig + 1) * 1024].rearrange("(k p) n -> p k n", p=128),
                  writes=[r_wbig[s_]])
            for j8 in range(8):
                jj = cbig * 8 + j8
                for kc in range(NCH):
                    mm(mps0[:, jj, :], wbig[s_][:, kc, j8 * 128:(j8 + 1) * 128], scond[:, kc, :],
                       kc == 0, kc == NCH - 1, [r_wbig[s_], r_const], [r_bank[ADA_BANK]])
        ada_fin(0)
        P.barrier()

        for i in range(DEPTH):
            if i + 1 < DEPTH:
                ada_layer(i + 1)
            if "na" in stages and i % 2 == 0:
                na_layer(i)
            if "s5" in stages and i % 2 == 1:
                s5_layer(i)
            if "ffn" in stages:
                for sbk in range(2):
                    if sbk == 0:
                        norm_mod(i, 1, 0)
                        ffn(i, 0, between=lambda i=i: norm_mod(i, 1, 1))
                    else:
                        ffn(i, 1)
            ada_flush()

        P.barrier()
        r_stage = [Res("ostage0"), Res("ostage1")]
        for t in range(16):
            s = t % 2
            for half in range(2):
                b = nb()
                for j in range(4):
                    c = half * 4 + j
                    tr(banks[b][:, j * 128:(j + 1) * 128], xT[:, c, t * 128:(t + 1) * 128], ident[:],
                       [r_xT[t], r_const], [r_bank[b]])
                cp("dve", stage[s][:, half * 512:(half + 1) * 512], banks[b][:, :], [r_bank[b]], [r_stage[s]])
            tk = P.dma("sp", y_tok[t * 128:(t + 1) * 128, :], stage[s], reads=[r_stage[s]], is_output=True)
            P.store_keys.add(tk[0])

        P.finish()
        with nc.Block() as block:
            P.emit(block)
    return nc


_NC_CACHE = {}


def _build_ebias(rpb):
    L, H = rpb.shape[0], rpb.shape[1]
    pad = np.concatenate([rpb.reshape(L, H, -1), np.full((L, H, 1), -30000.0, np.float32)], -1)
    u = np.arange(2)[:, None, None, None]
    kc = np.arange(64)[None, :, None, None]
    d = np.arange(15)[None, None, :, None]
    qc = np.arange(64)[None, None, None, :]
    dr = d + u - 7
    cs = np.clip(qc - 8, 0, 48)
    ok = (kc >= cs) & (kc < cs + 16) & (dr <= 7)
    dc = np.clip(kc - qc, -15, 15)
    idx = np.where(ok, (np.clip(dr, -7, 7) + 7) * 31 + dc + 15, 15 * 31)
    idx = np.broadcast_to(idx, (2, 64, 15, 64)).reshape(128, 960)
    return np.ascontiguousarray(pad[:, :, idx])


def _qf(a):
    lead = a.shape[:-3]
    a = a.reshape(lead + (2, 32, 2, 64))
    nl = len(lead)
    a = np.moveaxis(a, (nl + 2, nl + 3, nl + 0, nl + 1), (nl + 0, nl + 1, nl + 2, nl + 3))
    return a.reshape(lead + (128, 64))


def _build_s5(lre, lim, lst, bre, bim, cre, cim, dsk):
    L = lre.shape[0]
    lam = np.stack([_qf(lre), _qf(lim), _qf(np.broadcast_to(lst[..., None], lre.shape))], 2)
    def qfc(a):
        a = a.reshape(L, 2, 32, 2, 64, 16).transpose(0, 3, 4, 1, 2, 5)
        return a.reshape(L, 128, 64, 16)
    b = np.stack([qfc(bre), qfc(bim)], 2)
    ct = lambda a: np.swapaxes(a, -1, -2)
    c = np.stack([qfc(ct(cre)), qfc(ct(cim))], 2)
    dvec = np.ascontiguousarray(np.tile(dsk.reshape(L, 64, 1, 16), (1, 1, 8, 1)).reshape(L, 64, 128).transpose(2, 0, 1))
    s_ = np.arange(128)[:, None] // 16
    t_ = np.arange(128)[None, :] // 16
    masks = np.concatenate([(t_ >= s_), (t_ <= s_)], 1).astype(np.float32)
    return {"lam": np.ascontiguousarray(lam), "b": np.ascontiguousarray(b), "c": np.ascontiguousarray(c),
            "dvec": dvec, "masks": masks}


def _build_h0(hre, him):
    a = np.stack([hre, him], -1)
    L = a.shape[0]
    a = a.reshape(L, 2, 2, 16, 2, 64, 2)
    a = a.transpose(0, 4, 5, 2, 1, 3, 6)
    return np.ascontiguousarray(a.reshape(L, 128, 2, 64))


def _fm(v):
    return np.ascontiguousarray(v.reshape(NCH, 128).T)


def kernel(x_prompt, x_sample, cache_k, cache_v, state_ssm_re, state_ssm_im, c, c_ctx,
           norm_mix, norm_ffn, ada_w, ada_b, na_w_qkv, na_w_o, na_q_gain, na_k_gain, na_rpb,
           ssm_lambda_re, ssm_lambda_im, ssm_log_step, ssm_b_re, ssm_b_im, ssm_c_re, ssm_c_im,
           ssm_d, ssm_w_glu, ffn_w1, ffn_w3, ffn_w2):
    n = 8
    f = lambda a: np.ascontiguousarray(np.asarray(a, np.float32))
    x_prompt, x_sample, c, c_ctx = f(x_prompt), f(x_sample), f(c), f(c_ctx)
    norm_mix, norm_ffn, ada_w, ada_b = f(norm_mix), f(norm_ffn), f(ada_w), f(ada_b)
    ffn_w1, ffn_w3, ffn_w2 = f(ffn_w1), f(ffn_w3), f(ffn_w2)
    na_w_qkv, na_w_o, na_q_gain, na_k_gain, na_rpb = f(na_w_qkv), f(na_w_o), f(na_q_gain), f(na_k_gain), f(na_rpb)
    cache_k, cache_v = f(cache_k), f(cache_v)
    s5in = _build_s5(f(ssm_lambda_re), f(ssm_lambda_im), f(ssm_log_step), f(ssm_b_re), f(ssm_b_im), f(ssm_c_re), f(ssm_c_im), f(ssm_d))
    state_ssm_re, state_ssm_im, ssm_w_glu = f(state_ssm_re), f(state_ssm_im), f(ssm_w_glu)
    qkg = np.ascontiguousarray(np.stack([np.tile(na_q_gain, (1, 2)).T, np.tile(na_k_gain, (1, 2)).T], -1))
    bones = np.kron(np.eye(2, dtype=np.float32), np.ones((64, 64), np.float32))
    ebias = _build_ebias(na_rpb)
    if "nc" not in _NC_CACHE:
        _NC_CACHE["nc"] = build_program()
    nc = _NC_CACHE["nc"]
    ident = np.eye(128, dtype=np.float32)
    gmix = np.ascontiguousarray(norm_mix.reshape(DEPTH, NCH, 128).transpose(2, 0, 1))
    gffn = np.ascontiguousarray(norm_ffn.reshape(DEPTH, NCH, 128).transpose(2, 0, 1))
    adab = np.ascontiguousarray(ada_b.reshape(DEPTH, 48, 128).transpose(2, 0, 1))
    in_maps = []
    for core in range(n):
        xp = x_prompt[4 * core:4 * core + 4].reshape(NP_TOK, D)
        xs = x_sample[core % 2]
        cond = np.ascontiguousarray(np.stack([_fm(c_ctx), _fm(c[core % 2])], -1))
        in_maps.append({"x_tok": np.ascontiguousarray(np.concatenate([xp, xs], 0)), "ident": ident, "cond": cond,
                        "gmix": gmix, "gffn": gffn, "adab": adab, "ada_w": ada_w,
                        "ffn_w1": ffn_w1, "ffn_w3": ffn_w3, "ffn_w2": ffn_w2,
                        "na_w_qkv": na_w_qkv, "na_w_o": na_w_o, "qkg": qkg, "bones": bones, "ebias": ebias,
                        "s5lam": s5in["lam"], "s5b": s5in["b"], "s5c": s5in["c"], "dvec": s5in["dvec"], "masks": s5in["masks"],
                        "s5h0": _build_h0(state_ssm_re[core % 2], state_ssm_im[core % 2]), "glu_w": ssm_w_glu,
                        "ck": np.ascontiguousarray(cache_k[core % 2].reshape(2, 512, D)),
                        "cv": np.ascontiguousarray(cache_v[core % 2].reshape(2, 512, D))})
    res = run_bass_kernel_spmd(nc, in_maps, core_ids=list(range(n)))
    outs = res.results
    y_prompt = np.concatenate([outs[c_]["y_tok"][:NP_TOK].reshape(4, 256, D) for c_ in range(n)], 0)
    y_sample = np.stack([outs[0]["y_tok"][NP_TOK:], outs[1]["y_tok"][NP_TOK:]], 0)
    new_k = np.concatenate([outs[c_]["k_out"] for c_ in range(n)], 0).reshape(32, 2, 256, 16, 64)
    new_v = np.concatenate([outs[c_]["v_out"] for c_ in range(n)], 0).reshape(32, 2, 256, 16, 64)
    st = np.concatenate([outs[c_]["st_out"] for c_ in range(n)], 0).reshape(32, 2, 2, 2, 64, 64)
    st_re = np.ascontiguousarray(st[:, :, 0])
    st_im = np.ascontiguousarray(st[:, :, 1])
    return (y_prompt, y_sample, new_k, new_v, st_re, st_im)
```

```python
import contextlib
import math
import numpy as np
import concourse.bass as bass
import concourse.mybir as mybir
from concourse.bass_utils import run_bass_kernel_spmd

F32 = mybir.dt.float32
BF16 = mybir.dt.bfloat16
AF = mybir.ActivationFunctionType
ALU = mybir.AluOpType

D = 1024
NCH = 8
NT = 2048
NP_TOK = 1024
DFF = 2816
NFF = 22
DEPTH = 4
EPS = 1e-6
NDMA_SEMS = 16


class Res:
    __slots__ = ("name", "w", "r")

    def __init__(self, name):
        self.name = name
        self.w = None
        self.r = {}


class Prog:
    def __init__(self, nc):
        self.nc = nc
        self.eng = {"pe": nc.tensor, "act": nc.scalar, "dve": nc.vector, "pool": nc.gpsimd, "sp": nc.sync}
        self.sem = {e: nc.alloc_semaphore(name=f"sem_{e}") for e in self.eng}
        self.cnt = {e: 0 for e in self.eng}
        self.seen = {e: {} for e in self.eng}
        self.ops = {e: [] for e in self.eng}
        self.dsem = [nc.alloc_semaphore(name=f"sem_dma{i}") for i in range(NDMA_SEMS)]
        self.dval = [0] * NDMA_SEMS
        self.drr = 0
        self.drr_sw = 0
        self.out_tickets = []

    def _sem_of(self, key):
        return self.sem[key] if isinstance(key, str) else self.dsem[key[1]]

    def _deps(self, eng, reads, writes, extra=()):
        need = {}

        def add(t):
            if t is None:
                return
            k, v = t
            if need.get(k, 0) < v:
                need[k] = v

        for r in reads:
            add(r.w)
        for w in writes:
            add(w.w)
            for k, v in w.r.items():
                add((k, v))
        for t in extra:
            add(t)
        waits = []
        for k, v in need.items():
            if eng == "pe" and k == "pe":
                continue
            if self.seen[eng].get(k, 0) < v:
                self.seen[eng][k] = v
                waits.append((k, v))
        return waits

    def _commit(self, tk, reads, writes):
        for r in reads:
            k, v = tk
            if r.r.get(k, 0) < v:
                r.r[k] = v
        for w in writes:
            w.w = tk
            w.r = {}

    def op(self, eng, fn, reads=(), writes=()):
        waits = self._deps(eng, reads, writes)
        self.cnt[eng] += 1
        tk = (eng, self.cnt[eng])
        self.ops[eng].append((waits, fn, self.sem[eng], 1))
        self._commit(tk, reads, writes)
        return tk

    def dma(self, q, out, in_, reads=(), writes=(), is_output=False):
        half = NDMA_SEMS // 2
        if q == "pool":
            i = self.drr_sw
            self.drr_sw = (self.drr_sw + 1) % half
        else:
            i = half + self.drr
            self.drr = (self.drr + 1) % half
        prev = self.dval[i]
        self.dval[i] = prev + 16
        key = ("dma", i)
        extra = [(key, prev)] if prev > 0 else []
        waits = self._deps(q, reads, writes, extra)
        e = self.eng[q]
        fn = lambda: e.dma_start(out=out, in_=in_)
        self.ops[q].append((waits, fn, self.dsem[i], 16))
        tk = (key, prev + 16)
        self._commit(tk, reads, writes)
        if is_output:
            self.out_tickets.append(tk)
        return tk

    def finish(self):
        need = {}
        for k, v in self.out_tickets:
            need[k] = max(need.get(k, 0), v)
        for i in range(NDMA_SEMS):
            if self.dval[i] > 0:
                need[("dma", i)] = self.dval[i]
        for e in self.eng:
            if e != "sp" and self.cnt[e] > 0:
                need[e] = self.cnt[e]
        waits = list(need.items())
        self.ops["sp"].append((waits, None, None, 0))

    def emit(self, block):
        def make(ename):
            def body(_e):
                e = self.eng[ename]
                for waits, fn, sem, inc in self.ops[ename]:
                    for k, v in waits:
                        e.wait_ge(self._sem_of(k), v)
                    if fn is not None:
                        ins = fn()
                        ins.then_inc(sem, inc)
            return body

        block.tensor(make("pe"))
        block.scalar(make("act"))
        block.vector(make("dve"))
        block.gpsimd(make("pool"))
        block.sync(make("sp"))


def _barrier(self):
    comp = [e for e in self.eng if e != "sp"]
    for e in self.eng:
        waits = []
        for o in comp:
            if o == e or self.cnt[o] == 0:
                continue
            if self.seen[e].get(o, 0) < self.cnt[o]:
                self.seen[e][o] = self.cnt[o]
                waits.append((o, self.cnt[o]))
        for i in range(NDMA_SEMS):
            k = ("dma", i)
            if self.dval[i] > 0 and self.seen[e].get(k, 0) < self.dval[i] and k in self.store_keys:
                self.seen[e][k] = self.dval[i]
                waits.append((k, self.dval[i]))
        if waits:
            self.ops[e].append((waits, None, None, 0))


Prog.barrier = _barrier


def build_program(stages=None):
    stages = ("na", "s5", "ffn") if stages is None else stages
    nc = bass.Bass("TRN2", target_bir_lowering=False)

    def din(name, shape, dt=F32):
        return nc.dram_tensor(name, list(shape), dt, kind="ExternalInput").ap()

    def dout(name, shape, dt=F32):
        return nc.dram_tensor(name, list(shape), dt, kind="ExternalOutput").ap()

    x_tok = din("x_tok", [NT, D])
    ident_in = din("ident", [128, 128])
    cond_in = din("cond", [128, NCH, 2])
    gmix_in = din("gmix", [128, DEPTH, NCH])
    gffn_in = din("gffn", [128, DEPTH, NCH])
    adab_in = din("adab", [128, DEPTH, 48])
    ada_w = din("ada_w", [DEPTH, D, 6 * D])
    ffn_w1 = din("ffn_w1", [DEPTH, D, DFF])
    ffn_w3 = din("ffn_w3", [DEPTH, D, DFF])
    ffn_w2 = din("ffn_w2", [DEPTH, DFF, D])
    na_w_qkv = din("na_w_qkv", [2, D, 3 * D])
    na_w_o = din("na_w_o", [2, D, D])
    qkg_in = din("qkg", [128, 2, 2])
    bones_in = din("bones", [128, 128])
    ck_in = din("ck", [2, 512, D])
    cv_in = din("cv", [2, 512, D])
    ebias_in = din("ebias", [2, 16, 128, 960])
    s5lam_in = din("s5lam", [2, 128, 3, 64])
    s5b_in = din("s5b", [2, 128, 2, 64, 16])
    s5c_in = din("s5c", [2, 128, 2, 64, 16])
    s5h0_in = din("s5h0", [2, 128, 2, 64])
    dvec_in = din("dvec", [128, 2, 64])
    masks_in = din("masks", [128, 256])
    glu_w = din("glu_w", [2, D, 2 * D])
    cW1 = nc.dram_tensor("cache_w1t", [2, 4, 128, 2048], BF16, kind="Internal").ap()
    cW2 = nc.dram_tensor("cache_w2", [2, 128, 8192], BF16, kind="Internal").ap()
    cM = nc.dram_tensor("cache_m", [2, 128, 4096], BF16, kind="Internal").ap()
    y_tok = dout("y_tok", [NT, D])
    st_out = dout("st_out", [4, 2, 2, 2, 32, 128])
    k_out = dout("k_out", [4, 2, 256, D])
    v_out = dout("v_out", [4, 2, 256, D])

    P = Prog(nc)
    P.store_keys = set()
    with contextlib.ExitStack() as es:
        def sb(name, shape, dt=F32):
            return es.enter_context(nc.sbuf_tensor(name, list(shape), dt))

        def ps(name, shape, dt=F32):
            return es.enter_context(nc.psum_tensor(name, list(shape), dt))

        xT = sb("xT", [128, NCH, NT])
        hT = sb("hT", [128, NCH, 1024], BF16)
        R1 = sb("R1", [128, 12288])
        R2 = sb("R2", [128, 10240])
        wada = [sb(f"wada{i}", [128, NCH, 128], BF16) for i in range(2)]
        ident = sb("ident_sb", [128, 128])
        ones = sb("ones_sb", [128, 128])
        cond = sb("cond_sb", [128, NCH, 2])
        scond = sb("scond_sb", [128, NCH, 2], BF16)
        gmix = sb("gmix_sb", [128, DEPTH, NCH])
        gffn = sb("gffn_sb", [128, DEPTH, NCH])
        adab = sb("adab_sb", [128, DEPTH, 48])
        mod = sb("mod_sb", [128, DEPTH, 48, 2])
        Amod = sb("Amod_sb", [128, DEPTH, 2, NCH, 2])
        sq = [sb(f"sq{i}", [128, 512]) for i in range(2)]
        rt = sb("rt_sb", [128, 512])
        rstd = sb("rstd_sb", [128, 512])
        tmp = [sb(f"tmp{i}", [128, 512]) for i in range(2)]
        EB = sb("EB_sb", [128, 2, 960])
        qkg = sb("qkg_sb", [128, 2, 2])
        bones = sb("bones_sb", [128, 128])
        ones_bf = sb("ones_bf_sb", [128, 64], BF16)
        identb = sb("identb_sb", [128, 128], BF16)
        epsc = sb("epsc_sb", [128, 1])
        masks = sb("masks_sb", [128, 256])
        dvec = sb("dvec_sb", [128, 2, 64])
        banks = [ps(f"bank{i}", [128, 512]) for i in range(8)]

        r_const = Res("const")
        r_bank = [Res(f"bank{i}") for i in range(8)]
        r_xT = [Res(f"xT{t}") for t in range(16)]
        r_hT = [Res(f"hT{t}") for t in range(2)]
        r_sq = [Res("sq0"), Res("sq1")]
        r_rt = Res("rt")
        r_rstd = Res("rstd")
        r_tmp = [Res("tmp0"), Res("tmp1")]
        r_wada = [Res("wada0"), Res("wada1")]
        r_mod = [Res(f"mod{i}") for i in range(DEPTH)]
        STAT_BANK, ADA_BANK = 6, 7
        state = {"bi": 0}

        def nb():
            b = state["bi"] % 6
            state["bi"] += 1
            return b

        def xres(t0, n):
            return r_xT[t0 // 128:(t0 + n + 127) // 128]

        def mm(out, lhsT, rhs, start, stop, reads, writes):
            P.op("pe", lambda: nc.tensor.matmul(out, lhsT, rhs, start=start, stop=stop), reads, writes)

        def tr(out, in_, idt, reads, writes):
            P.op("pe", lambda: nc.tensor.transpose(out=out, in_=in_, identity=idt), reads, writes)

        def act(out, in_, func, reads, writes, bias=None, scale=None):
            kw = {}
            if bias is not None:
                kw["bias"] = bias
            if scale is not None:
                kw["scale"] = scale
            P.op("act", lambda: nc.scalar.activation(out=out, in_=in_, func=func, **kw), reads, writes)

        def tt(eng, out, in0, in1, op, reads, writes):
            e = nc.vector if eng == "dve" else nc.gpsimd
            P.op(eng, lambda: e.tensor_tensor(out=out, in0=in0, in1=in1, op=op), reads, writes)

        def ts(eng, out, in0, s1, s2, op0, op1, reads, writes):
            e = nc.vector if eng == "dve" else nc.gpsimd
            if op1 is None:
                P.op(eng, lambda: e.tensor_scalar(out=out, in0=in0, scalar1=s1, scalar2=None, op0=op0), reads, writes)
            else:
                P.op(eng, lambda: e.tensor_scalar(out=out, in0=in0, scalar1=s1, scalar2=s2, op0=op0, op1=op1), reads, writes)

        def stt(out, in0, scalar, in1, op0, op1, reads, writes):
            P.op("dve", lambda: nc.vector.scalar_tensor_tensor(out=out, in0=in0, scalar=scalar, in1=in1, op0=op0, op1=op1),
                 reads, writes)

        def cp(eng, out, in_, reads, writes):
            if eng == "act":
                P.op("act", lambda: nc.scalar.copy(out=out, in_=in_), reads, writes)
            elif eng == "dve":
                P.op("dve", lambda: nc.vector.tensor_copy(out=out, in_=in_), reads, writes)
            else:
                P.op("pool", lambda: nc.gpsimd.tensor_copy(out=out, in_=in_), reads, writes)

        P.dma("sp", ident[:], ident_in[:, :], writes=[r_const])
        P.dma("sp", cond[:], cond_in[:, :, :], writes=[r_const])
        P.dma("sp", gmix[:], gmix_in[:, :, :], writes=[r_const])
        P.dma("sp", gffn[:], gffn_in[:, :, :], writes=[r_const])
        P.dma("sp", adab[:], adab_in[:, :, :], writes=[r_const])
        P.op("pool", lambda: nc.gpsimd.memset(ones[:], 1.0), writes=[r_const])
        P.op("pool", lambda: nc.gpsimd.memset(ones_bf[:], 1.0), writes=[r_const])
        P.op("pool", lambda: nc.gpsimd.memset(epsc[:], EPS), writes=[r_const])
        P.dma("sp", qkg[:], qkg_in[:, :, :], writes=[r_const])
        P.dma("sp", bones[:], bones_in[:, :], writes=[r_const])
        act(scond[:], cond[:], AF.Silu, [r_const], [r_const])
        P.dma("sp", masks[:], masks_in[:, :], writes=[r_const])
        P.dma("sp", dvec[:], dvec_in[:, :, :], writes=[r_const])
        cp("dve", identb[:], ident[:], [r_const], [r_const])
        stst = [rt[:, 0:128], rstd[:, 0:128]]
        r_stst = [r_rt, r_rstd]

        pending_ada = []

        def ada_step():
            for _ in range(2):
                if pending_ada:
                    pending_ada.pop(0)()

        def ada_flush():
            while pending_ada:
                pending_ada.pop(0)()

        def ada_layer(i):
            for cb in range(48):
                pending_ada.append(lambda cb=cb: ada_piece(i, cb))
            pending_ada.append(lambda: ada_fin(i))

        def ada_piece(i, cb):
            mps = banks[ADA_BANK][:, 0:96].rearrange("p (a b) -> p a b", b=2)
            if True:
                s = cb % 2
                P.dma("pool", wada[s][:], ada_w[i, :, cb * 128:(cb + 1) * 128].rearrange("(k p) n -> p k n", p=128),
                      writes=[r_wada[s]])
                for j2 in range(1):
                    j = cb
                    for kc in range(NCH):
                        mm(mps[:, j, :], wada[s][:, kc, :], scond[:, kc, :],
                           kc == 0, kc == NCH - 1, [r_wada[s], r_const], [r_bank[ADA_BANK]])
        def ada_fin(i):
            mps = banks[ADA_BANK][:, 0:96].rearrange("p (a b) -> p a b", b=2)
            for jc in range(2):
                tt("dve", mod[:, i, :, jc], mps[:, :, jc], adab[:, i, :], ALU.add, [r_bank[ADA_BANK], r_const], [r_mod[i]])
            for jc in range(2):
                stt(Amod[:, i, 0, :, jc], mod[:, i, 8:16, jc], 1.0, gmix[:, i, :], ALU.add, ALU.mult, [r_mod[i], r_const], [r_mod[i]])
                stt(Amod[:, i, 1, :, jc], mod[:, i, 32:40, jc], 1.0, gffn[:, i, :], ALU.add, ALU.mult, [r_mod[i], r_const], [r_mod[i]])

        def norm_mod(i, which, sbk):
            jc = sbk
            sh0 = 0 if which == 0 else 24
            for tb in range(2):
                t0 = sbk * 1024 + tb * 512
                xr = xres(t0, 512)
                for c in range(NCH):
                    s = c % 2
                    act(sq[s][:], xT[:, c, t0:t0 + 512], AF.Square, xr, [r_sq[s]])
                    mm(banks[STAT_BANK][:, :], ones[:], sq[s][:], c == 0, c == NCH - 1, [r_sq[s], r_const], [r_bank[STAT_BANK]])
                act(rt[:], banks[STAT_BANK][:, :], AF.Ln, [r_bank[STAT_BANK], r_const], [r_rt], scale=1.0 / D, bias=epsc[:, 0:1])
                act(rstd[:], rt[:], AF.Exp, [r_rt], [r_rstd], scale=-0.5)
                for c in range(NCH):
                    s = c % 2
                    stt(tmp[s][:], xT[:, c, t0:t0 + 512], Amod[:, i, which, c, jc:jc + 1], rstd[:], ALU.mult, ALU.mult,
                        xr + [r_rstd, r_mod[i]], [r_tmp[s]])
                    act(hT[:, c, tb * 512:(tb + 1) * 512], tmp[s][:], AF.Identity, [r_tmp[s], r_mod[i]], [r_hT[tb]],
                        bias=mod[:, i, sh0 + c, jc:jc + 1])

        gT = R1[:, 0:11264].bitcast(BF16).rearrange("p (j n) -> p j n", j=NFF)
        w13 = [[R2[:, (s * 2 + w) * 1024:(s * 2 + w + 1) * 1024].bitcast(BF16).rearrange("p (k n) -> p k n", k=NCH)
                for w in range(2)] for s in range(2)]
        w2b = [R2[:, 4096 + s * 2816:4096 + (s + 1) * 2816].bitcast(BF16).rearrange("p (j n) -> p j n", j=NFF)
               for s in range(2)]
        r_gT = [Res("gT0"), Res("gT1")]
        r_w13 = [Res("w13_0"), Res("w13_1")]
        r_w2 = [Res("w2_0"), Res("w2_1")]
        wcount = {"a": 0, "b": 0}

        def ffn(i, sbk, between=None):
            jc = sbk
            for wb in range(11):
                s = wcount["a"] % 2
                wcount["a"] += 1
                P.dma("pool", w13[s][0], ffn_w1[i, :, wb * 256:(wb + 1) * 256].rearrange("(k p) n -> p k n", p=128), writes=[r_w13[s]])
                P.dma("pool", w13[s][1], ffn_w3[i, :, wb * 256:(wb + 1) * 256].rearrange("(k p) n -> p k n", p=128), writes=[r_w13[s]])
                for j2 in range(2):
                    j = wb * 2 + j2
                    for tb in range(2):
                        b1, b3 = nb(), nb()
                        for kc in range(NCH):
                            mm(banks[b1][:, :], w13[s][0][:, kc, j2 * 128:(j2 + 1) * 128], hT[:, kc, tb * 512:(tb + 1) * 512],
                               kc == 0, kc == NCH - 1, [r_w13[s], r_hT[tb]], [r_bank[b1]])
                        for kc in range(NCH):
                            mm(banks[b3][:, :], w13[s][1][:, kc, j2 * 128:(j2 + 1) * 128], hT[:, kc, tb * 512:(tb + 1) * 512],
                               kc == 0, kc == NCH - 1, [r_w13[s], r_hT[tb]], [r_bank[b3]])
                        ti = (j * 2 + tb) % 2
                        act(tmp[ti][:], banks[b1][:, :], AF.Silu, [r_bank[b1]], [r_tmp[ti]])
                        tt("dve", gT[:, j, tb * 512:(tb + 1) * 512], tmp[ti][:], banks[b3][:, :], ALU.mult,
                           [r_tmp[ti], r_bank[b3]], [r_gT[tb]])
                ada_step()
            if between is not None:
                between()
            for cb in range(4):
                ada_step()
                s = wcount["b"] % 2
                wcount["b"] += 1
                P.dma("pool", w2b[s], ffn_w2[i, :, cb * 256:(cb + 1) * 256].rearrange("(j p) n -> p j n", p=128), writes=[r_w2[s]])
                for tb in range(2):
                    t0 = sbk * 1024 + tb * 512
                    for i2 in range(2):
                        dc = cb * 2 + i2
                        b = nb()
                        for j in range(NFF):
                            mm(banks[b][:, :], w2b[s][:, j, i2 * 128:(i2 + 1) * 128], gT[:, j, tb * 512:(tb + 1) * 512],
                               j == 0, j == NFF - 1, [r_w2[s], r_gT[tb]], [r_bank[b]])
                        stt(xT[:, dc, t0:t0 + 512], banks[b][:, :], mod[:, i, 40 + dc, jc:jc + 1], xT[:, dc, t0:t0 + 512],
                            ALU.mult, ALU.add, [r_bank[b], r_mod[i]] + xres(t0, 512), xres(t0, 512))


        qT = R1[:, 0:4096].bitcast(BF16).rearrange("p (c n) -> p c n", c=NCH)
        kT = R1[:, 4096:8192].bitcast(BF16).rearrange("p (c n) -> p c n", c=NCH)
        Vt = R1[:, 8192:12288].bitcast(BF16).rearrange("p (t n) -> p t n", t=8)
        Vs = R2[:, 0:3584].bitcast(BF16).rearrange("p (t n) -> p t n", t=7)
        ckst = [R2[:, s_ * 1024:(s_ + 1) * 1024] for s_ in range(2)]
        kstage = [R2[:, s_ * 512:(s_ + 1) * 512].rearrange("p (t n) -> p t n", t=4) for s_ in range(2)]
        vstage = [R2[:, 1024 + s_ * 256:1024 + (s_ + 1) * 256] for s_ in range(2)]
        kctxT = R2[:, 3584:5632].bitcast(BF16).rearrange("p (c n) -> p c n", c=NCH)
        vctx = R2[:, 5632:7680].bitcast(BF16).rearrange("p (t n) -> p t n", t=4)
        wblk = [R2[:, 7680 + s_ * 1024:7680 + (s_ + 1) * 1024].bitcast(BF16).rearrange("p (k n) -> p k n", k=NCH) for s_ in range(2)]
        PTc = [R2[:, 9728 + s_ * 256:9728 + (s_ + 1) * 256].bitcast(BF16) for s_ in range(2)]
        PTl = [sq[s_][:, 0:128].bitcast(BF16) for s_ in range(2)]
        oT = hT
        PTc3 = [PTc[0], PTc[1], tmp[0][:, 256:512].bitcast(BF16)]
        r_pt3 = [Res("ptc0"), Res("ptc1"), Res("ptc2")]
        Tl3 = [tmp[0][:, 0:256], tmp[1][:, 0:256], rt[:, 0:256]]
        r_tl3 = [r_tmp[0], r_tmp[1], r_rt]
        PTl3 = [PTl[0], PTl[1], rstd[:, 0:128].bitcast(BF16)]
        r_ptl3 = [r_sq[0], r_sq[1], r_rstd]
        nast = {"w": 0, "pt": 0, "ks": 0, "vs": 0, "eb": 0, "lt": 0}

        parts = {"ctx", "qk", "v", "attp", "atts", "o"}

        def na_layer(i):
            j = i // 2
            for sbk in range(2):
                jc = sbk
                P.barrier()
                r_q, r_k, r_v, r_vs = Res("q"), Res("k"), Res("v"), Res("vs")
                r_wb = [Res("wb0"), Res("wb1")]
                r_pt = [Res("pt0"), Res("pt1")]
                r_kst = [Res("kst0"), Res("kst1")]
                r_vst = [Res("vst0"), Res("vst1")]
                r_ckst = [Res("ckst0"), Res("ckst1")]
                r_kctx, r_vctx = Res("kctx"), Res("vctx")
                r_eb = [Res("eb0"), Res("eb1")]
                norm_mod(i, 0, sbk)
                if sbk == 1 and "ctx" in parts:
                    for t in range(4):
                        s_ = t % 2
                        P.dma("sp", ckst[s_], ck_in[j, t * 128:(t + 1) * 128, :], writes=[r_ckst[s_]])
                        for half in range(2):
                            b = nb()
                            for c4 in range(4):
                                c = half * 4 + c4
                                tr(banks[b][:, c4 * 128:(c4 + 1) * 128], ckst[s_][:, c * 128:(c + 1) * 128], ident[:],
                                   [r_ckst[s_], r_const], [r_bank[b]])
                            cp("act", kctxT[:, half * 4:half * 4 + 4, t * 128:(t + 1) * 128],
                               banks[b][:, :].rearrange("p (c n) -> p c n", c=4), [r_bank[b]], [r_kctx])
                    P.dma("pool", vctx, cv_in[j, :, :].rearrange("(t p) n -> p t n", p=128), writes=[r_vctx])
                pend_qk = []
                for cb in range(8 if "qk" in parts else 0):
                    s_ = nast["w"] % 2
                    nast["w"] += 1
                    P.dma("pool", wblk[s_], na_w_qkv[j, :, cb * 256:(cb + 1) * 256].rearrange("(k p) n -> p k n", p=128),
                          writes=[r_wb[s_]])
                    isk = cb >= 4
                    for m2 in range(2):
                        m = (cb % 4) * 2 + m2
                        for tb in range(2):
                            b = nb()
                            for kc in range(NCH):
                                mm(banks[b][:, :], wblk[s_][:, kc, m2 * 128:(m2 + 1) * 128], hT[:, kc, tb * 512:(tb + 1) * 512],
                                   kc == 0, kc == NCH - 1, [r_wb[s_], r_hT[tb]], [r_bank[b]])
                            q_ = (m * 2 + tb) % 2
                            act(sq[q_][:], banks[b][:, :], AF.Square, [r_bank[b]], [r_sq[q_]])

                            def _stage_b(b=b, q_=q_, m=m, tb=tb, isk=isk):
                                mm(banks[STAT_BANK][:, :], bones[:], sq[q_][:], True, True, [r_sq[q_], r_const], [r_bank[STAT_BANK]])
                                act(rt[:], banks[STAT_BANK][:, :], AF.Ln, [r_bank[STAT_BANK], r_const], [r_rt], scale=1.0 / 64, bias=epsc[:, 0:1])
                                act(rstd[:], rt[:], AF.Exp, [r_rt], [r_rstd], scale=-0.5)
                                gsc = qkg[:, j, (1 if isk else 0):(2 if isk else 1)]
                                if not isk:
                                    stt(qT[:, m, tb * 512:(tb + 1) * 512], banks[b][:, :], gsc, rstd[:], ALU.mult, ALU.mult,
                                        [r_bank[b], r_rstd, r_const], [r_q])
                                elif sbk == 1:
                                    stt(kT[:, m, tb * 512:(tb + 1) * 512], banks[b][:, :], gsc, rstd[:], ALU.mult, ALU.mult,
                                        [r_bank[b], r_rstd, r_const], [r_k])
                                else:
                                    stt(tmp[q_][:], banks[b][:, :], gsc, rstd[:], ALU.mult, ALU.mult,
                                        [r_bank[b], r_rstd, r_const], [r_tmp[q_]])
                                    cp("act", kT[:, m, tb * 512:(tb + 1) * 512], tmp[q_][:], [r_tmp[q_]], [r_k])
                                    b2 = nb()
                                    for t4 in range(4):
                                        tr(banks[b2][:, t4 * 128:(t4 + 1) * 128], tmp[q_][:, t4 * 128:(t4 + 1) * 128], ident[:],
                                           [r_tmp[q_], r_const], [r_bank[b2]])
                                    ks = nast["ks"] % 2
                                    nast["ks"] += 1
                                    cp("act", kstage[ks], banks[b2][:, :].rearrange("p (t n) -> p t n", t=4), [r_bank[b2]], [r_kst[ks]])
                                    for sq_ in range(2):
                                        seq = tb * 2 + sq_
                                        tk = P.dma("sp", k_out[seq, j, :, m * 128:(m + 1) * 128].rearrange("(t p) n -> p t n", p=128),
                                                   kstage[ks][:, sq_ * 2:sq_ * 2 + 2, :], reads=[r_kst[ks]], is_output=True)
                                        P.store_keys.add(tk[0])

                            pend_qk.append(_stage_b)
                            if len(pend_qk) > 1:
                                pend_qk.pop(0)()
                while pend_qk:
                    pend_qk.pop(0)()
                for cb in (range(8, 12) if "v" in parts else []):
                    s_ = nast["w"] % 2
                    nast["w"] += 1
                    P.dma("pool", wblk[s_], na_w_qkv[j, :, cb * 256:(cb + 1) * 256].rearrange("(k p) n -> p k n", p=128),
                          writes=[r_wb[s_]])
                    vc0 = (cb - 8) * 256
                    ntile = 15 if sbk == 1 else 8
                    for tt_ in range(ntile):
                        shifted = tt_ >= 8
                        tok0 = (tt_ - 8) * 128 + 64 if shifted else tt_ * 128
                        b = nb()
                        for kc in range(NCH):
                            mm(banks[b][:, 0:256], hT[:, kc, tok0:tok0 + 128], wblk[s_][:, kc, :],
                               kc == 0, kc == NCH - 1, [r_wb[s_], r_hT[0], r_hT[1]], [r_bank[b]])
                        if shifted:
                            cp("act", Vs[:, tt_ - 8, vc0:vc0 + 256], banks[b][:, 0:256], [r_bank[b]], [r_vs])
                        elif sbk == 1:
                            cp("act", Vt[:, tt_, vc0:vc0 + 256], banks[b][:, 0:256], [r_bank[b]], [r_v])
                        else:
                            if True:
                                vs_ = nast["vs"] % 2
                                nast["vs"] += 1
                                cp("act", vstage[vs_], banks[b][:, 0:256], [r_bank[b]], [r_vst[vs_]])
                                cp("dve", Vt[:, tt_, vc0:vc0 + 256], vstage[vs_], [r_vst[vs_]], [r_v])
                                tk = P.dma("sp", v_out[tt_ // 2, j, (tt_ % 2) * 128:(tt_ % 2) * 128 + 128, vc0:vc0 + 256],
                                           vstage[vs_], reads=[r_vst[vs_]], is_output=True)
                                P.store_keys.add(tk[0])
                if sbk == 0 and "attp" in parts:
                    pend_p = []
                    for seq in range(4):
                        for m in range(NCH):
                            bo = nb()
                            for hh in range(2):
                                h = 2 * m + hh
                                pb = 64 * hh
                                bs = nb()
                                for kt in range(2):
                                    mm(banks[bs][:, kt * 256:(kt + 1) * 256],
                                       kT[pb:pb + 64, m, seq * 256 + kt * 128:seq * 256 + kt * 128 + 128],
                                       qT[pb:pb + 64, m, seq * 256:seq * 256 + 256], True, True, [r_q, r_k], [r_bank[bs]])
                                k3 = nast["pt"] % 3
                                nast["pt"] += 1
                                act(PTc3[k3], banks[bs][:, :], AF.Exp, [r_bank[bs]], [r_pt3[k3]], scale=0.125)

                                def _pv(bo=bo, k3=k3, pb=pb, h=h, hh=hh, seq=seq, m=m):
                                    for kt in range(2):
                                        mm(banks[bo][pb:pb + 64, 0:256], Vt[:, seq * 2 + kt, h * 64:(h + 1) * 64],
                                           PTc3[k3][:, kt * 256:(kt + 1) * 256], kt == 0, kt == 1, [r_v, r_pt3[k3]], [r_bank[bo]])
                                    for kt in range(2):
                                        mm(banks[bo][pb:pb + 64, 256:512], ones_bf[:, :],
                                           PTc3[k3][:, kt * 256:(kt + 1) * 256], kt == 0, kt == 1, [r_const, r_pt3[k3]], [r_bank[bo]])
                                    if hh == 1:
                                        P.op("dve", lambda: nc.vector.reciprocal(out=rstd[:, 0:256], in_=banks[bo][:, 256:512]),
                                             [r_bank[bo]], [r_rstd])
                                        tt("dve", oT[:, m, seq * 256:seq * 256 + 256], banks[bo][:, 0:256], rstd[:, 0:256], ALU.mult,
                                           [r_bank[bo], r_rstd], [r_hT[seq // 2]])

                                pend_p.append(_pv)
                                if len(pend_p) > 2:
                                    pend_p.pop(0)()
                    while pend_p:
                        pend_p.pop(0)()
                elif sbk == 1 and "atts" in parts:
                    for m in range(NCH):
                        for hh in range(2):
                            h = 2 * m + hh
                            pb = 64 * hh
                            e_ = nast["eb"] % 2
                            nast["eb"] += 1
                            P.dma("sp", EB[:, e_, :], ebias_in[j, h, :, :], writes=[r_eb[e_]])
                            ebv = EB[:, e_, :].rearrange("p (d q) -> p d q", q=64)
                            first = [True, True]
                            units = [("c", qb, kt) for qb in range(2) for kt in range(4)] + [("l", r) for r in range(16)]

                            def stage_a(u, slot):
                                bs = 4 + slot % 3
                                k3 = slot % 3
                                if u[0] == "c":
                                    _, qb, kt = u
                                    mm(banks[bs][:, :], kctxT[pb:pb + 64, m, kt * 128:(kt + 1) * 128],
                                       qT[pb:pb + 64, m, qb * 512:(qb + 1) * 512], True, True, [r_q, r_kctx], [r_bank[bs]])
                                    act(PTc3[k3], banks[bs][:, :], AF.Exp, [r_bank[bs]], [r_pt3[k3]], scale=0.125)
                                else:
                                    r = u[1]
                                    rs = min(max(r - 4, 0), 8)
                                    d0 = rs - r + 7
                                    for m4 in range(4):
                                        kr0 = rs + 2 * m4
                                        mm(banks[bs][:, m4 * 64:(m4 + 1) * 64], kT[pb:pb + 64, m, kr0 * 64:kr0 * 64 + 128],
                                           qT[pb:pb + 64, m, r * 64:(r + 1) * 64], True, True, [r_q, r_k], [r_bank[bs]])
                                    stt(Tl3[k3].rearrange("p (a q) -> p a q", q=64),
                                        banks[bs][:, 0:256].rearrange("p (a q) -> p a q", q=64), 0.125,
                                        ebv[:, d0:d0 + 7:2, :], ALU.mult, ALU.add, [r_bank[bs], r_eb[e_]], [r_tl3[k3]])
                                    act(PTl3[k3], Tl3[k3], AF.Exp, [r_tl3[k3]], [r_ptl3[k3]])

                            def stage_b(u, slot):
                                k3 = slot % 3
                                if u[0] == "c":
                                    _, qb, kt = u
                                    mm(banks[qb][pb:pb + 64, :], vctx[:, kt, h * 64:(h + 1) * 64], PTc3[k3], first[qb], False,
                                       [r_vctx, r_pt3[k3]], [r_bank[qb]])
                                    mm(banks[2 + qb][pb:pb + 64, :], ones_bf[:, :], PTc3[k3], first[qb], False,
                                       [r_const, r_pt3[k3]], [r_bank[2 + qb]])
                                    first[qb] = False
                                else:
                                    r = u[1]
                                    rs = min(max(r - 4, 0), 8)
                                    qb = r // 8
                                    qc0 = (r % 8) * 64
                                    for m4 in range(4):
                                        kr0 = rs + 2 * m4
                                        vsrc = Vt[:, kr0 // 2, h * 64:(h + 1) * 64] if kr0 % 2 == 0 else Vs[:, (kr0 - 1) // 2, h * 64:(h + 1) * 64]
                                        last = (r % 8 == 7) and m4 == 3
                                        mm(banks[qb][pb:pb + 64, qc0:qc0 + 64], vsrc, PTl3[k3][:, m4 * 64:(m4 + 1) * 64], False, last,
                                           [r_v, r_vs, r_ptl3[k3]], [r_bank[qb]])
                                        mm(banks[2 + qb][pb:pb + 64, qc0:qc0 + 64], ones_bf[:, :], PTl3[k3][:, m4 * 64:(m4 + 1) * 64],
                                           False, last, [r_const, r_ptl3[k3]], [r_bank[2 + qb]])

                            g0 = nast["lt"]
                            for t_ in range(len(units) + 2):
                                if t_ < len(units):
                                    stage_a(units[t_], g0 + t_)
                                if t_ >= 2:
                                    stage_b(units[t_ - 2], g0 + t_ - 2)
                            nast["lt"] += len(units)
                        for qb in range(2):
                            P.op("dve", lambda qb=qb: nc.vector.reciprocal(out=rt[:], in_=banks[2 + qb][:, :]),
                                 [r_bank[2 + qb]], [r_rt])
                            tt("dve", oT[:, m, qb * 512:(qb + 1) * 512], banks[qb][:, :], rt[:], ALU.mult,
                               [r_bank[qb], r_rt], [r_hT[qb]])
                for cb in range(4 if "o" in parts else 0):
                    s_ = nast["w"] % 2
                    nast["w"] += 1
                    P.dma("pool", wblk[s_], na_w_o[j, :, cb * 256:(cb + 1) * 256].rearrange("(k p) n -> p k n", p=128),
                          writes=[r_wb[s_]])
                    for tb in range(2):
                        t0 = sbk * 1024 + tb * 512
                        for i2 in range(2):
                            dc = cb * 2 + i2
                            b = nb()
                            for kc in range(NCH):
                                mm(banks[b][:, :], wblk[s_][:, kc, i2 * 128:(i2 + 1) * 128], oT[:, kc, tb * 512:(tb + 1) * 512],
                                   kc == 0, kc == NCH - 1, [r_wb[s_], r_hT[tb]], [r_bank[b]])
                            stt(xT[:, dc, t0:t0 + 512], banks[b][:, :], mod[:, i, 16 + dc, jc:jc + 1], xT[:, dc, t0:t0 + 512],
                                ALU.mult, ALU.add, [r_bank[b], r_mod[i]] + xres(t0, 512), xres(t0, 512))
                ada_step()
            P.barrier()

        S5B = sb("S5B_sb", [128, 2304])
        R1t, R2t, hTt, EBt, S5t = R1, R2, hT, EB, S5B

        def raw(t, psz, off, dims):
            return bass.AP(t, off, [[psz, 128]] + [list(d) for d in dims])

        TM8 = R1[:, 8192:12288].bitcast(BF16)
        TM8t = TM8.tensor
        W2 = R2[:, 0:4096].bitcast(BF16).rearrange("p (d g r n) -> p d g r n", d=2, g=16, r=2)
        Msb = R2[:, 4096:6144].bitcast(BF16).rearrange("p (g n) -> p g n", g=32)
        Z1 = R2[:, 6144:7168].bitcast(BF16).rearrange("p (d g r n) -> p d g r n", d=2, g=4, r=2)
        Z2 = R2[:, 7168:8192].bitcast(BF16).rearrange("p (d g r n) -> p d g r n", d=2, g=4, r=2)
        W1T = R2[:, 8192:9216].bitcast(BF16).rearrange("p (d g r n) -> p d g r n", d=2, g=4, r=2)
        T1 = R2[:, 9216:9728]
        T2 = R2[:, 9728:10240]
        Xin = hT[:, :, :].rearrange("p c n -> p (c n)").rearrange("p (d f k) -> p d f k", d=2, f=32)
        Ust = [sq[s_][:, 0:256].bitcast(BF16) for s_ in range(2)]
        ebf = EB[:, :, :].rearrange("p a n -> p (a n)")
        LAM = ebf[:, 0:192].rearrange("p (a f) -> p a f", a=3)
        MISC = ebf[:, 192:1216].rearrange("p (a f) -> p a f", f=64)
        BRAW = ebf[:, 1216:1472]
        CRAW = ebf[:, 1472:1728]
        H0T = ebf[:, 1728:1792]
        TS = ebf[:, 1856:1920]
        APOW = S5B[:, 0:1152]
        AINV = S5B[:, 1152:2304]
        FIN = T2[:, 256:512]
        r_s5c = Res("s5c")
        r_tm8, r_S, r_W2, r_M, r_Z1, r_Z2, r_W1T = Res("tm8"), Res("S"), Res("W2"), Res("M"), Res("Z1"), Res("Z2"), Res("W1T")
        r_T1, r_T2, r_braw, r_craw, r_xin, r_fin, r_h0 = Res("T1"), Res("T2"), Res("braw"), Res("craw"), Res("xin"), Res("fin"), Res("h0")
        r_ust = [Res("ust0"), Res("ust1")]
        s5st = {"u": 0, "w": 0}
        r_cW1 = [[Res(f"cw1_{a}_{c}") for c in range(4)] for a in range(2)]
        r_cW2 = [Res("cw2_0"), Res("cw2_1")]
        r_cM = [Res("cm_0"), Res("cm_1")]
        W1Tflat = R2[:, 8192:9216].bitcast(BF16)
        W2flat = R2[:, 0:4096].bitcast(BF16)
        Mflat = R2[:, 4096:6144].bitcast(BF16)

        def mi(k):
            return MISC[:, k, :]

        def s5_discretise(j):
            P.dma("sp", LAM, s5lam_in[j, :, :, :], writes=[r_s5c])
            rc = [r_s5c]
            lre, lim, lst = LAM[:, 0, :], LAM[:, 1, :], LAM[:, 2, :]
            step, mag, th2, kk, are, aim, den, nr, t_a, t_b, fre, fim = [mi(k) for k in range(12)]
            act(step, lst, AF.Exp, rc, rc)
            tt("dve", t_a, lre, step, ALU.mult, rc, rc)
            act(mag, t_a, AF.Exp, rc, rc)
            TH = MISC[:, 12:14, :]
            SC = MISC[:, 14:16, :]
            tt("dve", TH[:, 0, :], lim, step, ALU.mult, rc, rc)
            ts("dve", TH[:, 1, :], TH[:, 0, :], math.pi / 2, None, ALU.add, None, rc, rc)
            KI = T1[:, 0:128].bitcast(mybir.dt.int32).rearrange("p (a f) -> p a f", a=2)
            KF = T2[:, 0:128].rearrange("p (a f) -> p a f", a=2)
            ts("dve", KF, TH, 1.0 / (2 * math.pi), None, ALU.mult, None, rc, [r_T2])
            cp("dve", KI, KF, [r_T2], [r_T1])
            cp("dve", KF, KI, [r_T1], [r_T2])
            stt(TH, KF, -2 * math.pi, TH, ALU.mult, ALU.add, [r_T2] + rc, rc)
            ts("dve", KF, TH, math.pi, -2 * math.pi, ALU.is_gt, ALU.mult, rc, [r_T2])
            tt("dve", TH, TH, KF, ALU.add, [r_T2] + rc, rc)
            ts("dve", KF, TH, -math.pi, 2 * math.pi, ALU.is_lt, ALU.mult, rc, [r_T2])
            tt("dve", TH, TH, KF, ALU.add, [r_T2] + rc, rc)
            ts("dve", TH, TH, math.pi, -math.pi, ALU.min, ALU.max, rc, rc)
            act(SC, TH, AF.Sin, rc, rc)
            tt("dve", aim, mag, SC[:, 0, :], ALU.mult, rc, rc)
            tt("dve", are, mag, SC[:, 1, :], ALU.mult, rc, rc)
            tt("dve", den, lre, lre, ALU.mult, rc, rc)
            tt("dve", t_a, lim, lim, ALU.mult, rc, rc)
            tt("dve", den, den, t_a, ALU.add, rc, rc)
            P.op("dve", lambda: nc.vector.reciprocal(out=den, in_=den), rc, rc)
            ts("dve", nr, are, -1.0, None, ALU.add, None, rc, rc)
            tt("dve", t_a, nr, lre, ALU.mult, rc, rc)
            tt("dve", t_b, aim, lim, ALU.mult, rc, rc)
            tt("dve", t_a, t_a, t_b, ALU.add, rc, rc)
            tt("dve", fre, t_a, den, ALU.mult, rc, rc)
            tt("dve", t_a, aim, lre, ALU.mult, rc, rc)
            tt("dve", t_b, nr, lim, ALU.mult, rc, rc)
            tt("dve", t_a, t_a, t_b, ALU.subtract, rc, rc)
            tt("dve", fim, t_a, den, ALU.mult, rc, rc)
            ire, iim = mi(3), mi(6)
            tt("dve", t_a, mag, mag, ALU.mult, rc, rc)
            P.op("dve", lambda: nc.vector.reciprocal(out=t_a, in_=t_a), rc, rc)
            tt("dve", ire, are, t_a, ALU.mult, rc, rc)
            stt(iim, aim, -1.0, t_a, ALU.mult, ALU.mult, rc, rc)

            def pw(tab, e):
                return tab[:, e * 128:e * 128 + 64], tab[:, e * 128 + 64:e * 128 + 128]

            def cmul(o_re, o_im, x_re, x_im, y_re, y_im):
                tt("dve", t_a, x_re, y_re, ALU.mult, rc, rc)
                tt("dve", t_b, x_im, y_im, ALU.mult, rc, rc)
                tt("dve", TS, x_re, y_im, ALU.mult, rc, rc)
                tt("dve", o_re, t_a, t_b, ALU.subtract, rc, rc)
                tt("dve", t_a, x_im, y_re, ALU.mult, rc, rc)
                tt("dve", o_im, TS, t_a, ALU.add, rc, rc)

            for tab, (b_re, b_im) in ((APOW, (are, aim)), (AINV, (ire, iim))):
                r0, i0 = pw(tab, 0)
                P.op("pool", lambda r0=r0: nc.gpsimd.memset(r0, 1.0), rc, rc)
                P.op("pool", lambda i0=i0: nc.gpsimd.memset(i0, 0.0), rc, rc)
                r1, i1 = pw(tab, 1)
                cp("dve", r1, b_re, rc, rc)
                cp("dve", i1, b_im, rc, rc)
                for e in range(2, 9):
                    pr, pi_ = pw(tab, e - 1)
                    cr, ci = pw(tab, e)
                    cmul(cr, ci, pr, pi_, b_re, b_im)
            P.op("pool", lambda: nc.gpsimd.memset(TS, 0.0), rc, rc)
            return fre, fim

        def cprod(o_view, tab, e0, estep, xraw, d, gp0, neg_im, dstr=1024):
            fo = d * 32 + gp0
            er = raw(S5t, 2304, (0 if tab is APOW else 1152) + e0 * 128 + fo, [(estep * 128, 8), (1, 4), (0, 16)])
            ei = raw(S5t, 2304, (0 if tab is APOW else 1152) + e0 * 128 + 64 + fo, [(estep * 128, 8), (1, 4), (0, 16)])
            xoff = 1216 if xraw is BRAW else 1472
            xr = raw(EBt, 1920, xoff + (0 * 2 + d) * 64, [(0, 8), (16, 4), (1, 16)])
            xi = raw(EBt, 1920, xoff + (1 * 2 + d) * 64, [(0, 8), (16, 4), (1, 16)])
            t1 = raw(R2t, 10240, 9216, [(16, 8), (128, 4), (1, 16)])
            t2 = raw(R2t, 10240, 9728, [(16, 8), (128, 4), (1, 16)])
            ovt = o_view.tensor
            obase = o_view.offset
            o_re = raw(ovt, 20480, obase + d * dstr, [(16, 8), (256, 4), (1, 16)])
            o_im = raw(ovt, 20480, obase + d * dstr + 128, [(16, 8), (256, 4), (1, 16)])
            rx = [r_s5c, r_braw if xraw is BRAW else r_craw]
            ro = [r_Z1 if o_view is Z1 else (r_Z2 if o_view is Z2 else r_W2)]
            tt("dve", t1, er, xr, ALU.mult, rx, [r_T1])
            tt("pool", t2, ei, xi, ALU.mult, rx, [r_T2])
            tt("dve", o_re, t1, t2, ALU.subtract, [r_T1, r_T2], ro)
            tt("dve", t1, er, xi, ALU.mult, rx, [r_T1])
            tt("pool", t2, ei, xr, ALU.mult, rx, [r_T2])
            if neg_im:
                tt("dve", t1, t1, t2, ALU.add, [r_T1, r_T2], [r_T1])
                tt("dve", o_im, raw(EBt, 1920, 1856, [(0, 8), (0, 4), (0, 16)]), t1, ALU.subtract, [r_T1, r_s5c], ro)
            else:
                tt("dve", o_im, t1, t2, ALU.add, [r_T1, r_T2], ro)

        def s5_layer(i):
            j = i // 2
            P.barrier()
            fre, fim = s5_discretise(j)
            rc = [r_s5c]
            for sbk in range(2):
                jc = sbk
                P.barrier()
                norm_mod(i, 0, sbk)
                for s_ in range(8):
                    b = nb()
                    pb16 = banks[b][:, :].bitcast(BF16)
                    for dc in range(NCH):
                        hsl = raw(hTt, 8192, dc * 1024 + s_, [(8, 128)])
                        tr(pb16[:, dc * 128:(dc + 1) * 128], hsl, identb[:], [r_hT[0], r_hT[1], r_const], [r_bank[b]])
                    cp("act", raw(TM8t, 24576, TM8.offset + s_ * 16, [(128, 64), (1, 16)]),
                       pb16.rearrange("p (g c) -> p g c", c=16), [r_bank[b]], [r_tm8])
                for gb in range(2):
                    P.barrier()
                    if sbk == 1:
                        P.dma("sp", H0T, s5h0_in[j, :, gb, :], writes=[r_h0])
                    for sub in range(4):
                        gp0 = gb * 16 + sub * 4
                        if sbk == 1:
                            P.dma("sp", W1Tflat, cW1[gb, sub, :, :], reads=[r_cW1[gb][sub]], writes=[r_W1T])
                            if sub == 0:
                                P.dma("sp", W2flat, cW2[gb, :, :], reads=[r_cW2[gb]], writes=[r_W2])
                                P.dma("sp", Mflat, cM[gb, :, :], reads=[r_cM[gb]], writes=[r_M])
                        else:
                            for d in range(2):
                                P.dma("sp", BRAW.rearrange("p (r d g c) -> p r d g c", r=2, d=2, g=4)[:, :, d, :, :],
                                      s5b_in[j, :, :, d * 32 + gp0:d * 32 + gp0 + 4, :], writes=[r_braw])
                                P.dma("sp", CRAW.rearrange("p (r d g c) -> p r d g c", r=2, d=2, g=4)[:, :, d, :, :],
                                      s5c_in[j, :, :, d * 32 + gp0:d * 32 + gp0 + 4, :], writes=[r_craw])
                            bv = BRAW.rearrange("p (r f c) -> p r f c", r=2, f=8)
                            for d in range(2):
                                fr_ = raw(EBt, 1920, 192 + 10 * 64 + d * 32 + gp0, [(1, 4), (0, 16)])
                                fi_ = raw(EBt, 1920, 192 + 11 * 64 + d * 32 + gp0, [(1, 4), (0, 16)])
                                br = bv[:, 0, d * 4:(d + 1) * 4, :]
                                bi_ = bv[:, 1, d * 4:(d + 1) * 4, :]
                                u1 = T1[:, 0:64].rearrange("p (g c) -> p g c", c=16)
                                u2 = T2[:, 0:64].rearrange("p (g c) -> p g c", c=16)
                                u3 = T1[:, 64:128].rearrange("p (g c) -> p g c", c=16)
                                u4 = T2[:, 64:128].rearrange("p (g c) -> p g c", c=16)
                                tt("dve", u1, fr_, br, ALU.mult, [r_s5c, r_braw], [r_T1])
                                tt("dve", u2, fi_, bi_, ALU.mult, [r_s5c, r_braw], [r_T2])
                                tt("dve", u3, fr_, bi_, ALU.mult, [r_s5c, r_braw], [r_T1])
                                tt("dve", u4, fi_, br, ALU.mult, [r_s5c, r_braw], [r_T2])
                                tt("dve", br, u1, u2, ALU.subtract, [r_T1, r_T2], [r_braw])
                                tt("dve", bi_, u3, u4, ALU.add, [r_T1, r_T2], [r_braw])
                            for d in range(2):
                                cprod(Z1, APOW, 7 if d == 0 else 0, -1 if d == 0 else 1, BRAW, d, gp0, False)
                                cprod(Z2, AINV, 7 if d == 0 else 0, -1 if d == 0 else 1, CRAW, d, gp0, True)
                            for hb in range(2):
                                b = nb()
                                pb16 = banks[b][:, :].bitcast(BF16)
                                for q4 in range(8):
                                    idx = hb * 8 + q4
                                    d, gl, r = idx // 8, (idx // 2) % 4, idx % 2
                                    tr(pb16[:, q4 * 128:(q4 + 1) * 128], Z1[:, d, gl, r, :], identb[:], [r_Z1, r_const], [r_bank[b]])
                                cp("act", W1T[:, hb, :, :, :].rearrange("p g r n -> p (g r n)"), pb16, [r_bank[b]], [r_W1T])
                            for gl in range(4):
                                for g2 in range(2):
                                    g = 2 * (gp0 + gl) + g2
                                    glb = 2 * (sub * 4 + gl) + g2
                                    b = nb()
                                    for d in range(2):
                                        for r in range(2):
                                            mm(banks[b][:, d * 128:(d + 1) * 128], Z1[g2 * 64:(g2 + 1) * 64, d, gl, r, :],
                                               Z2[g2 * 64:(g2 + 1) * 64, d, gl, r, :], r == 0, r == 1, [r_Z1, r_Z2], [r_bank[b]])
                                    tt("dve", T1[:, 0:256], banks[b][:, 0:256], masks[:, :], ALU.mult, [r_bank[b], r_const], [r_T1])
                                    tt("dve", T1[:, 0:128], T1[:, 0:128], T1[:, 128:256], ALU.add, [r_T1], [r_T1])
                                    stt(Msb[:, glb, :], ident[:], dvec[:, j, g:g + 1], T1[:, 0:128], ALU.mult, ALU.add,
                                        [r_T1, r_const], [r_M])
                            for d in range(2):
                                cprod(W2[:, :, sub * 4:(sub + 1) * 4, :, :], APOW, 1 if d == 0 else 8, 1 if d == 0 else -1, CRAW, d, gp0, True, dstr=4096)
                            tk = P.dma("sp", cW1[gb, sub, :, :], W1Tflat, reads=[r_W1T], writes=[r_cW1[gb][sub]])
                            P.store_keys.add(tk[0])
                            if sub == 3:
                                tk = P.dma("sp", cW2[gb, :, :], W2flat, reads=[r_W2], writes=[r_cW2[gb]])
                                P.store_keys.add(tk[0])
                                tk = P.dma("sp", cM[gb, :, :], Mflat, reads=[r_M], writes=[r_cM[gb]])
                                P.store_keys.add(tk[0])
                        for gl in range(4):
                            gpl = sub * 4 + gl
                            u_ = s5st["u"] % 2
                            s5st["u"] += 1
                            bu = nb()
                            pu16 = banks[bu][:, :].bitcast(BF16)
                            for g2 in range(2):
                                g = 2 * (gp0 + gl) + g2
                                tr(pu16[:, g2 * 128:(g2 + 1) * 128], TM8[:, g * 128:(g + 1) * 128],
                                   identb[:], [r_tm8, r_const], [r_bank[bu]])
                            cp("act", Ust[u_][:, 0:256], pu16[:, 0:256], [r_bank[bu]], [r_ust[u_]])
                            b = nb()
                            for d in range(2):
                                for r in range(2):
                                    for g2 in range(2):
                                        mm(banks[b][g2 * 64:(g2 + 1) * 64, (d * 2 + r) * 128:(d * 2 + r + 1) * 128],
                                           W1T[:, d, gl, r, g2 * 64:(g2 + 1) * 64], Ust[u_][:, g2 * 128:(g2 + 1) * 128],
                                           True, True, [r_W1T, r_ust[u_]], [r_bank[b]])
                            cp("act", raw(R1t, 12288, gpl * 256, [(4096, 2), (128, 2), (1, 128)]),
                               banks[b][:, :].rearrange("p (d r k) -> p d r k", d=2, r=2), [r_bank[b]], [r_S])
                        ada_step()
                    nseq = 4 if sbk == 0 else 1
                    kl = 32 if sbk == 0 else 128

                    def sview(kk, sel):
                        dstr = 4096 + (kl - 1) - 2 * kk
                        if sel is None:
                            dims = [(dstr, 2), (128, 32)]
                        else:
                            dims = [(dstr, 2), (256, 16)]
                        off = kk + (0 if sel is None else sel * 128)
                        if nseq > 1:
                            dims = dims + [(32, nseq)]
                        return raw(R1t, 12288, off, dims)

                    sq_dims = [(0, nseq)] if nseq > 1 else []
                    tS = raw(R2t, 10240, 9216, [(32 * nseq, 2), (nseq, 32)] + ([(1, nseq)] if nseq > 1 else []))
                    uS = raw(R2t, 10240, 9728, [(32 * nseq, 2), (nseq, 32)] + ([(1, nseq)] if nseq > 1 else []))
                    uS_re = raw(R2t, 10240, 9728, [(32 * nseq, 2), (2 * nseq, 16)] + ([(1, nseq)] if nseq > 1 else []))
                    uS_im = raw(R2t, 10240, 9728 + nseq, [(32 * nseq, 2), (2 * nseq, 16)] + ([(1, nseq)] if nseq > 1 else []))

                    def scan_step(prv_all, prv_re, prv_im, cur):
                        tt("dve", tS, A8R, prv_all, ALU.mult, [r_S, r_s5c, r_h0], [r_T1])
                        tt("dve", uS_re, A8IN, prv_im, ALU.mult, [r_S, r_s5c, r_h0], [r_T2])
                        tt("dve", uS_im, A8I, prv_re, ALU.mult, [r_S, r_s5c, r_h0], [r_T2])
                        tt("dve", tS, tS, uS, ALU.add, [r_T1, r_T2], [r_T1])
                        tt("dve", cur, cur, tS, ALU.add, [r_T1, r_S], [r_S])

                    cof = raw(EBt, 1920, 192 + 12 * 64, [(32, 2), (2, 16), (1, 2)])
                    P.op("dve", lambda cof=cof, gb=gb: nc.vector.tensor_copy(
                        out=cof, in_=raw(S5t, 2304, 8 * 128 + gb * 16, [(32, 2), (1, 16), (0, 2)])), [r_s5c], [r_s5c])
                    coi = raw(EBt, 1920, 192 + 13 * 64, [(16, 2), (1, 16)])
                    P.op("dve", lambda coi=coi, gb=gb: nc.vector.tensor_copy(
                        out=coi, in_=raw(S5t, 2304, 8 * 128 + 64 + gb * 16, [(32, 2), (1, 16)])), [r_s5c], [r_s5c])
                    con = raw(EBt, 1920, 192 + 14 * 64, [(16, 2), (1, 16)])
                    P.op("dve", lambda con=con, coi=coi: nc.vector.tensor_scalar(
                        out=con, in0=coi, scalar1=-1.0, scalar2=None, op0=ALU.mult), [r_s5c], [r_s5c])
                    A8R = raw(EBt, 1920, 192 + 12 * 64, [(32, 2), (1, 32)] + sq_dims)
                    A8I = raw(EBt, 1920, 192 + 13 * 64, [(16, 2), (1, 16)] + sq_dims)
                    A8IN = raw(EBt, 1920, 192 + 14 * 64, [(16, 2), (1, 16)] + sq_dims)
                    if sbk == 1:
                        h_all = raw(EBt, 1920, 1728, [(32, 2), (1, 32)])
                        h_re = raw(EBt, 1920, 1728, [(32, 2), (2, 16)])
                        h_im = raw(EBt, 1920, 1729, [(32, 2), (2, 16)])
                        scan_step(h_all, h_re, h_im, sview(0, None))
                    for kk in range(1, kl):
                        scan_step(sview(kk - 1, None), sview(kk - 1, 0), sview(kk - 1, 1), sview(kk, None))
                    if sbk == 0:
                        for seq in range(4):
                            cp("act", FIN.rearrange("p (s r d g) -> p s r d g", s=4, r=2, d=2)[:, seq, :, :, :],
                               raw(R1t, 12288, seq * 32 + 31, [(128, 2), (4096 - 31, 2), (256, 16)]), [r_S], [r_T2])
                        for seq in range(4):
                            for r in range(2):
                                b = nb()
                                tr(banks[b][0:32, 0:128], FIN.rearrange("p (s r f) -> p s r f", s=4, r=2)[:, seq, r, :], ident[:],
                                   [r_T2, r_const], [r_bank[b]])
                                o_ = s5st["w"] % 2
                                s5st["w"] += 1
                                cp("act", stst[o_][0:32, :], banks[b][0:32, 0:128], [r_bank[b]], [r_stst[o_]])
                                for d in range(2):
                                    tk = P.dma("sp", st_out[seq, j, r, d, gb * 16:(gb + 1) * 16, :],
                                               stst[o_][d * 16:(d + 1) * 16, :], reads=[r_stst[o_]], is_output=True)
                                    P.store_keys.add(tk[0])
                    for quad in range(8):
                        gq0 = gb * 32 + quad * 4
                        u_ = s5st["u"] % 2
                        s5st["u"] += 1
                        bu = nb()
                        pu16 = banks[bu][:, :].bitcast(BF16)
                        for g4 in range(4):
                            g = gq0 + g4
                            tr(pu16[:, g4 * 128:(g4 + 1) * 128], TM8[:, g * 128:(g + 1) * 128],
                               identb[:], [r_tm8, r_const], [r_bank[bu]])
                        cp("act", Ust[u_][:, :], pu16[:, 0:512], [r_bank[bu]], [r_ust[u_]])
                        xq_t = T1 if quad % 2 == 0 else T2
                        r_xq = r_T1 if quad % 2 == 0 else r_T2
                        Xq = xq_t[:, :].bitcast(BF16).rearrange("p (d f k) -> p d f k", d=2, f=4)
                        fr0 = quad * 4
                        if sbk == 1:
                            cp("act", Xq[:, 0, :, 1:128], raw(R1t, 12288, fr0 * 128, [(128, 4), (1, 127)]), [r_S], [r_xq])
                            cp("act", Xq[:, 1, :, 0:127], raw(R1t, 12288, 4096 + fr0 * 128 + 1, [(128, 4), (1, 127)]), [r_S], [r_xq])
                            cp("pool", Xq[:, 0, :, 0:1], raw(EBt, 1920, 1728 + fr0, [(1, 4), (1, 1)]), [r_h0], [r_xq])
                            cp("pool", Xq[:, 1, :, 127:128], raw(EBt, 1920, 1728 + 32 + fr0, [(1, 4), (1, 1)]), [r_h0], [r_xq])
                        else:
                            xv = Xq.rearrange("p d f (s k) -> p d f s k", s=4)
                            cp("act", xv[:, 0, :, :, 1:32], raw(R1t, 12288, fr0 * 128, [(128, 4), (32, 4), (1, 31)]), [r_S], [r_xq])
                            cp("act", xv[:, 1, :, :, 0:31], raw(R1t, 12288, 4096 + fr0 * 128 + 1, [(128, 4), (32, 4), (1, 31)]), [r_S], [r_xq])
                            P.op("pool", lambda xv=xv: nc.gpsimd.memset(xv[:, 0, :, :, 0:1], 0.0), [], [r_xq])
                            P.op("pool", lambda xv=xv: nc.gpsimd.memset(xv[:, 1, :, :, 31:32], 0.0), [], [r_xq])
                        b = nb()
                        for g4 in range(4):
                            g = gq0 + g4
                            glb = quad * 4 + g4
                            gpl, g2 = glb // 2, glb % 2
                            mm(banks[b][:, g4 * 128:(g4 + 1) * 128], Ust[u_][:, g4 * 128:(g4 + 1) * 128], Msb[:, glb, :],
                               True, False, [r_ust[u_], r_M], [r_bank[b]])
                            for d in range(2):
                                for r in range(2):
                                    mm(banks[b][:, g4 * 128:(g4 + 1) * 128], Xq[g2 * 64:(g2 + 1) * 64, d, (g4 // 2) * 2 + r, :],
                                       W2[g2 * 64:(g2 + 1) * 64, d, gpl, r, :], False, d == 1 and r == 1,
                                       [r_xq, r_W2], [r_bank[b]])
                        t_ = tmp[quad % 2]
                        rt_ = r_tmp[quad % 2]
                        act(t_[:], banks[b][:, :], AF.Square, [r_bank[b]], [rt_])
                        ts("dve", t_[:], t_[:], 0.044715, 1.0, ALU.mult, ALU.add, [rt_], [rt_])
                        tt("dve", t_[:], t_[:], banks[b][:, :], ALU.mult, [rt_, r_bank[b]], [rt_])
                        act(t_[:], t_[:], AF.Sigmoid, [rt_], [rt_], scale=1.5957691216057308)
                        tt("dve", raw(hTt, 8192, gq0 * 16, [(16, 4), (1024, 8), (1, 16)]),
                           t_[:].rearrange("p (g t c) -> p g t c", g=4, t=8), banks[b][:, :].rearrange("p (g t c) -> p g t c", g=4, t=8),
                           ALU.mult, [rt_, r_bank[b]], [r_hT[0], r_hT[1]])
                    ada_step()
                P.barrier()
                Yv = hT[:, :, :].rearrange("p c n -> p (c n)")
                zT = TM8.rearrange("p (c n) -> p c n", c=NCH)
                for dc in range(NCH):
                    b = nb()
                    pb16 = banks[b][:, :].bitcast(BF16)
                    for t8 in range(8):
                        tr(pb16[:, t8 * 128:(t8 + 1) * 128], Yv[:, t8 * 1024 + dc * 128:t8 * 1024 + (dc + 1) * 128], identb[:],
                           [r_hT[0], r_hT[1], r_const], [r_bank[b]])
                    cp("act", raw(TM8t, 24576, TM8.offset + dc * 1024, [(1, 8), (8, 128)]), pb16.rearrange("p (t k) -> p t k", t=8),
                       [r_bank[b]], [r_tm8])
                r_wg = [Res("wg0"), Res("wg1")]
                wgb = [[R2[:, (s_ * 2 + w_) * 1024:(s_ * 2 + w_ + 1) * 1024].bitcast(BF16).rearrange("p (k n) -> p k n", k=NCH)
                        for w_ in range(2)] for s_ in range(2)]
                for cb in range(4):
                    s_ = cb % 2
                    P.dma("pool", wgb[s_][0], glu_w[j, :, cb * 256:(cb + 1) * 256].rearrange("(k p) n -> p k n", p=128), writes=[r_wg[s_]])
                    P.dma("pool", wgb[s_][1], glu_w[j, :, D + cb * 256:D + (cb + 1) * 256].rearrange("(k p) n -> p k n", p=128), writes=[r_wg[s_]])
                    for i2 in range(2):
                        dc = cb * 2 + i2
                        for tb in range(2):
                            t0 = sbk * 1024 + tb * 512
                            bv_, bg_ = nb(), nb()
                            for kc in range(NCH):
                                mm(banks[bv_][:, :], wgb[s_][0][:, kc, i2 * 128:(i2 + 1) * 128], zT[:, kc, tb * 512:(tb + 1) * 512],
                                   kc == 0, kc == NCH - 1, [r_wg[s_], r_tm8], [r_bank[bv_]])
                            for kc in range(NCH):
                                mm(banks[bg_][:, :], wgb[s_][1][:, kc, i2 * 128:(i2 + 1) * 128], zT[:, kc, tb * 512:(tb + 1) * 512],
                                   kc == 0, kc == NCH - 1, [r_wg[s_], r_tm8], [r_bank[bg_]])
                            ti = (dc * 2 + tb) % 2
                            act(tmp[ti][:], banks[bg_][:, :], AF.Sigmoid, [r_bank[bg_]], [r_tmp[ti]])
                            tt("dve", tmp[ti][:], tmp[ti][:], banks[bv_][:, :], ALU.mult, [r_tmp[ti], r_bank[bv_]], [r_tmp[ti]])
                            stt(xT[:, dc, t0:t0 + 512], tmp[ti][:], mod[:, i, 16 + dc, jc:jc + 1], xT[:, dc, t0:t0 + 512],
                                ALU.mult, ALU.add, [r_tmp[ti], r_mod[i]] + xres(t0, 512), xres(t0, 512))
                ada_step()
            P.barrier()

        stage = [R1[:, s * 1024:(s + 1) * 1024] for s in range(2)]
        r_stage = [Res("stage0"), Res("stage1")]
        for t in range(16):
            s = t % 2
            P.dma("sp", stage[s], x_tok[t * 128:(t + 1) * 128, :], writes=[r_stage[s]])
            for half in range(2):
                b = nb()
                for j in range(4):
                    c = half * 4 + j
                    tr(banks[b][:, j * 128:(j + 1) * 128], stage[s][:, c * 128:(c + 1) * 128], ident[:],
                       [r_stage[s], r_const], [r_bank[b]])
                cp("act", xT[:, half * 4:half * 4 + 4, t * 128:(t + 1) * 128],
                   banks[b][:, :].rearrange("p (c n) -> p c n", c=4), [r_bank[b]], [r_xT[t]])
        wbig = [R2[:, s_ * 4096:(s_ + 1) * 4096].bitcast(BF16).rearrange("p (k n) -> p k n", k=NCH) for s_ in range(2)]
        r_wbig = [Res("wbig0"), Res("wbig1")]
        mps0 = banks[ADA_BANK][:, 0:96].rearrange("p (a b) -> p a b", b=2)
        for cbig in range(6):
            s_ = cbig % 2
            P.dma("pool", wbig[s_], ada_w[0, :, cbig * 1024:(cbig + 1) * 1024].rearrange("(k p) n -> p k n", p=128),
                  writes=[r_wbig[s_]])
            for j8 in range(8):
                jj = cbig * 8 + j8
                for kc in range(NCH):
                    mm(mps0[:, jj, :], wbig[s_][:, kc, j8 * 128:(j8 + 1) * 128], scond[:, kc, :],
                       kc == 0, kc == NCH - 1, [r_wbig[s_], r_const], [r_bank[ADA_BANK]])
        ada_fin(0)
        P.barrier()

        for i in range(DEPTH):
            if i + 1 < DEPTH:
                ada_layer(i + 1)
            if "na" in stages and i % 2 == 0:
                na_layer(i)
            if "s5" in stages and i % 2 == 1:
                s5_layer(i)
            if "ffn" in stages:
                for sbk in range(2):
                    if sbk == 0:
                        norm_mod(i, 1, 0)
                        ffn(i, 0, between=lambda i=i: norm_mod(i, 1, 1))
                    else:
                        ffn(i, 1)
            ada_flush()

        P.barrier()
        r_stage = [Res("ostage0"), Res("ostage1")]
        for t in range(16):
            s = t % 2
            for half in range(2):
                b = nb()
                for j in range(4):
                    c = half * 4 + j
                    tr(banks[b][:, j * 128:(j + 1) * 128], xT[:, c, t * 128:(t + 1) * 128], ident[:],
                       [r_xT[t], r_const], [r_bank[b]])
                cp("dve", stage[s][:, half * 512:(half + 1) * 512], banks[b][:, :], [r_bank[b]], [r_stage[s]])
            tk = P.dma("sp", y_tok[t * 128:(t + 1) * 128, :], stage[s], reads=[r_stage[s]], is_output=True)
            P.store_keys.add(tk[0])

        P.finish()
        with nc.Block() as block:
            P.emit(block)
    return nc


_NC_CACHE = {}


def _build_ebias(rpb):
    L, H = rpb.shape[0], rpb.shape[1]
    pad = np.concatenate([rpb.reshape(L, H, -1), np.full((L, H, 1), -30000.0, np.float32)], -1)
    u = np.arange(2)[:, None, None, None]
    kc = np.arange(64)[None, :, None, None]
    d = np.arange(15)[None, None, :, None]
    qc = np.arange(64)[None, None, None, :]
    dr = d + u - 7
    cs = np.clip(qc - 8, 0, 48)
    ok = (kc >= cs) & (kc < cs + 16) & (dr <= 7)
    dc = np.clip(kc - qc, -15, 15)
    idx = np.where(ok, (np.clip(dr, -7, 7) + 7) * 31 + dc + 15, 15 * 31)
    idx = np.broadcast_to(idx, (2, 64, 15, 64)).reshape(128, 960)
    return np.ascontiguousarray(pad[:, :, idx])


def _qf(a):
    lead = a.shape[:-3]
    a = a.reshape(lead + (2, 32, 2, 64))
    nl = len(lead)
    a = np.moveaxis(a, (nl + 2, nl + 3, nl + 0, nl + 1), (nl + 0, nl + 1, nl + 2, nl + 3))
    return a.reshape(lead + (128, 64))


def _build_s5(lre, lim, lst, bre, bim, cre, cim, dsk):
    L = lre.shape[0]
    lam = np.stack([_qf(lre), _qf(lim), _qf(np.broadcast_to(lst[..., None], lre.shape))], 2)
    def qfc(a):
        a = a.reshape(L, 2, 32, 2, 64, 16).transpose(0, 3, 4, 1, 2, 5)
        return a.reshape(L, 128, 64, 16)
    b = np.stack([qfc(bre), qfc(bim)], 2)
    ct = lambda a: np.swapaxes(a, -1, -2)
    c = np.stack([qfc(ct(cre)), qfc(ct(cim))], 2)
    dvec = np.ascontiguousarray(np.tile(dsk.reshape(L, 64, 1, 16), (1, 1, 8, 1)).reshape(L, 64, 128).transpose(2, 0, 1))
    s_ = np.arange(128)[:, None] // 16
    t_ = np.arange(128)[None, :] // 16
    masks = np.concatenate([(t_ >= s_), (t_ <= s_)], 1).astype(np.float32)
    return {"lam": np.ascontiguousarray(lam), "b": np.ascontiguousarray(b), "c": np.ascontiguousarray(c),
            "dvec": dvec, "masks": masks}


def _build_h0(hre, him):
    a = np.stack([hre, him], -1)
    L = a.shape[0]
    a = a.reshape(L, 2, 2, 16, 2, 64, 2)
    a = a.transpose(0, 4, 5, 2, 1, 3, 6)
    return np.ascontiguousarray(a.reshape(L, 128, 2, 64))


def _fm(v):
    return np.ascontiguousarray(v.reshape(NCH, 128).T)


def kernel(x_prompt, x_sample, cache_k, cache_v, state_ssm_re, state_ssm_im, c, c_ctx,
           norm_mix, norm_ffn, ada_w, ada_b, na_w_qkv, na_w_o, na_q_gain, na_k_gain, na_rpb,
           ssm_lambda_re, ssm_lambda_im, ssm_log_step, ssm_b_re, ssm_b_im, ssm_c_re, ssm_c_im,
           ssm_d, ssm_w_glu, ffn_w1, ffn_w3, ffn_w2):
    n = 8
    f = lambda a: np.ascontiguousarray(np.asarray(a, np.float32))
    x_prompt, x_sample, c, c_ctx = f(x_prompt), f(x_sample), f(c), f(c_ctx)
    norm_mix, norm_ffn, ada_w, ada_b = f(norm_mix), f(norm_ffn), f(ada_w), f(ada_b)
    ffn_w1, ffn_w3, ffn_w2 = f(ffn_w1), f(ffn_w3), f(ffn_w2)
    na_w_qkv, na_w_o, na_q_gain, na_k_gain, na_rpb = f(na_w_qkv), f(na_w_o), f(na_q_gain), f(na_k_gain), f(na_rpb)
    cache_k, cache_v = f(cache_k), f(cache_v)
    s5in = _build_s5(f(ssm_lambda_re), f(ssm_lambda_im), f(ssm_log_step), f(ssm_b_re), f(ssm_b_im), f(ssm_c_re), f(ssm_c_im), f(ssm_d))
    state_ssm_re, state_ssm_im, ssm_w_glu = f(state_ssm_re), f(state_ssm_im), f(ssm_w_glu)
    qkg = np.ascontiguousarray(np.stack([np.tile(na_q_gain, (1, 2)).T, np.tile(na_k_gain, (1, 2)).T], -1))
    bones = np.kron(np.eye(2, dtype=np.float32), np.ones((64, 64), np.float32))
    ebias = _build_ebias(na_rpb)
    if "nc" not in _NC_CACHE:
        _NC_CACHE["nc"] = build_program()
    nc = _NC_CACHE["nc"]
    ident = np.eye(128, dtype=np.float32)
    gmix = np.ascontiguousarray(norm_mix.reshape(DEPTH, NCH, 128).transpose(2, 0, 1))
    gffn = np.ascontiguousarray(norm_ffn.reshape(DEPTH, NCH, 128).transpose(2, 0, 1))
    adab = np.ascontiguousarray(ada_b.reshape(DEPTH, 48, 128).transpose(2, 0, 1))
    in_maps = []
    for core in range(n):
        xp = x_prompt[4 * core:4 * core + 4].reshape(NP_TOK, D)
        xs = x_sample[core % 2]
        cond = np.ascontiguousarray(np.stack([_fm(c_ctx), _fm(c[core % 2])], -1))
        in_maps.append({"x_tok": np.ascontiguousarray(np.concatenate([xp, xs], 0)), "ident": ident, "cond": cond,
                        "gmix": gmix, "gffn": gffn, "adab": adab, "ada_w": ada_w,
                        "ffn_w1": ffn_w1, "ffn_w3": ffn_w3, "ffn_w2": ffn_w2,
                        "na_w_qkv": na_w_qkv, "na_w_o": na_w_o, "qkg": qkg, "bones": bones, "ebias": ebias,
                        "s5lam": s5in["lam"], "s5b": s5in["b"], "s5c": s5in["c"], "dvec": s5in["dvec"], "masks": s5in["masks"],
                        "s5h0": _build_h0(state_ssm_re[core % 2], state_ssm_im[core % 2]), "glu_w": ssm_w_glu,
                        "ck": np.ascontiguousarray(cache_k[core % 2].reshape(2, 512, D)),
                        "cv": np.ascontiguousarray(cache_v[core % 2].reshape(2, 512, D))})
    res = run_bass_kernel_spmd(nc, in_maps, core_ids=list(range(n)))
    outs = res.results
    y_prompt = np.concatenate([outs[c_]["y_tok"][:NP_TOK].reshape(4, 256, D) for c_ in range(n)], 0)
    y_sample = np.stack([outs[0]["y_tok"][NP_TOK:], outs[1]["y_tok"][NP_TOK:]], 0)
    new_k = np.concatenate([outs[c_]["k_out"] for c_ in range(n)], 0).reshape(32, 2, 256, 16, 64)
    new_v = np.concatenate([outs[c_]["v_out"] for c_ in range(n)], 0).reshape(32, 2, 256, 16, 64)
    st = np.concatenate([outs[c_]["st_out"] for c_ in range(n)], 0).reshape(32, 2, 2, 2, 64, 64)
    st_re = np.ascontiguousarray(st[:, :, 0])
    st_im = np.ascontiguousarray(st[:, :, 1])
    return (y_prompt, y_sample, new_k, new_v, st_re, st_im)
```
